# Optimizing a Trainium2 kernel written in Bass

```python
import jax, jax.numpy as jnp
from jax import lax
import numpy as np

D_MODEL = 1024
BATCH = 4
SEQ = 4096
DEPTH = 2

N_BRANCH = 4
BRANCH_W = 256
SGU_GROUPS = 4
SGU_GD = BRANCH_W // SGU_GROUPS
SGU_CHUNK = 128
ATT_HEADS = 4
ATT_HD = 64
IDX_HEADS = 8
IDX_HD = 64
TOPK_MAX = 256
QBLOCK = 128
ROPE_THETA = 10000.0
MLSTM_HEADS = 4
MLSTM_HD = 64
MLSTM_CHUNK = 128
CONV_WIDTH = 3
D_FF = 4 * D_MODEL
EPS = 1e-6

SPLIT_SIZES = (
    BRANCH_W, BRANCH_W,
    ATT_HEADS * ATT_HD, ATT_HEADS * ATT_HD, ATT_HEADS * ATT_HD,
    IDX_HEADS * IDX_HD, IDX_HD, IDX_HEADS,
    MLSTM_HEADS * MLSTM_HD, MLSTM_HEADS * MLSTM_HD, MLSTM_HEADS * MLSTM_HD, MLSTM_HEADS * MLSTM_HD,
    MLSTM_HEADS, MLSTM_HEADS,
    BRANCH_W, BRANCH_W, BRANCH_W,
    N_BRANCH * D_MODEL,
)
IN_W = sum(SPLIT_SIZES)

kernel_name = "hybrid_gated_parallel_mixers"


def rms_norm(x, g):
    xf = x.astype(jnp.float32)
    y = xf * lax.rsqrt(jnp.mean(xf * xf, axis=-1, keepdims=True) + EPS)
    return (y * g.astype(jnp.float32)).astype(x.dtype)


def layer_norm_nobias(x, g):
    xf = x.astype(jnp.float32)
    mu = jnp.mean(xf, axis=-1, keepdims=True)
    var = jnp.mean(jnp.square(xf - mu), axis=-1, keepdims=True)
    return ((xf - mu) * lax.rsqrt(var + EPS) * g.astype(jnp.float32)).astype(x.dtype)


def rotary(x, pos):
    d = x.shape[-1]
    half = d // 2
    inv = jnp.float32(ROPE_THETA) ** (-jnp.arange(half, dtype=jnp.float32) * 2.0 / d)
    ang = pos.astype(jnp.float32)[:, None] * inv[None, :]
    cos = jnp.cos(ang)[None, :, None, :]
    sin = jnp.sin(ang)[None, :, None, :]
    xf = x.astype(jnp.float32)
    x1, x2 = xf[..., :half], xf[..., half:]
    return jnp.concatenate([x1 * cos - x2 * sin, x2 * cos + x1 * sin], axis=-1).astype(x.dtype)


def sgu_mix(u, v, norm_g, w_s, b_s):
    bn, s, w = v.shape
    v = rms_norm(v, norm_g).reshape(bn, s // SGU_CHUNK, SGU_CHUNK, SGU_GROUPS, SGU_GD)
    mask = jnp.tril(jnp.ones((SGU_CHUNK, SGU_CHUNK), dtype=w_s.dtype))
    mixed = jnp.einsum('gts,bcsgd->bctgd', w_s * mask, v) + b_s.T[:, :, None]
    return u * mixed.reshape(bn, s, w)


def dsa_attention(q, k, v, q_idx, k_idx, w_idx):
    bn, s = q.shape[:2]
    n_sel = min(TOPK_MAX, s // 4)
    nb = s // QBLOCK
    kpos = jnp.arange(s)

    def to_blocks(a):
        return jnp.moveaxis(a.reshape((bn, nb, QBLOCK) + a.shape[2:]), 1, 0)

    def one_block(args):
        blk, qb, qib, wb = args
        tpos = blk * QBLOCK + jnp.arange(QBLOCK)
        rel = jax.nn.relu(jnp.einsum('bqhd,bsd->bqhs', qib, k_idx).astype(jnp.float32) * (IDX_HD ** -0.5))
        score = jnp.einsum('bqh,bqhs->bqs', wb.astype(jnp.float32) * (IDX_HEADS ** -0.5), rel)
        causal = kpos[None, :] <= tpos[:, None]
        score = jnp.where(causal[None], score, -jnp.inf)
        _, idx = lax.top_k(score, n_sel)
        valid = idx <= tpos[None, :, None]
        ks = jax.vmap(lambda a, i: a[i])(k, idx)
        vs = jax.vmap(lambda a, i: a[i])(v, idx)
        logits = jnp.einsum('bqhd,bqkhd->bhqk', qb, ks).astype(jnp.float32) * (ATT_HD ** -0.5)
        logits = jnp.where(valid[:, None], logits, -jnp.inf)
        p = jax.nn.softmax(logits, axis=-1).astype(vs.dtype)
        return jnp.einsum('bhqk,bqkhd->bqhd', p, vs)

    out = lax.map(one_block, (jnp.arange(nb), to_blocks(q), to_blocks(q_idx), to_blocks(w_idx)))
    return jnp.moveaxis(out, 0, 1).reshape(bn, s, -1)


def mlstm_chunkwise(q, k, v, i_pre, f_pre):
    bn, s, nh, d = q.shape
    nc = s // MLSTM_CHUNK
    L = MLSTM_CHUNK

    def chunks(a):
        a = a.astype(jnp.float32).reshape((bn, nc, L) + a.shape[2:])
        return jnp.moveaxis(a, (1, 3), (0, 2))

    xs = (chunks(q), chunks(k * (d ** -0.5)), chunks(v), chunks(i_pre), chunks(f_pre))
    tril = jnp.tril(jnp.ones((L, L), dtype=bool))

    def step(carry, inp):
        c_st, n_st, m_st = carry
        qc, kc, vc, ig, fg = inp
        b = jnp.cumsum(jax.nn.log_sigmoid(fg), axis=-1)
        dmat = jnp.where(tril, b[..., :, None] - b[..., None, :] + ig[..., None, :], -jnp.inf)
        inter = b + m_st[..., None]
        mj = jnp.maximum(inter, jnp.max(dmat, axis=-1))
        a = jnp.exp(dmat - mj[..., None]) * jnp.einsum('bhjd,bhsd->bhjs', qc, kc)
        w_inter = jnp.exp(inter - mj)
        num = w_inter[..., None] * jnp.einsum('bhjd,bhde->bhje', qc, c_st) + jnp.einsum('bhjs,bhse->bhje', a, vc)
        den = w_inter * jnp.einsum('bhjd,bhd->bhj', qc, n_st) + jnp.sum(a, axis=-1)
        h = num / jnp.maximum(jnp.abs(den), jnp.exp(-mj))[..., None]
        b_last = b[..., -1]
        m_new = mj[..., -1]
        wk = jnp.exp(b_last[..., None] - b + ig - m_new[..., None])
        decay = jnp.exp(b_last + m_st - m_new)
        c_new = decay[..., None, None] * c_st + jnp.einsum('bhs,bhsd,bhse->bhde', wk, kc, vc)
        n_new = decay[..., None] * n_st + jnp.einsum('bhs,bhsd->bhd', wk, kc)
        return (c_new, n_new, m_new), h

    init = (jnp.zeros((bn, nh, d, d), jnp.float32), jnp.zeros((bn, nh, d), jnp.float32),
            jnp.zeros((bn, nh), jnp.float32))
    _, hs = lax.scan(step, init, xs)
    return jnp.moveaxis(hs, (0, 2), (1, 3)).reshape(bn, s, nh, d)


def short_conv(b_gate, c_gate, x_in, w):
    z = c_gate * x_in
    y = lax.conv_general_dilated(z, w[:, None, :], window_strides=(1,), padding=[(CONV_WIDTH - 1, 0)],
                                 dimension_numbers=('NWC', 'WIO', 'NWC'), feature_group_count=z.shape[-1])
    return b_gate * y


def hybrid_mixer(h, pos, w_in, sgu_norm, sgu_w, sgu_b, q_norm, k_norm, kidx_norm,
                 i_bias, f_bias, mlstm_norm, conv_w, w_branch, w_out):
    bn, s, _ = h.shape
    proj = h @ w_in
    offsets = np.cumsum(SPLIT_SIZES)[:-1].tolist()
    (a_u, a_v, b_q, b_k, b_v, b_qi, b_ki, b_wi, c_q, c_k, c_v, c_o, c_i, c_f,
     d_b, d_c, d_x, g) = jnp.split(proj, offsets, axis=-1)

    def heads(a, nh):
        return a.reshape(bn, s, nh, -1)

    y_a = sgu_mix(jax.nn.gelu(a_u), jax.nn.gelu(a_v), sgu_norm, sgu_w, sgu_b)
    q = rotary(rms_norm(heads(b_q, ATT_HEADS), q_norm), pos)
    k = rotary(rms_norm(heads(b_k, ATT_HEADS), k_norm), pos)
    v = heads(b_v, ATT_HEADS)
    qi = rotary(heads(b_qi, IDX_HEADS), pos)
    ki = rotary(layer_norm_nobias(b_ki, kidx_norm)[:, :, None, :], pos)[:, :, 0, :]
    y_b = dsa_attention(q, k, v, qi, ki, b_wi)
    hc = mlstm_chunkwise(heads(c_q, MLSTM_HEADS), heads(c_k, MLSTM_HEADS), heads(c_v, MLSTM_HEADS),
                         c_i + i_bias, c_f + f_bias).astype(h.dtype)
    hc = rms_norm(hc, mlstm_norm.reshape(MLSTM_HEADS, MLSTM_HD)).reshape(bn, s, -1)
    y_c = jax.nn.sigmoid(c_o) * hc
    y_d = short_conv(d_b, d_c, d_x, conv_w)
    gates = jax.nn.sigmoid(g.reshape(bn, s, N_BRANCH, D_MODEL))
    merged = gates[:, :, 0] * (y_a @ w_branch[0])
    merged = merged + gates[:, :, 1] * (y_b @ w_branch[1])
    merged = merged + gates[:, :, 2] * (y_c @ w_branch[2])
    merged = merged + gates[:, :, 3] * (y_d @ w_branch[3])
    return merged @ w_out


def squared_relu_mlp(h, w_up, w_down):
    return jnp.square(jax.nn.relu(h @ w_up)) @ w_down


def setup_inputs(seed: int = 0) -> dict:
    key = jax.random.key(seed)
    ks = jax.random.split(key, 20)
    f32 = jnp.float32

    def nrm(k, shape, scale):
        return jax.random.normal(k, shape, f32) * scale

    def gain(k, shape):
        return 1.0 + 0.02 * jax.random.normal(k, shape, f32)

    f_bias = jnp.linspace(3.0, 6.0, MLSTM_HEADS, dtype=f32)[None, :] + 0.1 * jax.random.normal(ks[11], (DEPTH, MLSTM_HEADS), f32)
    return {
        "x": nrm(ks[0], (BATCH, SEQ, D_MODEL), 1.0),
        "ln_mix": gain(ks[1], (DEPTH, D_MODEL)),
        "w_in": nrm(ks[2], (DEPTH, D_MODEL, IN_W), D_MODEL ** -0.5),
        "sgu_norm": gain(ks[3], (DEPTH, BRANCH_W)),
        "sgu_w": nrm(ks[4], (DEPTH, SGU_GROUPS, SGU_CHUNK, SGU_CHUNK), SGU_CHUNK ** -0.5),
        "sgu_b": gain(ks[5], (DEPTH, SGU_GROUPS, SGU_CHUNK)),
        "q_norm": gain(ks[6], (DEPTH, ATT_HD)),
        "k_norm": gain(ks[7], (DEPTH, ATT_HD)),
        "kidx_norm": gain(ks[8], (DEPTH, IDX_HD)),
        "mlstm_i_bias": nrm(ks[10], (DEPTH, MLSTM_HEADS), 0.1),
        "mlstm_f_bias": f_bias,
        "mlstm_norm": gain(ks[12], (DEPTH, MLSTM_HEADS * MLSTM_HD)),
        "conv_w": nrm(ks[13], (DEPTH, CONV_WIDTH, BRANCH_W), CONV_WIDTH ** -0.5),
        "w_branch": nrm(ks[14], (DEPTH, N_BRANCH, BRANCH_W, D_MODEL), BRANCH_W ** -0.5),
        "w_out": nrm(ks[15], (DEPTH, D_MODEL, D_MODEL), D_MODEL ** -0.5),
        "ln_mlp": gain(ks[16], (DEPTH, D_MODEL)),
        "w_up": nrm(ks[17], (DEPTH, D_MODEL, D_FF), D_MODEL ** -0.5),
        "w_down": nrm(ks[18], (DEPTH, D_FF, D_MODEL), D_FF ** -0.5),
    }


def reference(x, ln_mix, w_in, sgu_norm, sgu_w, sgu_b, q_norm, k_norm, kidx_norm,
              mlstm_i_bias, mlstm_f_bias, mlstm_norm, conv_w, w_branch, w_out,
              ln_mlp, w_up, w_down):
    pos = jnp.arange(x.shape[1], dtype=jnp.int32)
    for l in range(DEPTH):
        h = rms_norm(x, ln_mix[l])
        x = x + hybrid_mixer(h, pos, w_in[l], sgu_norm[l], sgu_w[l], sgu_b[l], q_norm[l], k_norm[l],
                             kidx_norm[l], mlstm_i_bias[l], mlstm_f_bias[l], mlstm_norm[l],
                             conv_w[l], w_branch[l], w_out[l])
        h = rms_norm(x, ln_mlp[l])
        x = x + squared_relu_mlp(h, w_up[l], w_down[l])
    return x
```

```python
import numpy as np
from contextlib import ExitStack
import concourse.bass as bass
import concourse.mybir as mybir
from concourse.bass_utils import run_bass_kernel_spmd

F32 = mybir.dt.float32
BF16 = mybir.dt.bfloat16
AF = mybir.ActivationFunctionType
ALU = mybir.AluOpType
AX = mybir.AxisListType

P = 128
D = 1024
KC = 8
DFF = 4096
IN_W = 7760
EPS = 1e-6
OFF = dict(a_u=0, a_v=256, b_q=512, b_k=768, b_v=1024, b_qi=1280, b_ki=1792, b_wi=1856,
           c_q=1864, c_k=2120, c_v=2376, c_o=2632, c_i=2888, c_f=2892,
           d_b=2896, d_c=3152, d_x=3408, g=3664)
NBISECT = 12
NEG = -1.0e30
MBIAS = -30000.0

CONST_SPEC = [
    ("ident", 128), ("maskT", 128), ("ones", 128), ("sh1", 128), ("sh2", 128), ("ba", 128), ("bb", 128),
    ("lnmixT", 8), ("lnmlpT", 8), ("sgu_norm", 256), ("sgu_b", 4), ("qn", 256), ("kn", 256),
    ("kidx", 64), ("ib", 4), ("fb", 4), ("mn", 256), ("cmask", 256), ("pflag", 1),
    ("neghalf", 8), ("conv", 768), ("sgu_wT", 512),
]
COFF = {}
_o = 0
for _n, _w in CONST_SPEC:
    COFF[_n] = (_o, _w)
    _o += _w
NCONST = _o
NC_TOP = COFF["conv"][0]


def _nbytes(shape, dtp):
    n = 1
    for d in shape[1:]:
        n *= d
    return n * (4 if dtp == F32 else 2)


def _alloc(nc, es, name, shape, dtp):
    _alloc.n += 1
    name = f"{name}_{_alloc.n}"
    t = es.enter_context(nc.sbuf_tensor(name, shape, dtp))
    rem = _nbytes(shape, dtp) % 32
    if rem:
        es.enter_context(nc.sbuf_tensor(name + "_pad", [P, (32 - rem) // 2], BF16))
    return t


_alloc.n = 0


def _unused():
    return None

class Dep:
    __slots__ = ("sem", "val", "eng", "key")

    def __init__(self, sem, val, eng, key):
        self.sem, self.val, self.eng, self.key = sem, val, eng, key


class Tok:
    __slots__ = ("w", "r", "name")

    def __init__(self, name=""):
        self.w = None
        self.r = {}
        self.name = name


class Eng:
    def __init__(self, name, h, sem):
        self.name, self.h, self.sem = name, h, sem
        self.count = 0
        self.waited = {}
        self.n_inst = 0


class Sched:
    NDMA = 8

    def __init__(self, nc):
        self.nc = nc
        self.E = {}
        for name, h in (("pe", nc.tensor), ("act", nc.scalar), ("dve", nc.vector),
                        ("pool", nc.gpsimd), ("sp", nc.sync)):
            self.E[name] = Eng(name, h, nc.alloc_semaphore("s_" + name))
        self.dq = {}
        for q in ("sp", "pool", "act"):
            self.dq[q] = dict(n=0, sems=[nc.alloc_semaphore(f"d_{q}{i}") for i in range(self.NDMA)])

    def _wait(self, E, d):
        if E.waited.get(d.key, 0) >= d.val:
            return
        E.h.wait_ge(d.sem, d.val)
        E.n_inst += 1
        E.waited[d.key] = d.val

    def op(self, eng, fn, r=(), w=(), dma=False, sig=True):
        E = self.E[eng]
        deps = []
        for t in r:
            if t.w is not None:
                deps.append(t.w)
        for t in w:
            if t.w is not None:
                deps.append(t.w)
            deps.extend(t.r.values())
        if dma:
            q = self.dq[eng]
            i = q["n"]
            slot = i % self.NDMA
            sem = q["sems"][slot]
            key = f"d_{eng}{slot}"
            if i >= self.NDMA:
                deps.append(Dep(sem, 16 * (i // self.NDMA), None, key))
            comp = Dep(sem, 16 * (i // self.NDMA + 1), None, key)
            q["n"] += 1
        else:
            if sig:
                E.count += 1
                comp = Dep(E.sem, E.count, eng, "e_" + eng)
            else:
                comp = Dep(E.sem, E.count + 1, eng, "e_" + eng)
        for d in deps:
            if d.eng == eng and not dma and eng == "pe":
                continue
            self._wait(E, d)
        inst = fn(E.h)
        E.n_inst += 1
        if dma:
            inst.then_inc(comp.sem, 16)
        elif sig:
            inst.then_inc(comp.sem, 1)
        for t in r:
            old = t.r.get(comp.key)
            if old is None or old.val < comp.val:
                t.r[comp.key] = comp
        for t in w:
            t.w = comp
            t.r = {}
        return comp

    def barrier(self):
        deps = [Dep(F.sem, F.count, F.name, "e_" + F.name) for F in self.E.values() if F.count > 0]
        for qn, q in self.dq.items():
            for slot in range(min(q["n"], self.NDMA)):
                last = ((q["n"] - 1 - slot) // self.NDMA) + 1
                deps.append(Dep(q["sems"][slot], 16 * last, None, f"d_{qn}{slot}"))
        for E in self.E.values():
            for d in deps:
                if d.eng == E.name:
                    continue
                self._wait(E, d)

    def final_wait(self, eng, deps):
        E = self.E[eng]
        for d in deps:
            self._wait(E, d)


class _Idx:
    def __init__(self, fn):
        self.fn = fn

    def __getitem__(self, j):
        return self.fn(j)


class Builder:
    def __init__(self, S, dbg=None):
        self.S = S
        self.NT = S // P
        self.NO = self.NT // 2
        self.dbg = dbg or ()
        nc = bass.Bass("TRN2", target_bir_lowering=False)
        self.nc = nc
        self.s = Sched(nc)
        NO, NT = self.NO, self.NT
        dt = nc.dram_tensor
        self.d_x = dt("x", [S, D], F32, kind="ExternalInput").ap()
        self.d_consts3 = dt("consts3", [3, P, NCONST], F32, kind="ExternalInput").ap()
        self.d_rope_all = dt("rope_all", [NT, P, 64], F32, kind="ExternalInput").ap()
        self.d_rope_own_in = dt("rope_own", [NO, P, 64], F32, kind="ExternalInput").ap()
        self.D_w_in = dt("w_in", [2, D, IN_W], F32, kind="ExternalInput").ap()
        self.D_w_branch = dt("w_branch", [2, 4, 256, D], F32, kind="ExternalInput").ap()
        self.D_w_out = dt("w_out", [2, D, D], F32, kind="ExternalInput").ap()
        self.D_w_up = dt("w_up", [2, D, DFF], F32, kind="ExternalInput").ap()
        self.D_w_down = dt("w_down", [2, DFF, D], F32, kind="ExternalInput").ap()
        self.d_y = dt("y", [NO * P, D], F32, kind="ExternalOutput").ap()
        self.d_x1 = dt("x1_scratch", [S, D], F32).ap()
        self.d_dbg = {}
        for name, shape in self.dbg:
            self.d_dbg[name] = dt("dbg_" + name, list(shape), F32, kind="ExternalOutput").ap()

    def mm(self, out, lhsT, rhs, start, stop, r, w, sig=None):
        sig = stop if sig is None else sig
        return self.s.op("pe", lambda e: e.matmul(out, lhsT, rhs, start=start, stop=stop,
                                                  skip_group_check=True), r=r, w=w, sig=sig)

    def tr(self, out, in_, ident, r, w, sig=True):
        return self.s.op("pe", lambda e: e.transpose(out, in_, ident), r=r, w=w, sig=sig)

    def act(self, out, in_, func, r, w, **kw):
        return self.s.op("act", lambda e: e.activation(out=out, in_=in_, func=func, **kw), r=r, w=w)

    def tt(self, eng, out, in0, in1, op, r, w):
        return self.s.op(eng, lambda e: e.tensor_tensor(out=out, in0=in0, in1=in1, op=op), r=r, w=w)

    def ts(self, eng, out, in0, s1, s2, op0, op1=None, r=(), w=(), accum_out=None):
        def f(e):
            kw = {}
            if op1 is not None:
                kw["op1"] = op1
            if accum_out is not None:
                kw["accum_out"] = accum_out
            return e.tensor_scalar(out=out, in0=in0, scalar1=s1, scalar2=s2, op0=op0, **kw)
        return self.s.op(eng, f, r=r, w=w)

    def stt(self, eng, out, in0, scalar, in1, op0, op1, r, w):
        return self.s.op(eng, lambda e: e.scalar_tensor_tensor(out=out, in0=in0, scalar=scalar, in1=in1,
                                                               op0=op0, op1=op1), r=r, w=w)

    def cp(self, eng, out, in_, r, w):
        if eng == "act":
            return self.s.op("act", lambda e: e.copy(out=out, in_=in_), r=r, w=w)
        return self.s.op(eng, lambda e: e.tensor_copy(out=out, in_=in_), r=r, w=w)

    def red(self, eng, out, in_, op, r, w, absval=False):
        return self.s.op(eng, lambda e: e.tensor_reduce(out=out, in_=in_, axis=AX.X, op=op,
                                                        apply_absolute_value=absval), r=r, w=w)

    def memset(self, eng, ap, val, w):
        return self.s.op(eng, lambda e: e.memset(ap, val), r=(), w=w)

    def dma(self, q, out, in_, r, w):
        h = {"sp": self.nc.sync, "pool": self.nc.gpsimd, "act": self.nc.scalar}[q]
        return self.s.op(q, lambda e: h.dma_start(out=out, in_=in_), r=r, w=w, dma=True)

    def C(self, name, a=0, b=None):
        o, wd = COFF[name]
        b = wd if b is None else b
        return self.c32[:, o + a:o + b]

    def rstd(self, ss, n, width, eps, out, r, w):
        tv = self.t_rs
        self.ts("pool", self.rs_tmp[:, 0:n], ss, 1.0 / width, eps, ALU.mult, ALU.add, r=r, w=[tv])
        self.tt("pool", out, self.rs_tmp[:, 0:n], self.C("neghalf", 0, n), ALU.pow, r=[tv, self.t_c32], w=w)

    def norm_transpose(self, x_ap, t_x, gainT, hT, t_hT, slot):
        xn, t_xn = self.xn[slot], self.t_xn[slot]
        ss, t_ss = self.nt_ss[slot], self.t_nt_ss[slot]
        rs, t_rsd = self.nt_rs[slot], self.t_nt_rs[slot]
        self.act(xn[:], x_ap, AF.Square, r=[t_x], w=[t_xn, t_ss], accum_out=ss[:, 0:1])
        self.rstd(ss[:, 0:1], 1, D, EPS, rs[:, 0:1], r=[t_ss], w=[t_rsd])
        self.ts("dve", xn[:], x_ap, rs[:, 0:1], None, ALU.mult, r=[t_x, t_rsd], w=[t_xn])
        bT, t_bT = self.bankT, self.t_bank[0]
        for kc in range(KC):
            self.tr(bT[:, kc * P:(kc + 1) * P], xn[:, kc * P:(kc + 1) * P], self.ident_bf[:],
                    r=[t_xn, self.t_cbf], w=[t_bT], sig=(kc == KC - 1))
        self.tt("dve", hT, bT[:].rearrange("p (k t) -> p k t", k=KC),
                gainT.unsqueeze(2).to_broadcast([P, KC, P]), ALU.mult, r=[t_bT, self.t_c32], w=[t_hT])

    def headnorm(self, src, H, gain, out, r, w, eng="dve"):
        t = self.t_hn
        sq = self.hn_sq[:, 0:H * 64]
        self.tt(eng, sq, src, src, ALU.mult, r=r, w=[t])
        self.red(eng, self.hn_ss[:, 0:H], sq.rearrange("p (h e) -> p h e", h=H), ALU.add, r=[t], w=[t])
        self.rstd(self.hn_ss[:, 0:H], H, 64, EPS, self.hn_rs[:, 0:H], r=[t], w=[t])
        self.tt(eng, out.rearrange("p (h e) -> p h e", h=H), src.rearrange("p (h e) -> p h e", h=H),
                self.hn_rs[:, 0:H].unsqueeze(2).to_broadcast([P, H, 64]), ALU.mult, r=list(r) + [t], w=w)
        if gain is not None:
            self.tt(eng, out, out, gain, ALU.mult, r=list(w) + [self.t_c32], w=w)

    def rotary(self, src, H, rope, t_rope, out, r, w, eng="pool", scratch=None):
        ro_a, ro_b, t = scratch if scratch is not None else (self.ro_a, self.ro_b, self.t_ro)
        s4 = src.rearrange("p (h two e) -> p h two e", h=H, two=2)
        a4 = ro_a[:, 0:H * 64].rearrange("p (h two e) -> p h two e", h=H, two=2)
        b4 = ro_b[:, 0:H * 64].rearrange("p (h two e) -> p h two e", h=H, two=2)
        o4 = out.rearrange("p (h two e) -> p h two e", h=H, two=2)
        cosb = rope[:, 0:32].unsqueeze(1).unsqueeze(1).to_broadcast([P, H, 2, 32])
        sinb = rope[:, 32:64].unsqueeze(1).to_broadcast([P, H, 32])
        rr = list(r) + [t_rope]
        self.tt(eng, a4, s4, cosb, ALU.mult, r=rr, w=[t])
        self.tt(eng, b4[:, :, 0, :], s4[:, :, 1, :], sinb, ALU.mult, r=rr, w=[t])
        self.tt(eng, b4[:, :, 1, :], s4[:, :, 0, :], sinb, ALU.mult, r=rr, w=[t])
        self.tt(eng, o4[:, :, 0, :], a4[:, :, 0, :], b4[:, :, 0, :], ALU.subtract, r=[t], w=w)
        self.tt(eng, o4[:, :, 1, :], a4[:, :, 1, :], b4[:, :, 1, :], ALU.add, r=[t], w=w)

    def load_w(self, dst, src, t_dst):
        return self.dma("pool", dst, src, r=(), w=[t_dst])

    def build(self):
        nc, s = self.nc, self.s
        NO, NT, S = self.NO, self.NT, self.S
        with ExitStack() as top:
            sb = lambda name, shape, dtp: _alloc(nc, top, name, shape, dtp)
            self.bank = [top.enter_context(nc.psum_tensor(f"bank{i}", [P, 512], F32)) for i in range(1, 8)]
            self.bankT = top.enter_context(nc.psum_tensor("bankT", [P, 1024], BF16))
            self.t_bank = [Tok(f"bank{i}") for i in range(8)]
            self.c32 = sb("c32", [P, NC_TOP], F32)
            self.t_c32 = Tok("c32")
            self.ident_bf = sb("ident_bf", [P, P], BF16)
            self.irep_bf = sb("irep_bf", [P, 4, P], BF16)
            self.maskT_bf = sb("maskT_bf", [P, 4, P], BF16)
            self.ones_bf = sb("ones_bf", [P, P], BF16)
            self.t_cbf = Tok("cbf")
            self.x_own = sb("x_own", [P, NO, D], F32)
            self.t_x = [Tok(f"x{j}") for j in range(NO)]
            self.yTn = [None] * 4
            self.yTn[1] = sb("yT1", [P, 2, NO * P], BF16)
            self.t_yT = [[Tok(f"yT{n}_{j}") for j in range(NO)] for n in range(4)]
            self.hl2 = sb("hl2", [P, NT, KC, 2], BF16)
            self.t_hl2 = [Tok(f"hl2_{g}") for g in range(NT)]
            xn0 = sb("xn0", [P, D], BF16)
            self.xn = [xn0, xn0]
            t_xn0 = Tok()
            self.t_xn = [t_xn0, t_xn0]
            self.nt_ss = [sb(f"ntss{i}", [P, 1], F32) for i in range(2)]
            self.t_nt_ss = [Tok() for _ in range(2)]
            self.nt_rs = [sb(f"ntrs{i}", [P, 1], F32) for i in range(2)]
            self.t_nt_rs = [Tok() for _ in range(2)]
            self.t_junk_act = Tok()
            self.t_junk_dve = Tok()
            self.rs_tmp = sb("rs_tmp", [P, 8], F32)
            self.t_rs = Tok()
            self.hn_sq = sb("hn_sq", [P, 512], F32)
            self.hn_ss = sb("hn_ss", [P, 8], F32)
            self.hn_rs = sb("hn_rs", [P, 8], F32)
            self.t_hn = Tok()
            self.t_ro = Tok()

            import os
            ph = os.environ.get("KPH", "att,mlstm,sgu,conv,merge,ffn").split(",")
            npass = int(os.environ.get("KNP", "3"))
            passes = [(0, 0), (0, 1), (1, None)][:npass]
            outs = []
            for pi, (l, q) in enumerate(passes):
                s.barrier()
                self.cur_consts = self.d_consts3[pi]
                self.dma("sp", self.c32[:], self.cur_consts[:, 0:NC_TOP], r=(), w=[self.t_c32])
                if pi == 0:
                    self.cp("dve", self.ident_bf[:], self.C("ident"), r=[self.t_c32], w=[self.t_cbf])
                    self.cp("dve", self.ones_bf[:], self.C("ones"), r=[self.t_c32], w=[self.t_cbf])
                    for h in range(4):
                        self.cp("dve", self.irep_bf[:, h, :], self.C("ident"), r=[self.t_c32], w=[self.t_cbf])
                        self.cp("dve", self.maskT_bf[:, h, :], self.C("maskT"), r=[self.t_c32], w=[self.t_cbf])
                self.d_w_in, self.d_w_branch, self.d_w_out = self.D_w_in[l], self.D_w_branch[l], self.D_w_out[l]
                self.d_w_up, self.d_w_down = self.D_w_up[l], self.D_w_down[l]
                if q is not None:
                    self.d_xf = self.d_x
                    self.d_rope_own = _Idx(lambda j, q=q: self.d_rope_all[2 * j + q])
                    for j in range(NO):
                        g = 2 * j + q
                        self.dma("sp", self.x_own[:, j, :], self.d_x[g * P:(g + 1) * P, :], r=(), w=[self.t_x[j]])
                else:
                    self.d_xf = self.d_x1
                    self.d_rope_own = self.d_rope_own_in
                    with ExitStack() as es:
                        xb = [_alloc(nc, es, f"xblend{i}", [P, D], F32) for i in range(2)]
                        t_xb = [Tok() for _ in range(2)]
                        for j in range(NO):
                            sl = j % 2
                            self.dma("sp", self.x_own[:, j, :], self.d_x1[(2 * j) * P:(2 * j + 1) * P, :], r=(), w=[self.t_x[j]])
                            self.dma("sp", xb[sl][:], self.d_x1[(2 * j + 1) * P:(2 * j + 2) * P, :], r=(), w=[t_xb[sl]])
                            self.tt("dve", xb[sl][:], xb[sl][:], self.x_own[:, j, :], ALU.subtract,
                                    r=[t_xb[sl], self.t_x[j]], w=[t_xb[sl]])
                            self.stt("dve", self.x_own[:, j, :], xb[sl][:], self.C("pflag"), self.x_own[:, j, :],
                                     ALU.mult, ALU.add, r=[t_xb[sl], self.t_x[j], self.t_c32], w=[self.t_x[j]])
                        s.barrier()
                if "att" in ph:
                    self.phase_attention()
                s.barrier()
                with ExitStack() as mid:
                    self.hT_own = _alloc(nc, mid, "hT_own", [P, KC, NO * P], BF16)
                    for n in (0, 2, 3):
                        self.yTn[n] = _alloc(nc, mid, f"yT{n}", [P, 2, NO * P], BF16)
                    self.t_hT = [Tok(f"hT{j}") for j in range(NO)]
                    self.phase_hT(self.C("lnmixT"))
                    if "mlstm" in ph:
                        self.phase_mlstm()
                    s.barrier()
                    if "sgu" in ph:
                        self.phase_sgu()
                    s.barrier()
                    if "conv" in ph:
                        self.phase_conv()
                    s.barrier()
                    if "merge" in ph:
                        self.phase_merge()
                    s.barrier()
                    if "ffn" in ph:
                        self.phase_hT(self.C("lnmlpT"))
                        self.phase_ffn()
                    s.barrier()
                last = (pi == len(passes) - 1)
                for j in range(NO):
                    if last:
                        dst = self.d_y[j * P:(j + 1) * P, :]
                    else:
                        g = 2 * j + q
                        dst = self.d_x1[g * P:(g + 1) * P, :]
                    outs.append(self.dma("sp", dst, self.x_own[:, j, :], r=[self.t_x[j]], w=()))
            s.barrier()
            s.final_wait("sp", outs + self.dbg_deps)
        return nc

    dbg_deps = []

    def dump(self, name, src_ap, toks, dst_slice=None):
        if name not in self.d_dbg:
            return
        dst = self.d_dbg[name] if dst_slice is None else dst_slice(self.d_dbg[name])
        self.dbg_deps = self.dbg_deps + [self.dma("sp", dst, src_ap, r=toks, w=())]

    def phase_hT(self, gainT):
        for j in range(self.NO):
            self.norm_transpose(self.x_own[:, j, :], self.t_x[j], gainT,
                                self.hT_own[:, :, j * P:(j + 1) * P], self.t_hT[j], j % 2)

    def phase_attention(self):
        nc, s = self.nc, self.s
        NO, NT, S = self.NO, self.NT, self.S
        B = self.bank
        tb = self.t_bank
        with ExitStack() as es:
            sb = lambda name, shape, dtp: _alloc(nc, es, name, shape, dtp)
            WA = sb("WA", [P, KC, 1352], BF16)
            t_WA = Tok("WA")
            wv = self.d_w_in.rearrange("(k p) c -> p k c", p=P)
            for (dst0, src0, n) in ((0, OFF["b_q"], 256), (256, OFF["b_wi"], 8), (264, OFF["b_qi"], 512),
                                    (776, OFF["b_k"], 512), (1288, OFF["b_ki"], 64)):
                self.load_w(WA[:, :, dst0:dst0 + n], wv[:, :, src0:src0 + n], t_WA)
            KT = sb("KT", [P, 2, S], BF16)
            t_KT = [Tok(f"KT{g}") for g in range(NT)]
            VA = sb("VA", [P, NT, 4, 65], BF16)
            t_VA = [Tok(f"VA{g}") for g in range(NT)]
            KI2 = sb("KI2", [P, S], BF16)
            t_KI = [Tok(f"KI{g}") for g in range(NT)]
            xt0 = sb("xt0", [P, D], F32)
            xt = [xt0, xt0]
            t_xt0 = Tok()
            t_xt = [t_xt0, t_xt0]
            hTg0 = sb("hTg0", [P, KC, P], BF16)
            hT = [hTg0, hTg0]
            t_hTg0 = Tok()
            t_hT = [t_hTg0, t_hTg0]
            rope_g = [sb(f"ropeg{i}", [P, 64], F32) for i in range(2)]
            t_rope_g = [Tok() for _ in range(2)]
            rope_o = sb("ropeo", [P, 64], F32)
            t_rope_o = Tok()
            ksb = sb("ksb", [P, 256], F32); t_ksb = Tok()
            kisb = sb("kisb", [P, 64], F32); t_kisb = Tok()
            kr = sb("kr", [P, 256], BF16); t_kr = Tok()
            ki2 = sb("ki2", [P, 128], BF16); t_ki2 = Tok()
            bn6 = sb("bn6", [P, 8], F32); t_bn = Tok()
            SC = sb("SC", [P, S], F32); t_SC = Tok("SC")
            self.ro_a = SC[:, 0:512]
            self.ro_b = SC[:, 512:1024]
            self.t_ro = t_SC
            rog = sb("rog", [P, 1024], F32)
            sc_g = (rog[:, 0:512], rog[:, 512:1024], Tok("rog"))
            sc_o = sc_g
            MB = sb("MB", [P, S], BF16); t_MB = Tok("MB")
            qsb = sb("qsb", [P, 264], F32); t_qsb = Tok()
            qisb = sb("qisb", [P, 512], F32); t_qisb = Tok()
            qr = sb("qr", [P, 256], BF16); t_qr = Tok()
            qir = sb("qir", [P, 512], BF16); t_qir = Tok()
            QTp2 = [sb(f"QTp{i}", [P, 4, P], BF16) for i in range(2)]; t_QTp2 = [Tok() for _ in range(2)]

            QiTp = sb("QiTp", [P, 8, P], BF16); t_QiTp = Tok()
            Dg = sb("Dg", [P, 8, P], BF16); t_Dg = Tok()
            wab = sb("wab", [P, 8], F32); wsg = sb("wsg", [P, 8], F32); wtmp = sb("wtmp", [P, 8], F32); t_w8 = Tok()
            Rr = [sb(f"Rr{i}", [P, 512], BF16) for i in range(4)]
            t_Rr = [Tok() for _ in range(4)]
            PT = [sb(f"PT{i}", [P, 4, P], BF16) for i in range(2)]
            t_PT = [Tok() for _ in range(2)]
            bs = sb("bs", [P, 8], F32); t_bs = Tok()
            hwt = sb("hwt", [P, 64], F32); t_hw = Tok()
            t_cnt = Tok()
            hk = sb("hk", [P, NBISECT + 2], F32)
            cnt = sb("cnt", [P, 1], F32)
            osb = sb("osb", [P, 4, 65], F32); t_osb = Tok()
            rden = sb("rden", [P, 4], F32)
            yb = sb("yb", [P, 256], BF16); t_yb = Tok()

            self.memset("pool", VA[:], 1.0, w=t_VA)
            for i in range(2):
                self.memset("pool", QTp2[i][:], 0.0, w=[t_QTp2[i]])
            self.memset("pool", QiTp[:], 0.0, w=[t_QiTp])
            for k in range(NBISECT + 2):
                self.memset("pool", hk[:, k:k + 1], 2.0 ** (-(k + 1)), w=[t_bs])

            def proc_global(g):
                sl = g % 2
                self.dma("sp", xt[sl][:], self.d_xf[g * P:(g + 1) * P, :], r=(), w=[t_xt[sl]])
                self.dma("sp", rope_g[sl][:], self.d_rope_all[g], r=(), w=[t_rope_g[sl]])
                self.norm_transpose(xt[sl][:], t_xt[sl], self.C("lnmixT"), hT[sl][:], t_hT[sl], sl)
                self.cp("pool", self.hl2[:, g, :, :], hT[sl][:, :, P - 2:P], r=[t_hT[sl]], w=[self.t_hl2[g]])
                for kc in range(KC):
                    self.mm(B[0][:, 0:512], hT[sl][:, kc, :], WA[:, kc, 776:1288], kc == 0, kc == KC - 1,
                            r=[t_hT[sl], t_WA], w=[tb[1]])
                for kc in range(KC):
                    self.mm(B[1][:, 0:64], hT[sl][:, kc, :], WA[:, kc, 1288:1352], kc == 0, kc == KC - 1,
                            r=[t_hT[sl], t_WA], w=[tb[2]])
                self.cp("dve", ksb[:], B[0][:, 0:256], r=[tb[1]], w=[t_ksb])
                self.cp("dve", VA[:, g, :, 0:64], B[0][:, 256:512].rearrange("p (h e) -> p h e", h=4),
                        r=[tb[1]], w=[t_VA[g]])
                self.cp("dve", kisb[:], B[1][:, 0:64], r=[tb[2]], w=[t_kisb])
                self.headnorm(ksb[:], 4, self.C("kn"), ksb[:], r=[t_ksb], w=[t_ksb])
                self.rotary(ksb[:], 4, rope_g[sl], t_rope_g[sl], kr[:], r=[t_ksb], w=[t_kr], eng="dve", scratch=sc_g)
                bT = self.bankT
                for pr in range(2):
                    self.tr(bT[:, pr * P:(pr + 1) * P], kr[:, pr * P:(pr + 1) * P], self.ident_bf[:],
                            r=[t_kr, self.t_cbf], w=[tb[0]], sig=False)
                self.red("dve", bn6[:, 0:1], kisb[:], ALU.add, r=[t_kisb], w=[t_bn])
                self.ts("dve", bn6[:, 1:2], bn6[:, 0:1], 1.0 / 64, None, ALU.mult, r=[t_bn], w=[t_bn])
                self.ts("dve", kisb[:], kisb[:], bn6[:, 1:2], None, ALU.subtract, r=[t_bn, t_kisb], w=[t_kisb])
                self.tt("dve", self.hn_sq[:, 0:64], kisb[:], kisb[:], ALU.mult, r=[t_kisb], w=[self.t_hn])
                self.red("dve", bn6[:, 2:3], self.hn_sq[:, 0:64], ALU.add, r=[self.t_hn], w=[t_bn])
                self.rstd(bn6[:, 2:3], 1, 64, EPS, self.hn_rs[:, 0:1], r=[t_bn], w=[self.t_hn])
                self.ts("dve", kisb[:], kisb[:], self.hn_rs[:, 0:1], None, ALU.mult, r=[self.t_hn, t_kisb], w=[t_kisb])
                self.tt("dve", kisb[:], kisb[:], self.C("kidx"), ALU.mult, r=[t_kisb, self.t_c32], w=[t_kisb])
                self.rotary(kisb[:], 1, rope_g[sl], t_rope_g[sl], ki2[:, 0:64], r=[t_kisb], w=[t_ki2], eng="dve", scratch=sc_g)
                self.cp("dve", ki2[:, 64:128], ki2[:, 0:64], r=[t_ki2], w=[t_ki2])
                self.tr(bT[:, 2 * P:3 * P], ki2[:], self.ident_bf[:], r=[t_ki2, self.t_cbf], w=[tb[0]], sig=True)
                self.cp("dve", KT[:, :, g * P:(g + 1) * P], bT[:, 0:2 * P].rearrange("p (a t) -> p a t", a=2),
                        r=[tb[0]], w=[t_KT[g]])
                self.cp("dve", KI2[:, g * P:(g + 1) * P], bT[:, 2 * P:3 * P], r=[tb[0]], w=[t_KI[g]])

            def proc_own(j):
                QTp, t_QTp = QTp2[j % 2], t_QTp2[j % 2]
                sl = j % 2
                xj = self.x_own[:, j, :]
                self.dma("sp", rope_o[:], self.d_rope_own[j], r=(), w=[t_rope_o])
                self.norm_transpose(xj, self.t_x[j], self.C("lnmixT"), hT[sl][:], t_hT[sl], sl)
                for kc in range(KC):
                    self.mm(B[0][:, 0:264], hT[sl][:, kc, :], WA[:, kc, 0:264], kc == 0, kc == KC - 1,
                            r=[t_hT[sl], t_WA], w=[tb[1]])
                for kc in range(KC):
                    self.mm(B[1][:, 0:512], hT[sl][:, kc, :], WA[:, kc, 264:776], kc == 0, kc == KC - 1,
                            r=[t_hT[sl], t_WA], w=[tb[2]])
                self.cp("dve", qsb[:], B[0][:, 0:264], r=[tb[1]], w=[t_qsb])
                self.cp("dve", qisb[:], B[1][:, 0:512], r=[tb[2]], w=[t_qisb])
                self.headnorm(qsb[:, 0:256], 4, self.C("qn"), qsb[:, 0:256], r=[t_qsb], w=[t_qsb])
                self.rotary(qsb[:, 0:256], 4, rope_o, t_rope_o, qr[:], r=[t_qsb], w=[t_qr], eng="dve", scratch=sc_g)
                self.rotary(qisb[:], 8, rope_o, t_rope_o, qir[:], r=[t_qisb], w=[t_qir], eng="dve", scratch=sc_o)
                bT = self.bankT
                for pr in range(2):
                    self.tr(bT[:, pr * P:(pr + 1) * P], qr[:, pr * P:(pr + 1) * P], self.ident_bf[:],
                            r=[t_qr, self.t_cbf], w=[tb[0]], sig=False)
                for pr in range(4):
                    self.tr(bT[:, (2 + pr) * P:(3 + pr) * P], qir[:, pr * P:(pr + 1) * P], self.ident_bf[:],
                            r=[t_qir, self.t_cbf], w=[tb[0]], sig=(pr == 3))
                for h in range(4):
                    lo = (h % 2) * 64
                    self.cp("dve", QTp[lo:lo + 64, h, :], bT[lo:lo + 64, (h // 2) * P:(h // 2 + 1) * P],
                            r=[tb[0]], w=[t_QTp])
                for h in range(8):
                    lo = (h % 2) * 64
                    self.cp("dve", QiTp[lo:lo + 64, h, :],
                            bT[lo:lo + 64, (2 + h // 2) * P:(3 + h // 2) * P], r=[tb[0]], w=[t_QiTp])
                wsrc = qsb[:, 256:264]
                self.ts("dve", wsg[:], wsrc, -1.0, None, ALU.mult, r=[t_qsb], w=[t_w8])
                self.tt("dve", wab[:], wsrc, wsg[:], ALU.max, r=[t_qsb, t_w8], w=[t_w8])
                self.ts("dve", wab[:], wab[:], float(8 ** -0.5 * 64 ** -0.5), None, ALU.mult, r=[t_w8], w=[t_w8])
                self.ts("dve", wsg[:], wsrc, 0.0, None, ALU.is_gt, r=[t_qsb, t_w8], w=[t_w8])
                self.ts("dve", wtmp[:], wsrc, 0.0, None, ALU.is_lt, r=[t_qsb], w=[t_w8])
                self.tt("dve", wsg[:], wsg[:], wtmp[:], ALU.subtract, r=[t_w8], w=[t_w8])
                for h in range(8):
                    self.ts("dve", Dg[:, h, :], self.C("ident"), wsg[:, h:h + 1], None, ALU.mult,
                            r=[t_w8, self.t_c32], w=[t_Dg])

            def proc_own_idx(j):
                nk = (j + 1) * 256
                nblk = (nk + 511) // 512
                for b in range(nblk):
                    k0 = b * 512
                    kw = min(512, nk - k0)
                    kg = list(range(k0 // P, (k0 + kw) // P))
                    def emit_d(hp, last):
                        for hh in (2 * hp, 2 * hp + 1):
                            ri_ = (hp % 2) * 2 + (hh % 2)
                            fin = last and hh == 7
                            self.mm(B[4][:, 0:kw], Dg[:, hh, :], Rr[ri_][:, 0:kw], hh == 0, fin,
                                    r=[t_Dg, t_Rr[ri_]], w=[tb[5]], sig=fin)
                    for hp in range(4):
                        for hh in (2 * hp, 2 * hp + 1):
                            rb = 3 + (hh % 2)
                            lo = (hh % 2) * 64
                            self.mm(B[rb - 1][:, 0:kw], QiTp[lo:lo + 64, hh, :], KI2[lo:lo + 64, k0:k0 + kw], True, True,
                                    r=[t_QiTp] + [t_KI[g] for g in kg], w=[tb[rb]])
                        for hh in (2 * hp, 2 * hp + 1):
                            rb = 3 + (hh % 2)
                            ri_ = (hp % 2) * 2 + (hh % 2)
                            self.ts("dve", Rr[ri_][:, 0:kw], B[rb - 1][:, 0:kw], 0.0, wab[:, hh:hh + 1], ALU.max, ALU.mult,
                                    r=[tb[rb], t_w8], w=[t_Rr[ri_]])
                        if hp >= 1:
                            emit_d(hp - 1, False)
                    emit_d(3, True)
                    self.cp("dve", SC[:, k0:k0 + kw], B[4][:, 0:kw], r=[tb[5]], w=[t_SC])
                self.red("dve", bs[:, 0:1], SC[:, 0:nk], ALU.max, r=[t_SC], w=[t_bs], absval=True)
                self.tt("dve", SC[:, nk - 256:nk], SC[:, nk - 256:nk], self.C("cmask"), ALU.add,
                        r=[t_SC, self.t_c32], w=[t_SC])
                self.ts("dve", bs[:, 1:2], bs[:, 0:1], 2.002, 2e-20, ALU.mult, ALU.add, r=[t_bs], w=[t_bs])
                hw = hwt
                self.ts("dve", hw[:, 0:NBISECT + 2], hk[:, 0:NBISECT + 2], bs[:, 1:2], None, ALU.mult,
                        r=[t_bs], w=[t_hw])
                self.ts("dve", hw[:, 32:32 + NBISECT + 2], hw[:, 0:NBISECT + 2], -0.5, None, ALU.mult,
                        r=[t_hw], w=[t_hw])
                self.memset("dve", bs[:, 3:4], 0.0, w=[t_bs])

            def proc_own_b(j):
                QTp, t_QTp = QTp2[j % 2], t_QTp2[j % 2]
                nk = (j + 1) * 256
                nch = nk // P
                hw = hwt
                bT = self.bankT
                for k in range(NBISECT):
                    self.act(MB[:, 0:nk], SC[:, 0:nk], AF.Sign, r=[t_SC, t_bs], w=[t_MB, t_cnt],
                             bias=bs[:, 3:4], accum_out=cnt[:, 0:1])
                    self.act(bs[:, 4:5], cnt[:, 0:1], AF.Sign, r=[t_cnt], w=[t_bs], bias=float(nk - 512 + 0.5))
                    self.act(bs[:, 3:4], bs[:, 4:5], AF.Identity, r=[t_bs, t_hw], w=[t_bs],
                             scale=hw[:, 32 + k:33 + k], bias=bs[:, 3:4])
                self.stt("dve", bs[:, 5:6], bs[:, 3:4], hw[:, NBISECT:NBISECT + 1], self.C("neghalf", 0, 1),
                         ALU.add, ALU.mult, r=[t_bs, t_hw, self.t_c32], w=[t_bs])
                self.ts("dve", bs[:, 5:6], bs[:, 5:6], 2.0, None, ALU.mult, r=[t_bs], w=[t_bs])
                self.ts("dve", MB[:, 0:nk], SC[:, 0:nk], bs[:, 5:6], MBIAS, ALU.is_lt, ALU.mult,
                        r=[t_SC, t_bs], w=[t_MB])
                def emit_pv(c):
                    pt = c % 2
                    for h in range(4):
                        self.mm(B[4][:, h * 65:(h + 1) * 65], PT[pt][:, h, :], VA[:, c, h, :],
                                (c == 0 and h == 0), (c == nch - 1 and h == 3),
                                r=[t_PT[pt], t_VA[c]], w=[tb[5]], sig=(h == 3))

                for c in range(nch):
                    stb = 6 + (c % 2)
                    pt = c % 2
                    for h in range(4):
                        self.mm(B[stb - 1][:, h * P:(h + 1) * P], KT[:, h // 2, c * P:(c + 1) * P], QTp[:, h, :],
                                h == 0, False, r=[t_KT[c], t_QTp], w=[tb[stb]], sig=False)
                    self.mm(B[stb - 1][:, 0:512], MB[:, c * P:(c + 1) * P], self.irep_bf[:].rearrange("p a t -> p (a t)"),
                            False, True, r=[t_MB, self.t_cbf], w=[tb[stb]])
                    self.act(PT[pt][:].rearrange("p a t -> p (a t)"), B[stb - 1][:, 0:512], AF.Exp,
                             r=[tb[stb]], w=[t_PT[pt]], scale=0.125)
                    if c >= 1:
                        emit_pv(c - 1)
                emit_pv(nch - 1)
                self.cp("act", osb[:].rearrange("p h e -> p (h e)"), B[4][:, 0:260], r=[tb[5]], w=[t_osb])
                self.s.op("dve", lambda e: e.reciprocal(out=rden[:], in_=osb[:, :, 64]), r=[t_osb], w=[t_osb])
                self.tt("dve", yb[:].rearrange("p (h e) -> p h e", h=4), osb[:, :, 0:64],
                        rden[:].unsqueeze(2).to_broadcast([P, 4, 64]), ALU.mult, r=[t_osb], w=[t_yb])
                for pr in range(2):
                    self.tr(bT[:, pr * P:(pr + 1) * P], yb[:, pr * P:(pr + 1) * P], self.ident_bf[:],
                            r=[t_yb, self.t_cbf], w=[tb[0]], sig=(pr == 1))
                self.cp("act", self.yTn[1][:, :, j * P:(j + 1) * P], bT[:, 0:2 * P].rearrange("p (a t) -> p a t", a=2),
                        r=[tb[0]], w=[self.t_yT[1][j]])

            proc_global(0)
            proc_global(1)
            proc_own(0)
            for j in range(NO):
                proc_own_idx(j)
                if j + 1 < NO:
                    proc_global(2 * j + 2)
                    proc_global(2 * j + 3)
                    proc_own(j + 1)
                proc_own_b(j)

    def phase_mlstm(self):
        nc, s = self.nc, self.s
        NO, NT, S = self.NO, self.NT, self.S
        B, tb = self.bank, self.t_bank
        with ExitStack() as es:
            sb = lambda name, shape, dtp: _alloc(nc, es, name, shape, dtp)
            WC = sb("WC", [P, KC, 1032], BF16); t_WC = Tok("WC")
            wv = self.d_w_in.rearrange("(k p) c -> p k c", p=P)
            for (dst0, src0, n) in ((0, OFF["c_k"], 512), (512, OFF["c_i"], 8), (520, OFF["c_q"], 256),
                                    (776, OFF["c_o"], 256)):
                self.load_w(WC[:, :, dst0:dst0 + n], wv[:, :, src0:src0 + n], t_WC)
            xt = [sb(f"mxt{i}", [P, D], F32) for i in range(2)]; t_xt = [Tok() for _ in range(2)]
            hT = [sb(f"mhT{i}", [P, KC, P], BF16) for i in range(2)]; t_hT = [Tok() for _ in range(2)]
            Cst = sb("Cst", [P, 2, 65], F32); t_C = Tok("Cst")
            Ca = sb("Ca", [P, 2, 65], F32); t_Ca = Tok("Ca")
            Csel = sb("Csel", [P, 2, 65], BF16); t_Csel = Tok("Csel")
            kp = [sb(f"kp{i}", [P, 256], BF16) for i in range(3)]
            va = [sb(f"va{i}", [P, 4, 65], BF16) for i in range(3)]
            gsc = [sb(f"gsc{i}", [P, 24], F32) for i in range(3)]
            gsb = [sb(f"gsb{i}", [P, 16], BF16) for i in range(3)]
            t_g = [Tok() for _ in range(3)]
            qb = sb("qb", [P, 256], BF16); t_qb = Tok()
            qTp = sb("mqTp", [P, 4, P], BF16); t_qTp = Tok()
            kT = sb("mkT", [P, 2, P], BF16); t_kT = Tok()
            Sm = sb("Sm", [P, 4, P], BF16); t_Sm = Tok()
            ep = sb("ep", [P, 16], F32); t_ep = Tok()
            hc = sb("hc", [P, 256], F32); t_hc = Tok()
            so = sb("so", [P, 256], F32); t_so = Tok()
            yc = sb("yc", [P, 256], BF16); t_yc = Tok()
            self.memset("pool", Cst[:], 0.0, w=[t_C])
            self.memset("pool", qTp[:], 0.0, w=[t_qTp])

            def kv_gates(hTap, t_h, sl):
                for kc in range(KC):
                    self.mm(B[0][:, 0:512], hTap[:, kc, :], WC[:, kc, 0:512], kc == 0, kc == KC - 1,
                            r=[t_h, t_WC], w=[tb[1]])
                for kc in range(KC):
                    self.mm(B[1][:, 0:8], hTap[:, kc, :], WC[:, kc, 512:520], kc == 0, kc == KC - 1,
                            r=[t_h, t_WC], w=[tb[2]])
                G = gsc[sl]; tg = t_g[sl]
                self.tt("dve", G[:, 0:4], B[1][:, 4:8], self.C("fb"), ALU.add, r=[tb[2], self.t_c32], w=[tg])
                self.act(G[:, 0:4], G[:, 0:4], AF.Exp, r=[tg], w=[tg], scale=-1.0)
                self.act(G[:, 0:4], G[:, 0:4], AF.Ln, r=[tg], w=[tg], bias=1.0)
                Gb = gsb[sl]
                self.cp("dve", Gb[:, 0:4], G[:, 0:4], r=[tg], w=[tg])
                self.tt("dve", G[:, 20:24], G[:, 0:4], Gb[:, 0:4], ALU.subtract, r=[tg], w=[tg])
                self.cp("dve", Gb[:, 4:8], G[:, 20:24], r=[tg], w=[tg])
                self.mm(B[2][:, 0:4], self.maskT_bf[:, 0, :], Gb[:, 0:4], True, False, r=[tg, self.t_cbf], w=[tb[3]], sig=False)
                self.mm(B[2][:, 0:4], self.maskT_bf[:, 0, :], Gb[:, 4:8], False, False, r=[tg, self.t_cbf], w=[tb[3]], sig=False)
                self.mm(B[2][:, 4:8], self.ones_bf[:], Gb[:, 0:4], False, False, r=[tg, self.t_cbf], w=[tb[3]], sig=False)
                self.mm(B[2][:, 4:8], self.ones_bf[:], Gb[:, 4:8], False, True, r=[tg, self.t_cbf], w=[tb[3]])
                self.tt("dve", G[:, 4:8], B[1][:, 0:4], self.C("ib"), ALU.add, r=[tb[2], self.t_c32], w=[tg])
                self.tt("dve", G[:, 4:8], G[:, 4:8], B[2][:, 0:4], ALU.add, r=[tg, tb[3]], w=[tg])
                self.act(G[:, 8:12], G[:, 4:8], AF.Exp, r=[tg], w=[tg])
                self.act(G[:, 12:20], B[2][:, 0:8], AF.Exp, r=[tb[3]], w=[tg], scale=-1.0)
                self.act(kp[sl][:], B[0][:, 0:256], AF.Copy, r=[tb[1]], w=[tg], scale=0.125)
                self.tt("dve", va[sl][:, :, 0:64], B[0][:, 256:512].rearrange("p (h e) -> p h e", h=4),
                        G[:, 8:12].unsqueeze(2).to_broadcast([P, 4, 64]), ALU.mult, r=[tb[1], tg], w=[tg])
                self.cp("dve", va[sl][:, :, 64], G[:, 8:12], r=[tg], w=[tg])

            def state_update(sl):
                G = gsc[sl]; tg = t_g[sl]
                for h in range(4):
                    pr = h // 2
                    self.mm(B[3][:, h * 65:(h + 1) * 65], kp[sl][:, pr * P:(pr + 1) * P], va[sl][:, h, :],
                            True, True, r=[tg], w=[tb[4]], sig=(h == 3))
                for h in range(4):
                    lo = (h % 2) * 64
                    self.tt("dve", Cst[lo:lo + 64, h // 2, :], Cst[lo:lo + 64, h // 2, :],
                            B[3][lo:lo + 64, h * 65:(h + 1) * 65], ALU.add, r=[tb[4], t_C], w=[t_C])
                    self.ts("dve", Cst[lo:lo + 64, h // 2, :], Cst[lo:lo + 64, h // 2, :],
                            G[lo:lo + 64, 16 + h:17 + h], None, ALU.mult, r=[tg, t_C], w=[t_C])

            def proc_global(g, sl):
                self.dma("sp", xt[sl][:], self.d_xf[g * P:(g + 1) * P, :], r=(), w=[t_xt[sl]])
                self.norm_transpose(xt[sl][:], t_xt[sl], self.C("lnmixT"), hT[sl][:], t_hT[sl], sl)
                kv_gates(hT[sl], t_hT[sl], sl)
                state_update(sl)

            import os
            KO = int(os.environ.get('KO', '99'))
            def proc_own(j):
                hTo = self.hT_own[:, :, j * P:(j + 1) * P]
                for kc in range(KC):
                    self.mm(B[4][:, 0:512], hTo[:, kc, :], WC[:, kc, 520:1032], kc == 0, kc == KC - 1,
                            r=[self.t_hT[j], t_WC], w=[tb[5]])
                kv_gates(hTo, self.t_hT[j], 2)
                if KO < 1: return
                G = gsc[2]; tg = t_g[2]
                if KO < 2: return
                self.cp("act", qb[:], B[4][:, 0:256], r=[tb[5]], w=[t_qb])
                bT = self.bankT
                for pr in range(2):
                    for kc in range(KC):
                        self.mm(B[3][:, pr * P:(pr + 1) * P], WC[:, kc, 520 + pr * P:520 + (pr + 1) * P], hTo[:, kc, :],
                                kc == 0, kc == KC - 1, r=[self.t_hT[j], t_WC], w=[tb[4]], sig=False)
                for pr in range(2):
                    for kc in range(KC):
                        self.mm(B[3][:, (2 + pr) * P:(3 + pr) * P], WC[:, kc, pr * P:(pr + 1) * P], hTo[:, kc, :],
                                kc == 0, kc == KC - 1, r=[self.t_hT[j], t_WC], w=[tb[4]],
                                sig=(pr == 1 and kc == KC - 1))
                if KO < 3: return
                for h in range(4):
                    lo = (h % 2) * 64
                    self.cp("act", qTp[lo:lo + 64, h, :], B[3][lo:lo + 64, (h // 2) * P:(h // 2 + 1) * P],
                            r=[tb[4]], w=[t_qTp])
                self.act(kT[:].rearrange("p a t -> p (a t)"), B[3][:, 2 * P:4 * P], AF.Copy, r=[tb[4]], w=[t_kT], scale=0.125)
                if KO < 4: return
                for h in range(4):
                    self.mm(B[5][:, h * P:(h + 1) * P], kT[:, h // 2, :], qTp[:, h, :], True, True,
                            r=[t_kT, t_qTp], w=[tb[6]], sig=(h == 3))
                if KO < 5: return
                self.tt("dve", Sm[:], B[5][:, 0:512].rearrange("p (h t) -> p h t", h=4), self.maskT_bf[:], ALU.mult,
                        r=[tb[6], self.t_cbf], w=[t_Sm])
                if KO < 6: return
                for h in range(4):
                    self.mm(B[6][:, h * 65:(h + 1) * 65], qTp[:, h, :], Csel[:, h // 2, :], h == 0, False,
                            r=[t_qTp, t_Csel], w=[tb[7]], sig=False)
                    self.mm(B[6][:, h * 65:(h + 1) * 65], Sm[:, h, :], va[2][:, h, :], False, True,
                            r=[t_Sm, tg], w=[tb[7]], sig=(h == 3))
                if KO < 7: return
                acc = B[6][:, 0:260].rearrange("p (h e) -> p h e", h=4)
                self.tt("dve", ep[:, 0:4], acc[:, :, 64], G[:, 12:16], ALU.mult, r=[tb[7], tg], w=[t_ep])
                self.act(ep[:, 0:4], ep[:, 0:4], AF.Abs, r=[t_ep], w=[t_ep])
                self.ts("dve", ep[:, 0:4], ep[:, 0:4], 1.0, None, ALU.max, r=[t_ep], w=[t_ep])
                self.s.op("dve", lambda e: e.reciprocal(out=ep[:, 4:8], in_=ep[:, 0:4]), r=[t_ep], w=[t_ep])
                self.tt("dve", ep[:, 4:8], ep[:, 4:8], G[:, 12:16], ALU.mult, r=[t_ep, tg], w=[t_ep])
                self.tt("dve", hc[:].rearrange("p (h e) -> p h e", h=4), acc[:, :, 0:64],
                        ep[:, 4:8].unsqueeze(2).to_broadcast([P, 4, 64]), ALU.mult, r=[tb[7], t_ep], w=[t_hc])
                if KO < 8: return
                self.headnorm(hc[:], 4, self.C("mn"), hc[:], r=[t_hc], w=[t_hc])
                if KO < 9: return
                self.act(so[:], B[4][:, 256:512], AF.Exp, r=[tb[5]], w=[t_so], scale=-1.0)
                self.ts("pool", so[:], so[:], 1.0, None, ALU.add, r=[t_so], w=[t_so])
                self.s.op("dve", lambda e: e.reciprocal(out=so[:], in_=so[:]), r=[t_so], w=[t_so])
                self.tt("dve", yc[:], hc[:], so[:], ALU.mult, r=[t_hc, t_so], w=[t_yc])
                if KO < 10: return
                for pr in range(2):
                    self.tr(bT[:, pr * P:(pr + 1) * P], yc[:, pr * P:(pr + 1) * P], self.ident_bf[:],
                            r=[t_yc, self.t_cbf], w=[tb[0]], sig=(pr == 1))
                self.cp("act", self.yTn[2][:, :, j * P:(j + 1) * P], bT[:, 0:2 * P].rearrange("p (a t) -> p a t", a=2),
                        r=[tb[0]], w=[self.t_yT[2][j]])

            pf = self.C("pflag")
            import os
            km = os.environ.get("KM", "gso")
            for j in range(NO):
                self.cp("pool", Ca[:], Cst[:], r=[t_C], w=[t_Ca])
                if "g" in km:
                    proc_global(2 * j, 0)
                if "s" in km:
                    self.tt("dve", hc[:, 0:130].rearrange("p (a e) -> p a e", a=2), Cst[:], Ca[:], ALU.subtract,
                            r=[t_C, t_Ca], w=[t_hc])
                    self.stt("dve", Csel[:], hc[:, 0:130].rearrange("p (a e) -> p a e", a=2), pf, Ca[:], ALU.mult, ALU.add,
                             r=[t_hc, t_Ca, self.t_c32], w=[t_Csel])
                if "g" in km:
                    proc_global(2 * j + 1, 1)
                if "o" in km:
                    proc_own(j)

    def phase_sgu(self):
        nc = self.nc
        NO = self.NO
        B, tb = self.bank, self.t_bank
        with ExitStack() as es:
            sb = lambda name, shape, dtp: _alloc(nc, es, name, shape, dtp)
            WS = sb("WS", [P, KC, 512], BF16); t_WS = Tok()
            wv = self.d_w_in.rearrange("(k p) c -> p k c", p=P)
            self.load_w(WS[:], wv[:, :, 0:512], t_WS)
            WmT = sb("WmT", [P, 4, P], BF16); t_Wm = Tok()
            wT32 = sb("wT32", [P, 512], F32); t_wT32 = Tok()
            o_, w_ = COFF["sgu_wT"]
            self.dma("sp", wT32[:], self.cur_consts[:, o_:o_ + w_], r=(), w=[t_wT32])
            self.tt("dve", WmT[:], wT32[:].rearrange("p (g t) -> p g t", g=4),
                    self.C("maskT").unsqueeze(1).to_broadcast([P, 4, P]), ALU.mult, r=[self.t_c32, t_wT32], w=[t_Wm])
            x2 = sb("sx2", [P, 512], F32); xh = sb("sxh", [P, 512], F32); zz = sb("szz", [P, 512], F32)
            ge = sb("sge", [P, 512], F32); t_s = Tok()
            vn = sb("svn", [P, 256], BF16); t_vn = Tok()
            ssq = sb("sssq", [P, 2], F32)
            ya = sb("sya", [P, 256], BF16); t_ya = Tok()
            for j in range(NO):
                hTo = self.hT_own[:, :, j * P:(j + 1) * P]
                for kc in range(KC):
                    self.mm(B[0][:, 0:512], hTo[:, kc, :], WS[:, kc, :], kc == 0, kc == KC - 1,
                            r=[self.t_hT[j], t_WS], w=[tb[1]])
                xp = B[0][:, 0:512]
                self.act(x2[:], xp, AF.Square, r=[tb[1]], w=[t_s])
                self.act(xh[:], xp, AF.Copy, r=[tb[1]], w=[t_s], scale=0.5)
                self.ts("pool", x2[:], x2[:], 0.044715, 1.0, ALU.mult, ALU.add, r=[t_s], w=[t_s])
                self.tt("dve", zz[:], x2[:], xp, ALU.mult, r=[t_s, tb[1]], w=[t_s])
                self.act(zz[:], zz[:], AF.Tanh, r=[t_s], w=[t_s], scale=0.7978845608028654)
                self.stt("dve", ge[:], zz[:], 1.0, xh[:], ALU.add, ALU.mult, r=[t_s], w=[t_s])
                self.act(x2[:, 0:256], ge[:, 256:512], AF.Square, r=[t_s], w=[t_s], accum_out=ssq[:, 0:1])
                self.rstd(ssq[:, 0:1], 1, 256, EPS, ssq[:, 1:2], r=[t_s], w=[t_s])
                self.stt("dve", vn[:], ge[:, 256:512], ssq[:, 1:2], self.C("sgu_norm"), ALU.mult, ALU.mult,
                         r=[t_s, self.t_c32], w=[t_vn])
                for g in range(4):
                    self.mm(B[1][:, g * 64:(g + 1) * 64], WmT[:, g, :], vn[:, g * 64:(g + 1) * 64], True, True,
                            r=[t_Wm, t_vn], w=[tb[2]], sig=(g == 3))
                for g in range(4):
                    self.stt("dve", ya[:, g * 64:(g + 1) * 64], B[1][:, g * 64:(g + 1) * 64],
                             self.C("sgu_b", g, g + 1), ge[:, g * 64:(g + 1) * 64], ALU.add, ALU.mult,
                             r=[tb[2], t_s, self.t_c32], w=[t_ya])
                bT = self.bankT
                for pr in range(2):
                    self.tr(bT[:, pr * P:(pr + 1) * P], ya[:, pr * P:(pr + 1) * P], self.ident_bf[:],
                            r=[t_ya, self.t_cbf], w=[tb[0]], sig=(pr == 1))
                self.cp("act", self.yTn[0][:, :, j * P:(j + 1) * P], bT[:, 0:2 * P].rearrange("p (a t) -> p a t", a=2),
                        r=[tb[0]], w=[self.t_yT[0][j]])

    def phase_conv(self):
        nc = self.nc
        NO = self.NO
        B, tb = self.bank, self.t_bank
        with ExitStack() as es:
            sb = lambda name, shape, dtp: _alloc(nc, es, name, shape, dtp)
            WD = sb("WD", [P, KC, 768], BF16); t_WD = Tok()
            wv = self.d_w_in.rearrange("(k p) c -> p k c", p=P)
            self.load_w(WD[:], wv[:, :, OFF["d_b"]:OFF["d_b"] + 768], t_WD)
            hz = sb("hz", [P, KC, 2], BF16); t_hz = Tok()
            hd = sb("hd", [P, KC, 2], F32); hsel = sb("hsel", [P, KC, 2], BF16); t_hs = Tok()
            xs = sb("cxs", [P, 256], F32); zf = sb("czf", [P, 256], F32); t_z = Tok()
            z3 = sb("cz3", [P, 3, 256], F32); t_z3 = Tok()
            zp = sb("czp", [P, 2, 256], F32); t_zp = Tok()
            zpx = sb("czpx", [P, 256], F32); zpv = sb("czpv", [P, 256], F32); t_zq = Tok()
            ysb = sb("cys", [P, 256], F32); t_ys = Tok()
            yd = sb("cyd", [P, 256], BF16); t_yd = Tok()
            self.memset("pool", hz[:], 0.0, w=[t_hz])
            self.memset("pool", zp[:], 0.0, w=[t_zp])
            pf = self.C("pflag")
            cw_t = sb("cw_t", [P, 768], F32); t_cw = Tok()
            o_, w_ = COFF["conv"]
            self.dma("sp", cw_t[:], self.cur_consts[:, o_:o_ + w_], r=(), w=[t_cw])
            cw = cw_t
            for j in range(NO):
                hTo = self.hT_own[:, :, j * P:(j + 1) * P]
                if j == 0:
                    h0, t_h0 = hz[:], t_hz
                else:
                    h0, t_h0 = self.hl2[:, 2 * j - 1, :, :], self.t_hl2[2 * j - 1]
                h1, t_h1 = self.hl2[:, 2 * j, :, :], self.t_hl2[2 * j]
                self.tt("dve", hd[:], h1, h0, ALU.subtract, r=[t_h0, t_h1], w=[t_hs])
                self.stt("dve", hsel[:], hd[:], pf, h0, ALU.mult, ALU.add, r=[t_hs, t_h0, self.t_c32], w=[t_hs])
                for kc in range(KC):
                    self.mm(B[0][:, 0:512], hTo[:, kc, :], WD[:, kc, 0:512], kc == 0, kc == KC - 1,
                            r=[self.t_hT[j], t_WD], w=[tb[1]])
                for kc in range(KC):
                    self.mm(B[1][:, 0:256], hTo[:, kc, :], WD[:, kc, 512:768], kc == 0, kc == KC - 1,
                            r=[self.t_hT[j], t_WD], w=[tb[2]])
                for kc in range(KC):
                    self.mm(B[2][0:2, 0:512], hsel[:, kc, :], WD[:, kc, 256:768], kc == 0, kc == KC - 1,
                            r=[t_hs, t_WD], w=[tb[3]])
                self.cp("act", xs[:], B[1][:, 0:256], r=[tb[2]], w=[t_z])
                self.tt("dve", zf[:], B[0][:, 256:512], xs[:], ALU.mult, r=[tb[1], t_z], w=[t_z])
                for k in range(3):
                    self.tt("pool", z3[:, k, :], zf[:], cw[:, k * 256:(k + 1) * 256], ALU.mult,
                            r=[t_z, t_cw], w=[t_z3])
                self.cp("act", zpx[0:2, :], B[2][0:2, 256:512], r=[tb[3]], w=[t_zq])
                self.tt("dve", zpv[0:2, :], B[2][0:2, 0:256], zpx[0:2, :], ALU.mult, r=[tb[3], t_zq], w=[t_zq])
                self.tt("dve", zp[0:2, 0, :], zpv[0:2, :], cw[0:2, 0:256], ALU.mult, r=[t_zq, t_cw], w=[t_zp])
                self.tt("dve", zp[0:2, 1, :], zpv[0:2, :], cw[0:2, 256:512], ALU.mult, r=[t_zq, t_cw], w=[t_zp])
                yb_ = B[3][:, 0:256]
                self.mm(yb_, self.C("ident"), z3[:, 2, :], True, False, r=[t_z3, self.t_c32], w=[tb[4]], sig=False)
                self.mm(yb_, self.C("sh1"), z3[:, 1, :], False, False, r=[t_z3, self.t_c32], w=[tb[4]], sig=False)
                self.mm(yb_, self.C("sh2"), z3[:, 0, :], False, False, r=[t_z3, self.t_c32], w=[tb[4]], sig=False)
                self.mm(yb_, self.C("ba"), zp[:, 1, :], False, False, r=[t_zp, self.t_c32], w=[tb[4]], sig=False)
                self.mm(yb_, self.C("bb"), zp[:, 0, :], False, True, r=[t_zp, self.t_c32], w=[tb[4]])
                self.cp("act", ysb[:], yb_, r=[tb[4]], w=[t_ys])
                self.tt("dve", yd[:], B[0][:, 0:256], ysb[:], ALU.mult, r=[tb[1], t_ys], w=[t_yd])
                bT = self.bankT
                for pr in range(2):
                    self.tr(bT[:, pr * P:(pr + 1) * P], yd[:, pr * P:(pr + 1) * P], self.ident_bf[:],
                            r=[t_yd, self.t_cbf], w=[tb[0]], sig=(pr == 1))
                self.cp("act", self.yTn[3][:, :, j * P:(j + 1) * P], bT[:, 0:2 * P].rearrange("p (a t) -> p a t", a=2),
                        r=[tb[0]], w=[self.t_yT[3][j]])

    def phase_merge(self):
        nc, s = self.nc, self.s
        NO = self.NO
        B, tb = self.bank, self.t_bank
        TG = 4
        halves = [list(range(0, NO // 2)), list(range(NO // 2, NO))] if NO >= 8 else [list(range(NO))]
        wv = self.d_w_in.rearrange("(k p) c -> p k c", p=P)
        with ExitStack() as es0:
            HT = len(halves[0])
            mT = _alloc(nc, es0, "mT", [P, KC, HT * P], BF16)
            t_mT = [Tok() for _ in range(HT)]
            for tiles in halves:
                j0 = tiles[0]
                groups = [tiles[i:i + TG] for i in range(0, len(tiles), TG)]
                with ExitStack() as es:
                    sb = lambda name, shape, dtp: _alloc(nc, es, name, shape, dtp)
                    Wg = [sb(f"Wg{i}", [P, KC, 4, P], BF16) for i in range(2)]; t_Wg = [Tok() for _ in range(2)]
                    Wb = [sb(f"Wb{i}", [P, 2, 4, P], BF16) for i in range(2)]; t_Wb = [Tok() for _ in range(2)]
                    th = [sb(f"th{i}", [P, 512], F32) for i in range(2)]; t_th = [Tok() for _ in range(2)]
                    acc = sb("macc", [P, 512], F32); tmp = sb("mtmp", [P, 512], F32); t_acc = Tok(); t_tmp = Tok()
                    for cc in range(KC):
                        ws = cc % 2
                        for n in range(4):
                            c0 = OFF["g"] + n * D + cc * P
                            self.load_w(Wg[ws][:, :, n, :], wv[:, :, c0:c0 + P], t_Wg[ws])
                            self.load_w(Wb[ws][:, :, n, :],
                                        self.d_w_branch[n].rearrange("(f p) c -> p f c", p=P)[:, :, cc * P:(cc + 1) * P],
                                        t_Wb[ws])
                        for grp in groups:
                            T = len(grp) * P
                            c_lo = grp[0] * P
                            hts = [self.t_hT[j] for j in grp]
                            for n in range(4):
                                gb = n % 2
                                for kc in range(KC):
                                    self.mm(B[gb][:, 0:T], Wg[ws][:, kc, n, :], self.hT_own[:, kc, c_lo:c_lo + T],
                                            kc == 0, kc == KC - 1, r=[t_Wg[ws]] + hts, w=[tb[1 + gb]])
                                self.act(th[gb][:, 0:T], B[gb][:, 0:T], AF.Tanh, r=[tb[1 + gb]], w=[t_th[gb]], scale=0.5)
                                for f in range(2):
                                    self.mm(B[2 + gb][:, 0:T], Wb[ws][:, f, n, :], self.yTn[n][:, f, c_lo:c_lo + T],
                                            f == 0, f == 1, r=[t_Wb[ws]] + [self.t_yT[n][j] for j in grp],
                                            w=[tb[3 + gb]])
                                if n == 0:
                                    self.stt("dve", acc[:, 0:T], th[gb][:, 0:T], 1.0, B[2 + gb][:, 0:T], ALU.add, ALU.mult,
                                             r=[t_th[gb], tb[3 + gb]], w=[t_acc])
                                else:
                                    self.stt("dve", tmp[:, 0:T], th[gb][:, 0:T], 1.0, B[2 + gb][:, 0:T], ALU.add, ALU.mult,
                                             r=[t_th[gb], tb[3 + gb]], w=[t_tmp])
                                    self.tt("dve", acc[:, 0:T], acc[:, 0:T], tmp[:, 0:T], ALU.add,
                                            r=[t_acc, t_tmp], w=[t_acc])
                            m_lo = (grp[0] - j0) * P
                            self.act(mT[:, cc, m_lo:m_lo + T], acc[:, 0:T], AF.Copy, r=[t_acc],
                                     w=[t_mT[j - j0] for j in grp], scale=0.5)
                s.barrier()
                with ExitStack() as es:
                    Wo = _alloc(nc, es, "Wo", [P, KC, D], BF16); t_Wo = Tok()
                    self.load_w(Wo[:], self.d_w_out.rearrange("(k p) c -> p k c", p=P), t_Wo)
                    for j in tiles:
                        for half in range(2):
                            ob = 4 + half
                            for kc in range(KC):
                                self.mm(B[ob][:, 0:512], mT[:, kc, (j - j0) * P:(j - j0 + 1) * P],
                                        Wo[:, kc, half * 512:(half + 1) * 512], kc == 0, kc == KC - 1,
                                        r=[t_mT[j - j0], t_Wo], w=[tb[1 + ob]])
                            xs_ = self.x_own[:, j, half * 512:(half + 1) * 512]
                            self.tt("dve", xs_, xs_, B[ob][:, 0:512], ALU.add, r=[tb[1 + ob], self.t_x[j]], w=[self.t_x[j]])
                s.barrier()

    def phase_ffn(self):
        nc = self.nc
        NO = self.NO
        B, tb = self.bank, self.t_bank
        TG = 4
        groups = [list(range(i, min(i + TG, NO))) for i in range(0, NO, TG)]
        with ExitStack() as es:
            sb = lambda name, shape, dtp: _alloc(nc, es, name, shape, dtp)
            Wu = [sb(f"Wu{i}", [P, KC, 512], BF16) for i in range(2)]; t_Wu = [Tok() for _ in range(2)]
            Wd = [sb(f"Wd{i}", [P, 4, D], BF16) for i in range(2)]; t_Wd = [Tok() for _ in range(2)]
            rr = [sb(f"frr{i}", [P, 512], F32) for i in range(2)]; t_rr = [Tok() for _ in range(2)]
            uT = [sb(f"fuT{i}", [P, 4, 512], BF16) for i in range(2)]; t_uT = [Tok() for _ in range(2)]
            ob_i = 0
            gi = 0
            for slab in range(DFF // 512):
                ws = slab % 2
                self.load_w(Wu[ws][:], self.d_w_up.rearrange("(k p) c -> p k c", p=P)[:, :, slab * 512:(slab + 1) * 512],
                            t_Wu[ws])
                self.load_w(Wd[ws][:], self.d_w_down[slab * 512:(slab + 1) * 512, :].rearrange("(f p) c -> p f c", p=P),
                            t_Wd[ws])
                for grp in groups:
                    T = len(grp) * P
                    c_lo = grp[0] * P
                    hts = [self.t_hT[j] for j in grp]
                    us = gi % 2
                    gi += 1
                    for fc in range(4):
                        ub = fc % 2
                        for kc in range(KC):
                            self.mm(B[ub][:, 0:T], Wu[ws][:, kc, fc * P:(fc + 1) * P], self.hT_own[:, kc, c_lo:c_lo + T],
                                    kc == 0, kc == KC - 1, r=[t_Wu[ws]] + hts, w=[tb[1 + ub]])
                        self.act(rr[ub][:, 0:T], B[ub][:, 0:T], AF.Relu, r=[tb[1 + ub]], w=[t_rr[ub]])
                        self.act(uT[us][:, fc, 0:T], rr[ub][:, 0:T], AF.Square, r=[t_rr[ub]], w=[t_uT[us]])
                    for ti, j in enumerate(grp):
                        for half in range(2):
                            ob = 2 + (ob_i % 4)
                            ob_i += 1
                            for fc in range(4):
                                self.mm(B[ob][:, 0:512], uT[us][:, fc, ti * P:(ti + 1) * P],
                                        Wd[ws][:, fc, half * 512:(half + 1) * 512], fc == 0, fc == 3,
                                        r=[t_uT[us], t_Wd[ws]], w=[tb[1 + ob]])
                            xs_ = self.x_own[:, j, half * 512:(half + 1) * 512]
                            self.tt("dve", xs_, xs_, B[ob][:, 0:512], ALU.add, r=[tb[1 + ob], self.t_x[j]], w=[self.t_x[j]])


def _rope_table(pos):
    half = 32
    inv = np.float32(10000.0) ** (-np.arange(half, dtype=np.float32) * np.float32(2.0) / np.float32(64))
    ang = pos.astype(np.float32)[:, None] * inv[None, :].astype(np.float32)
    return np.concatenate([np.cos(ang), np.sin(ang)], axis=1).astype(np.float32)


def _consts(params, l, parity):
    c = np.zeros((P, NCONST), np.float32)

    def put(name, arr):
        o, w = COFF[name]
        c[:, o:o + w] = np.asarray(arr, np.float32).reshape(P, w) if np.ndim(arr) == 2 else np.asarray(arr, np.float32)

    idx = np.arange(P)
    put("ident", np.eye(P, dtype=np.float32))
    put("maskT", (idx[:, None] <= idx[None, :]).astype(np.float32))
    put("ones", np.ones((P, P), np.float32))
    put("sh1", (idx[:, None] == idx[None, :] - 1).astype(np.float32))
    put("sh2", (idx[:, None] == idx[None, :] - 2).astype(np.float32))
    ba = np.zeros((P, P), np.float32); ba[1, 0] = 1.0
    bb = np.zeros((P, P), np.float32); bb[0, 0] = 1.0; bb[1, 1] = 1.0
    put("ba", ba)
    put("bb", bb)
    put("lnmixT", params["ln_mix"][l].reshape(KC, P).T)
    put("lnmlpT", params["ln_mlp"][l].reshape(KC, P).T)
    rep = lambda v: np.broadcast_to(np.asarray(v, np.float32).reshape(1, -1), (P, np.size(v)))
    put("sgu_norm", rep(params["sgu_norm"][l]))
    put("sgu_b", params["sgu_b"][l].T)
    put("qn", rep(np.tile(params["q_norm"][l], 4)))
    put("kn", rep(np.tile(params["k_norm"][l], 4)))
    put("kidx", rep(params["kidx_norm"][l]))
    put("ib", rep(params["mlstm_i_bias"][l]))
    put("fb", rep(params["mlstm_f_bias"][l]))
    put("mn", rep(params["mlstm_norm"][l]))
    put("conv", rep(params["conv_w"][l].reshape(-1)))
    cm = np.zeros((P, 256), np.float32)
    tri = np.where(idx[None, :] <= idx[:, None], 0.0, NEG).astype(np.float32)
    if parity == 0:
        cm[:, 0:128] = tri
        cm[:, 128:256] = NEG
    else:
        cm[:, 128:256] = tri
    put("cmask", cm)
    put("pflag", np.full((P, 1), float(parity), np.float32))
    put("neghalf", np.full((P, 8), -0.5, np.float32))
    put("sgu_wT", np.transpose(params["sgu_w"][l], (2, 0, 1)).reshape(P, 4 * P))
    return c


_NC_CACHE = {}


def _get_nc(S, dbg=None):
    key = (S, tuple(dbg) if dbg else None)
    if key not in _NC_CACHE:
        _NC_CACHE[key] = Builder(S, dbg).build()
    return _NC_CACHE[key]


def run_model(x, params, dbg=None):
    Bn, S, _ = x.shape
    NT = S // P
    NO = NT // 2
    nc = Builder(S, dbg).build()
    rope_all = np.ascontiguousarray(_rope_table(np.arange(S)).reshape(NT, P, 64))
    c00, c01 = _consts(params, 0, 0), _consts(params, 0, 1)
    wts = {k: np.ascontiguousarray(params[k]) for k in ("w_in", "w_branch", "w_out", "w_up", "w_down")}
    in_maps = []
    for core in range(8):
        b, par = core // 2, core % 2
        own = [2 * j + par for j in range(NO)]
        in_maps.append(dict(
            x=np.ascontiguousarray(x[b]), consts3=np.ascontiguousarray(np.stack([c00, c01, _consts(params, 1, par)])),
            rope_all=rope_all, rope_own=np.ascontiguousarray(rope_all[own]), **wts))
    res = run_bass_kernel_spmd(nc, in_maps, core_ids=list(range(8)))
    out = np.empty_like(x)
    for core in range(8):
        b, par = core // 2, core % 2
        y = np.asarray(res.results[core]["y"]).reshape(NO, P, D)
        ov = out[b].reshape(NT, P, D)
        for j in range(NO):
            ov[2 * j + par] = y[j]
    return out, res


def kernel(**inputs):
    x = np.asarray(inputs["x"], np.float32)
    params = {k: np.asarray(v, np.float32) for k, v in inputs.items() if k != "x"}
    out, _ = run_model(x, params)
    return out
```

```python
import numpy as np
from contextlib import ExitStack
import concourse.bass as bass
import concourse.mybir as mybir
from concourse.bass_utils import run_bass_kernel_spmd

F32 = mybir.dt.float32
BF16 = mybir.dt.bfloat16
AF = mybir.ActivationFunctionType
ALU = mybir.AluOpType
AX = mybir.AxisListType

P = 128
D = 1024
KC = 8
DFF = 4096
IN_W = 7760
EPS = 1e-6
OFF = dict(a_u=0, a_v=256, b_q=512, b_k=768, b_v=1024, b_qi=1280, b_ki=1792, b_wi=1856,
           c_q=1864, c_k=2120, c_v=2376, c_o=2632, c_i=2888, c_f=2892,
           d_b=2896, d_c=3152, d_x=3408, g=3664)
NBISECT = 12
NEG = -1.0e30
MBIAS = -30000.0

CONST_SPEC = [
    ("ident", 128), ("maskT", 128), ("ones", 128), ("sh1", 128), ("sh2", 128), ("ba", 128), ("bb", 128),
    ("lnmixT", 8), ("lnmlpT", 8), ("sgu_norm", 256), ("sgu_b", 4), ("qn", 256), ("kn", 256),
    ("kidx", 64), ("ib", 4), ("fb", 4), ("mn", 256), ("cmask", 256), ("pflag", 1),
    ("neghalf", 8), ("conv", 768), ("sgu_wT", 512),
]
COFF = {}
_o = 0
for _n, _w in CONST_SPEC:
    COFF[_n] = (_o, _w)
    _o += _w
NCONST = _o
NC_TOP = COFF["conv"][0]


def _nbytes(shape, dtp):
    n = 1
    for d in shape[1:]:
        n *= d
    return n * (4 if dtp == F32 else 2)


def _alloc(nc, es, name, shape, dtp):
    _alloc.n += 1
    name = f"{name}_{_alloc.n}"
    t = es.enter_context(nc.sbuf_tensor(name, shape, dtp))
    rem = _nbytes(shape, dtp) % 32
    if rem:
        es.enter_context(nc.sbuf_tensor(name + "_pad", [P, (32 - rem) // 2], BF16))
    return t


_alloc.n = 0


def _unused():
    return None

class Dep:
    __slots__ = ("sem", "val", "eng", "key")

    def __init__(self, sem, val, eng, key):
        self.sem, self.val, self.eng, self.key = sem, val, eng, key


class Tok:
    __slots__ = ("w", "r", "name")

    def __init__(self, name=""):
        self.w = None
        self.r = {}
        self.name = name


class Eng:
    def __init__(self, name, h, sem):
        self.name, self.h, self.sem = name, h, sem
        self.count = 0
        self.waited = {}
        self.n_inst = 0


class Sched:
    NDMA = 8

    def __init__(self, nc):
        self.nc = nc
        self.E = {}
        for name, h in (("pe", nc.tensor), ("act", nc.scalar), ("dve", nc.vector),
                        ("pool", nc.gpsimd), ("sp", nc.sync)):
            self.E[name] = Eng(name, h, nc.alloc_semaphore("s_" + name))
        self.dq = {}
        for q in ("sp", "pool", "act"):
            self.dq[q] = dict(n=0, sems=[nc.alloc_semaphore(f"d_{q}{i}") for i in range(self.NDMA)])

    def _wait(self, E, d):
        if E.waited.get(d.key, 0) >= d.val:
            return
        E.h.wait_ge(d.sem, d.val)
        E.n_inst += 1
        E.waited[d.key] = d.val

    def op(self, eng, fn, r=(), w=(), dma=False, sig=True):
        E = self.E[eng]
        deps = []
        for t in r:
            if t.w is not None:
                deps.append(t.w)
        for t in w:
            if t.w is not None:
                deps.append(t.w)
            deps.extend(t.r.values())
        if dma:
            q = self.dq[eng]
            i = q["n"]
            slot = i % self.NDMA
            sem = q["sems"][slot]
            key = f"d_{eng}{slot}"
            if i >= self.NDMA:
                deps.append(Dep(sem, 16 * (i // self.NDMA), None, key))
            comp = Dep(sem, 16 * (i // self.NDMA + 1), None, key)
            q["n"] += 1
        else:
            if sig:
                E.count += 1
                comp = Dep(E.sem, E.count, eng, "e_" + eng)
            else:
                comp = Dep(E.sem, E.count + 1, eng, "e_" + eng)
        for d in deps:
            if d.eng == eng and not dma and eng == "pe":
                continue
            self._wait(E, d)
        inst = fn(E.h)
        E.n_inst += 1
        if dma:
            inst.then_inc(comp.sem, 16)
        elif sig:
            inst.then_inc(comp.sem, 1)
        for t in r:
            old = t.r.get(comp.key)
            if old is None or old.val < comp.val:
                t.r[comp.key] = comp
        for t in w:
            t.w = comp
            t.r = {}
        return comp

    def barrier(self):
        deps = [Dep(F.sem, F.count, F.name, "e_" + F.name) for F in self.E.values() if F.count > 0]
        for qn, q in self.dq.items():
            for slot in range(min(q["n"], self.NDMA)):
                last = ((q["n"] - 1 - slot) // self.NDMA) + 1
                deps.append(Dep(q["sems"][slot], 16 * last, None, f"d_{qn}{slot}"))
        for E in self.E.values():
            for d in deps:
                if d.eng == E.name:
                    continue
                self._wait(E, d)

    def final_wait(self, eng, deps):
        E = self.E[eng]
        for d in deps:
            self._wait(E, d)


class _Idx:
    def __init__(self, fn):
        self.fn = fn

    def __getitem__(self, j):
        return self.fn(j)


class Builder:
    def __init__(self, S, dbg=None):
        self.S = S
        self.NT = S // P
        self.NO = self.NT // 2
        self.dbg = dbg or ()
        nc = bass.Bass("TRN2", target_bir_lowering=False)
        self.nc = nc
        self.s = Sched(nc)
        NO, NT = self.NO, self.NT
        dt = nc.dram_tensor
        self.d_x = dt("x", [S, D], F32, kind="ExternalInput").ap()
        self.d_consts3 = dt("consts3", [3, P, NCONST], F32, kind="ExternalInput").ap()
        self.d_rope_all = dt("rope_all", [NT, P, 64], F32, kind="ExternalInput").ap()
        self.d_rope_own_in = dt("rope_own", [NO, P, 64], F32, kind="ExternalInput").ap()
        self.D_w_in = dt("w_in", [2, D, IN_W], F32, kind="ExternalInput").ap()
        self.D_w_branch = dt("w_branch", [2, 4, 256, D], F32, kind="ExternalInput").ap()
        self.D_w_out = dt("w_out", [2, D, D], F32, kind="ExternalInput").ap()
        self.D_w_up = dt("w_up", [2, D, DFF], F32, kind="ExternalInput").ap()
        self.D_w_down = dt("w_down", [2, DFF, D], F32, kind="ExternalInput").ap()
        self.d_y = dt("y", [NO * P, D], F32, kind="ExternalOutput").ap()
        self.d_x1 = dt("x1_scratch", [S, D], F32).ap()
        self.d_dbg = {}
        for name, shape in self.dbg:
            self.d_dbg[name] = dt("dbg_" + name, list(shape), F32, kind="ExternalOutput").ap()

    def mm(self, out, lhsT, rhs, start, stop, r, w, sig=None):
        sig = stop if sig is None else sig
        return self.s.op("pe", lambda e: e.matmul(out, lhsT, rhs, start=start, stop=stop,
                                                  skip_group_check=True), r=r, w=w, sig=sig)

    def tr(self, out, in_, ident, r, w, sig=True):
        return self.s.op("pe", lambda e: e.transpose(out, in_, ident), r=r, w=w, sig=sig)

    def act(self, out, in_, func, r, w, **kw):
        return self.s.op("act", lambda e: e.activation(out=out, in_=in_, func=func, **kw), r=r, w=w)

    def tt(self, eng, out, in0, in1, op, r, w):
        return self.s.op(eng, lambda e: e.tensor_tensor(out=out, in0=in0, in1=in1, op=op), r=r, w=w)

    def ts(self, eng, out, in0, s1, s2, op0, op1=None, r=(), w=(), accum_out=None):
        def f(e):
            kw = {}
            if op1 is not None:
                kw["op1"] = op1
            if accum_out is not None:
                kw["accum_out"] = accum_out
            return e.tensor_scalar(out=out, in0=in0, scalar1=s1, scalar2=s2, op0=op0, **kw)
        return self.s.op(eng, f, r=r, w=w)

    def stt(self, eng, out, in0, scalar, in1, op0, op1, r, w):
        return self.s.op(eng, lambda e: e.scalar_tensor_tensor(out=out, in0=in0, scalar=scalar, in1=in1,
                                                               op0=op0, op1=op1), r=r, w=w)

    def cp(self, eng, out, in_, r, w):
        if eng == "act":
            return self.s.op("act", lambda e: e.copy(out=out, in_=in_), r=r, w=w)
        return self.s.op(eng, lambda e: e.tensor_copy(out=out, in_=in_), r=r, w=w)

    def red(self, eng, out, in_, op, r, w, absval=False):
        return self.s.op(eng, lambda e: e.tensor_reduce(out=out, in_=in_, axis=AX.X, op=op,
                                                        apply_absolute_value=absval), r=r, w=w)

    def memset(self, eng, ap, val, w):
        return self.s.op(eng, lambda e: e.memset(ap, val), r=(), w=w)

    def dma(self, q, out, in_, r, w):
        h = {"sp": self.nc.sync, "pool": self.nc.gpsimd, "act": self.nc.scalar}[q]
        return self.s.op(q, lambda e: h.dma_start(out=out, in_=in_), r=r, w=w, dma=True)

    def C(self, name, a=0, b=None):
        o, wd = COFF[name]
        b = wd if b is None else b
        return self.c32[:, o + a:o + b]

    def rstd(self, ss, n, width, eps, out, r, w):
        tv = self.t_rs
        self.ts("pool", self.rs_tmp[:, 0:n], ss, 1.0 / width, eps, ALU.mult, ALU.add, r=r, w=[tv])
        self.tt("pool", out, self.rs_tmp[:, 0:n], self.C("neghalf", 0, n), ALU.pow, r=[tv, self.t_c32], w=w)

    def norm_transpose(self, x_ap, t_x, gainT, hT, t_hT, slot):
        xn, t_xn = self.xn[slot], self.t_xn[slot]
        ss, t_ss = self.nt_ss[slot], self.t_nt_ss[slot]
        rs, t_rsd = self.nt_rs[slot], self.t_nt_rs[slot]
        self.act(xn[:], x_ap, AF.Square, r=[t_x], w=[t_xn, t_ss], accum_out=ss[:, 0:1])
        self.rstd(ss[:, 0:1], 1, D, EPS, rs[:, 0:1], r=[t_ss], w=[t_rsd])
        self.ts("dve", xn[:], x_ap, rs[:, 0:1], None, ALU.mult, r=[t_x, t_rsd], w=[t_xn])
        bT, t_bT = self.bankT, self.t_bank[0]
        for kc in range(KC):
            self.tr(bT[:, kc * P:(kc + 1) * P], xn[:, kc * P:(kc + 1) * P], self.ident_bf[:],
                    r=[t_xn, self.t_cbf], w=[t_bT], sig=(kc == KC - 1))
        self.tt("dve", hT, bT[:].rearrange("p (k t) -> p k t", k=KC),
                gainT.unsqueeze(2).to_broadcast([P, KC, P]), ALU.mult, r=[t_bT, self.t_c32], w=[t_hT])

    def headnorm(self, src, H, gain, out, r, w, eng="dve"):
        t = self.t_hn
        sq = self.hn_sq[:, 0:H * 64]
        self.tt(eng, sq, src, src, ALU.mult, r=r, w=[t])
        self.red(eng, self.hn_ss[:, 0:H], sq.rearrange("p (h e) -> p h e", h=H), ALU.add, r=[t], w=[t])
        self.rstd(self.hn_ss[:, 0:H], H, 64, EPS, self.hn_rs[:, 0:H], r=[t], w=[t])
        self.tt(eng, out.rearrange("p (h e) -> p h e", h=H), src.rearrange("p (h e) -> p h e", h=H),
                self.hn_rs[:, 0:H].unsqueeze(2).to_broadcast([P, H, 64]), ALU.mult, r=list(r) + [t], w=w)
        if gain is not None:
            self.tt(eng, out, out, gain, ALU.mult, r=list(w) + [self.t_c32], w=w)

    def rotary(self, src, H, rope, t_rope, out, r, w, eng="pool", scratch=None):
        ro_a, ro_b, t = scratch if scratch is not None else (self.ro_a, self.ro_b, self.t_ro)
        s4 = src.rearrange("p (h two e) -> p h two e", h=H, two=2)
        a4 = ro_a[:, 0:H * 64].rearrange("p (h two e) -> p h two e", h=H, two=2)
        b4 = ro_b[:, 0:H * 64].rearrange("p (h two e) -> p h two e", h=H, two=2)
        o4 = out.rearrange("p (h two e) -> p h two e", h=H, two=2)
        cosb = rope[:, 0:32].unsqueeze(1).unsqueeze(1).to_broadcast([P, H, 2, 32])
        sinb = rope[:, 32:64].unsqueeze(1).to_broadcast([P, H, 32])
        rr = list(r) + [t_rope]
        self.tt(eng, a4, s4, cosb, ALU.mult, r=rr, w=[t])
        self.tt(eng, b4[:, :, 0, :], s4[:, :, 1, :], sinb, ALU.mult, r=rr, w=[t])
        self.tt(eng, b4[:, :, 1, :], s4[:, :, 0, :], sinb, ALU.mult, r=rr, w=[t])
        self.tt(eng, o4[:, :, 0, :], a4[:, :, 0, :], b4[:, :, 0, :], ALU.subtract, r=[t], w=w)
        self.tt(eng, o4[:, :, 1, :], a4[:, :, 1, :], b4[:, :, 1, :], ALU.add, r=[t], w=w)

    def load_w(self, dst, src, t_dst):
        return self.dma("pool", dst, src, r=(), w=[t_dst])

    def build(self):
        nc, s = self.nc, self.s
        NO, NT, S = self.NO, self.NT, self.S
        with ExitStack() as top:
            sb = lambda name, shape, dtp: _alloc(nc, top, name, shape, dtp)
            self.bank = [top.enter_context(nc.psum_tensor(f"bank{i}", [P, 512], F32)) for i in range(1, 8)]
            self.bankT = top.enter_context(nc.psum_tensor("bankT", [P, 1024], BF16))
            self.t_bank = [Tok(f"bank{i}") for i in range(8)]
            self.c32 = sb("c32", [P, NC_TOP], F32)
            self.t_c32 = Tok("c32")
            self.ident_bf = sb("ident_bf", [P, P], BF16)
            self.irep_bf = sb("irep_bf", [P, 4, P], BF16)
            self.maskT_bf = sb("maskT_bf", [P, 4, P], BF16)
            self.ones_bf = sb("ones_bf", [P, P], BF16)
            self.t_cbf = Tok("cbf")
            self.x_own = sb("x_own", [P, NO, D], F32)
            self.t_x = [Tok(f"x{j}") for j in range(NO)]
            self.yTn = [None] * 4
            self.yTn[1] = sb("yT1", [P, 2, NO * P], BF16)
            self.t_yT = [[Tok(f"yT{n}_{j}") for j in range(NO)] for n in range(4)]
            self.hl2 = sb("hl2", [P, NT, KC, 2], BF16)
            self.t_hl2 = [Tok(f"hl2_{g}") for g in range(NT)]
            xn0 = sb("xn0", [P, D], BF16)
            self.xn = [xn0, xn0]
            t_xn0 = Tok()
            self.t_xn = [t_xn0, t_xn0]
            self.nt_ss = [sb(f"ntss{i}", [P, 1], F32) for i in range(2)]
            self.t_nt_ss = [Tok() for _ in range(2)]
            self.nt_rs = [sb(f"ntrs{i}", [P, 1], F32) for i in range(2)]
            self.t_nt_rs = [Tok() for _ in range(2)]
            self.t_junk_act = Tok()
            self.t_junk_dve = Tok()
            self.rs_tmp = sb("rs_tmp", [P, 8], F32)
            self.t_rs = Tok()
            self.hn_sq = sb("hn_sq", [P, 512], F32)
            self.hn_ss = sb("hn_ss", [P, 8], F32)
            self.hn_rs = sb("hn_rs", [P, 8], F32)
            self.t_hn = Tok()
            self.t_ro = Tok()

            import os
            ph = os.environ.get("KPH", "att,mlstm,sgu,conv,merge,ffn").split(",")
            npass = int(os.environ.get("KNP", "3"))
            passes = [(0, 0), (0, 1), (1, None)][:npass]
            outs = []
            for pi, (l, q) in enumerate(passes):
                s.barrier()
                self.cur_consts = self.d_consts3[pi]
                self.dma("sp", self.c32[:], self.cur_consts[:, 0:NC_TOP], r=(), w=[self.t_c32])
                if pi == 0:
                    self.cp("dve", self.ident_bf[:], self.C("ident"), r=[self.t_c32], w=[self.t_cbf])
                    self.cp("dve", self.ones_bf[:], self.C("ones"), r=[self.t_c32], w=[self.t_cbf])
                    for h in range(4):
                        self.cp("dve", self.irep_bf[:, h, :], self.C("ident"), r=[self.t_c32], w=[self.t_cbf])
                        self.cp("dve", self.maskT_bf[:, h, :], self.C("maskT"), r=[self.t_c32], w=[self.t_cbf])
                self.d_w_in, self.d_w_branch, self.d_w_out = self.D_w_in[l], self.D_w_branch[l], self.D_w_out[l]
                self.d_w_up, self.d_w_down = self.D_w_up[l], self.D_w_down[l]
                if q is not None:
                    self.d_xf = self.d_x
                    self.d_rope_own = _Idx(lambda j, q=q: self.d_rope_all[2 * j + q])
                    for j in range(NO):
                        g = 2 * j + q
                        self.dma("sp", self.x_own[:, j, :], self.d_x[g * P:(g + 1) * P, :], r=(), w=[self.t_x[j]])
                else:
                    self.d_xf = self.d_x1
                    self.d_rope_own = self.d_rope_own_in
                    with ExitStack() as es:
                        xb = [_alloc(nc, es, f"xblend{i}", [P, D], F32) for i in range(2)]
                        t_xb = [Tok() for _ in range(2)]
                        for j in range(NO):
                            sl = j % 2
                            self.dma("sp", self.x_own[:, j, :], self.d_x1[(2 * j) * P:(2 * j + 1) * P, :], r=(), w=[self.t_x[j]])
                            self.dma("sp", xb[sl][:], self.d_x1[(2 * j + 1) * P:(2 * j + 2) * P, :], r=(), w=[t_xb[sl]])
                            self.tt("dve", xb[sl][:], xb[sl][:], self.x_own[:, j, :], ALU.subtract,
                                    r=[t_xb[sl], self.t_x[j]], w=[t_xb[sl]])
                            self.stt("dve", self.x_own[:, j, :], xb[sl][:], self.C("pflag"), self.x_own[:, j, :],
                                     ALU.mult, ALU.add, r=[t_xb[sl], self.t_x[j], self.t_c32], w=[self.t_x[j]])
                        s.barrier()
                if "att" in ph:
                    self.phase_attention()
                s.barrier()
                with ExitStack() as mid:
                    self.hT_own = _alloc(nc, mid, "hT_own", [P, KC, NO * P], BF16)
                    for n in (0, 2, 3):
                        self.yTn[n] = _alloc(nc, mid, f"yT{n}", [P, 2, NO * P], BF16)
                    self.t_hT = [Tok(f"hT{j}") for j in range(NO)]
                    self.phase_hT(self.C("lnmixT"))
                    if "mlstm" in ph:
                        self.phase_mlstm()
                    s.barrier()
                    if "sgu" in ph:
                        self.phase_sgu()
                    s.barrier()
                    if "conv" in ph:
                        self.phase_conv()
                    s.barrier()
                    if "merge" in ph:
                        self.phase_merge()
                    s.barrier()
                    if "ffn" in ph:
                        self.phase_hT(self.C("lnmlpT"))
                        self.phase_ffn()
                    s.barrier()
                last = (pi == len(passes) - 1)
                for j in range(NO):
                    if last:
                        dst = self.d_y[j * P:(j + 1) * P, :]
                    else:
                        g = 2 * j + q
                        dst = self.d_x1[g * P:(g + 1) * P, :]
                    outs.append(self.dma("sp", dst, self.x_own[:, j, :], r=[self.t_x[j]], w=()))
            s.barrier()
            s.final_wait("sp", outs + self.dbg_deps)
        return nc

    dbg_deps = []

    def dump(self, name, src_ap, toks, dst_slice=None):
        if name not in self.d_dbg:
            return
        dst = self.d_dbg[name] if dst_slice is None else dst_slice(self.d_dbg[name])
        self.dbg_deps = self.dbg_deps + [self.dma("sp", dst, src_ap, r=toks, w=())]

    def phase_hT(self, gainT):
        for j in range(self.NO):
            self.norm_transpose(self.x_own[:, j, :], self.t_x[j], gainT,
                                self.hT_own[:, :, j * P:(j + 1) * P], self.t_hT[j], j % 2)

    def phase_attention(self):
        nc, s = self.nc, self.s
        NO, NT, S = self.NO, self.NT, self.S
        B = self.bank
        tb = self.t_bank
        with ExitStack() as es:
            sb = lambda name, shape, dtp: _alloc(nc, es, name, shape, dtp)
            WA = sb("WA", [P, KC, 1352], BF16)
            t_WA = Tok("WA")
            wv = self.d_w_in.rearrange("(k p) c -> p k c", p=P)
            for (dst0, src0, n) in ((0, OFF["b_q"], 256), (256, OFF["b_wi"], 8), (264, OFF["b_qi"], 512),
                                    (776, OFF["b_k"], 512), (1288, OFF["b_ki"], 64)):
                self.load_w(WA[:, :, dst0:dst0 + n], wv[:, :, src0:src0 + n], t_WA)
            KT = sb("KT", [P, 2, S], BF16)
            t_KT = [Tok(f"KT{g}") for g in range(NT)]
            VA = sb("VA", [P, NT, 4, 65], BF16)
            t_VA = [Tok(f"VA{g}") for g in range(NT)]
            KI2 = sb("KI2", [P, S], BF16)
            t_KI = [Tok(f"KI{g}") for g in range(NT)]
            xt0 = sb("xt0", [P, D], F32)
            xt = [xt0, xt0]
            t_xt0 = Tok()
            t_xt = [t_xt0, t_xt0]
            hTg0 = sb("hTg0", [P, KC, P], BF16)
            hT = [hTg0, hTg0]
            t_hTg0 = Tok()
            t_hT = [t_hTg0, t_hTg0]
            rope_g = [sb(f"ropeg{i}", [P, 64], F32) for i in range(2)]
            t_rope_g = [Tok() for _ in range(2)]
            rope_o = sb("ropeo", [P, 64], F32)
            t_rope_o = Tok()
            ksb = sb("ksb", [P, 256], F32); t_ksb = Tok()
            kisb = sb("kisb", [P, 64], F32); t_kisb = Tok()
            kr = sb("kr", [P, 256], BF16); t_kr = Tok()
            ki2 = sb("ki2", [P, 128], BF16); t_ki2 = Tok()
            bn6 = sb("bn6", [P, 8], F32); t_bn = Tok()
            SC = sb("SC", [P, S], F32); t_SC = Tok("SC")
            self.ro_a = SC[:, 0:512]
            self.ro_b = SC[:, 512:1024]
            self.t_ro = t_SC
            rog = sb("rog", [P, 1024], F32)
            sc_g = (rog[:, 0:512], rog[:, 512:1024], Tok("rog"))
            sc_o = sc_g
            MB = sb("MB", [P, S], BF16); t_MB = Tok("MB")
            qsb = sb("qsb", [P, 264], F32); t_qsb = Tok()
            qisb = sb("qisb", [P, 512], F32); t_qisb = Tok()
            qr = sb("qr", [P, 256], BF16); t_qr = Tok()
            qir = sb("qir", [P, 512], BF16); t_qir = Tok()
            QTp2 = [sb(f"QTp{i}", [P, 4, P], BF16) for i in range(2)]; t_QTp2 = [Tok() for _ in range(2)]

            QiTp = sb("QiTp", [P, 8, P], BF16); t_QiTp = Tok()
            Dg = sb("Dg", [P, 8, P], BF16); t_Dg = Tok()
            wab = sb("wab", [P, 8], F32); wsg = sb("wsg", [P, 8], F32); wtmp = sb("wtmp", [P, 8], F32); t_w8 = Tok()
            Rr = [sb(f"Rr{i}", [P, 512], BF16) for i in range(2)]
            t_Rr = [Tok() for _ in range(2)]
            PT = [sb(f"PT{i}", [P, 4, P], BF16) for i in range(2)]
            t_PT = [Tok() for _ in range(2)]
            bs = sb("bs", [P, 8], F32); t_bs = Tok()
            hwt = sb("hwt", [P, 64], F32); t_hw = Tok()
            t_cnt = Tok()
            hk = sb("hk", [P, NBISECT + 2], F32)
            cnt = sb("cnt", [P, 1], F32)
            osb = sb("osb", [P, 4, 65], F32); t_osb = Tok()
            rden = sb("rden", [P, 4], F32)
            yb = sb("yb", [P, 256], BF16); t_yb = Tok()

            self.memset("pool", VA[:], 1.0, w=t_VA)
            for i in range(2):
                self.memset("pool", QTp2[i][:], 0.0, w=[t_QTp2[i]])
            self.memset("pool", QiTp[:], 0.0, w=[t_QiTp])
            for k in range(NBISECT + 2):
                self.memset("pool", hk[:, k:k + 1], 2.0 ** (-(k + 1)), w=[t_bs])

            def proc_global(g):
                sl = g % 2
                self.dma("sp", xt[sl][:], self.d_xf[g * P:(g + 1) * P, :], r=(), w=[t_xt[sl]])
                self.dma("sp", rope_g[sl][:], self.d_rope_all[g], r=(), w=[t_rope_g[sl]])
                self.norm_transpose(xt[sl][:], t_xt[sl], self.C("lnmixT"), hT[sl][:], t_hT[sl], sl)
                self.cp("pool", self.hl2[:, g, :, :], hT[sl][:, :, P - 2:P], r=[t_hT[sl]], w=[self.t_hl2[g]])
                for kc in range(KC):
                    self.mm(B[0][:, 0:512], hT[sl][:, kc, :], WA[:, kc, 776:1288], kc == 0, kc == KC - 1,
                            r=[t_hT[sl], t_WA], w=[tb[1]])
                for kc in range(KC):
                    self.mm(B[1][:, 0:64], hT[sl][:, kc, :], WA[:, kc, 1288:1352], kc == 0, kc == KC - 1,
                            r=[t_hT[sl], t_WA], w=[tb[2]])
                self.cp("dve", ksb[:], B[0][:, 0:256], r=[tb[1]], w=[t_ksb])
                self.cp("dve", VA[:, g, :, 0:64], B[0][:, 256:512].rearrange("p (h e) -> p h e", h=4),
                        r=[tb[1]], w=[t_VA[g]])
                self.cp("dve", kisb[:], B[1][:, 0:64], r=[tb[2]], w=[t_kisb])
                self.headnorm(ksb[:], 4, self.C("kn"), ksb[:], r=[t_ksb], w=[t_ksb])
                self.rotary(ksb[:], 4, rope_g[sl], t_rope_g[sl], kr[:], r=[t_ksb], w=[t_kr], eng="dve", scratch=sc_g)
                bT = self.bankT
                for pr in range(2):
                    self.tr(bT[:, pr * P:(pr + 1) * P], kr[:, pr * P:(pr + 1) * P], self.ident_bf[:],
                            r=[t_kr, self.t_cbf], w=[tb[0]], sig=False)
                self.red("dve", bn6[:, 0:1], kisb[:], ALU.add, r=[t_kisb], w=[t_bn])
                self.ts("dve", bn6[:, 1:2], bn6[:, 0:1], 1.0 / 64, None, ALU.mult, r=[t_bn], w=[t_bn])
                self.ts("dve", kisb[:], kisb[:], bn6[:, 1:2], None, ALU.subtract, r=[t_bn, t_kisb], w=[t_kisb])
                self.tt("dve", self.hn_sq[:, 0:64], kisb[:], kisb[:], ALU.mult, r=[t_kisb], w=[self.t_hn])
                self.red("dve", bn6[:, 2:3], self.hn_sq[:, 0:64], ALU.add, r=[self.t_hn], w=[t_bn])
                self.rstd(bn6[:, 2:3], 1, 64, EPS, self.hn_rs[:, 0:1], r=[t_bn], w=[self.t_hn])
                self.ts("dve", kisb[:], kisb[:], self.hn_rs[:, 0:1], None, ALU.mult, r=[self.t_hn, t_kisb], w=[t_kisb])
                self.tt("dve", kisb[:], kisb[:], self.C("kidx"), ALU.mult, r=[t_kisb, self.t_c32], w=[t_kisb])
                self.rotary(kisb[:], 1, rope_g[sl], t_rope_g[sl], ki2[:, 0:64], r=[t_kisb], w=[t_ki2], eng="dve", scratch=sc_g)
                self.cp("dve", ki2[:, 64:128], ki2[:, 0:64], r=[t_ki2], w=[t_ki2])
                self.tr(bT[:, 2 * P:3 * P], ki2[:], self.ident_bf[:], r=[t_ki2, self.t_cbf], w=[tb[0]], sig=True)
                self.cp("dve", KT[:, :, g * P:(g + 1) * P], bT[:, 0:2 * P].rearrange("p (a t) -> p a t", a=2),
                        r=[tb[0]], w=[t_KT[g]])
                self.cp("dve", KI2[:, g * P:(g + 1) * P], bT[:, 2 * P:3 * P], r=[tb[0]], w=[t_KI[g]])

            def proc_own(j):
                QTp, t_QTp = QTp2[j % 2], t_QTp2[j % 2]
                sl = j % 2
                xj = self.x_own[:, j, :]
                self.dma("sp", rope_o[:], self.d_rope_own[j], r=(), w=[t_rope_o])
                self.norm_transpose(xj, self.t_x[j], self.C("lnmixT"), hT[sl][:], t_hT[sl], sl)
                for kc in range(KC):
                    self.mm(B[0][:, 0:264], hT[sl][:, kc, :], WA[:, kc, 0:264], kc == 0, kc == KC - 1,
                            r=[t_hT[sl], t_WA], w=[tb[1]])
                for kc in range(KC):
                    self.mm(B[1][:, 0:512], hT[sl][:, kc, :], WA[:, kc, 264:776], kc == 0, kc == KC - 1,
                            r=[t_hT[sl], t_WA], w=[tb[2]])
                self.cp("dve", qsb[:], B[0][:, 0:264], r=[tb[1]], w=[t_qsb])
                self.cp("dve", qisb[:], B[1][:, 0:512], r=[tb[2]], w=[t_qisb])
                self.headnorm(qsb[:, 0:256], 4, self.C("qn"), qsb[:, 0:256], r=[t_qsb], w=[t_qsb])
                self.rotary(qsb[:, 0:256], 4, rope_o, t_rope_o, qr[:], r=[t_qsb], w=[t_qr], eng="dve", scratch=sc_g)
                self.rotary(qisb[:], 8, rope_o, t_rope_o, qir[:], r=[t_qisb], w=[t_qir], eng="dve", scratch=sc_o)
                bT = self.bankT
                for pr in range(2):
                    self.tr(bT[:, pr * P:(pr + 1) * P], qr[:, pr * P:(pr + 1) * P], self.ident_bf[:],
                            r=[t_qr, self.t_cbf], w=[tb[0]], sig=False)
                for pr in range(4):
                    self.tr(bT[:, (2 + pr) * P:(3 + pr) * P], qir[:, pr * P:(pr + 1) * P], self.ident_bf[:],
                            r=[t_qir, self.t_cbf], w=[tb[0]], sig=(pr == 3))
                for h in range(4):
                    lo = (h % 2) * 64
                    self.cp("dve", QTp[lo:lo + 64, h, :], bT[lo:lo + 64, (h // 2) * P:(h // 2 + 1) * P],
                            r=[tb[0]], w=[t_QTp])
                for h in range(8):
                    lo = (h % 2) * 64
                    self.cp("dve", QiTp[lo:lo + 64, h, :],
                            bT[lo:lo + 64, (2 + h // 2) * P:(3 + h // 2) * P], r=[tb[0]], w=[t_QiTp])
                wsrc = qsb[:, 256:264]
                self.ts("dve", wsg[:], wsrc, -1.0, None, ALU.mult, r=[t_qsb], w=[t_w8])
                self.tt("dve", wab[:], wsrc, wsg[:], ALU.max, r=[t_qsb, t_w8], w=[t_w8])
                self.ts("dve", wab[:], wab[:], float(8 ** -0.5 * 64 ** -0.5), None, ALU.mult, r=[t_w8], w=[t_w8])
                self.ts("dve", wsg[:], wsrc, 0.0, None, ALU.is_gt, r=[t_qsb, t_w8], w=[t_w8])
                self.ts("dve", wtmp[:], wsrc, 0.0, None, ALU.is_lt, r=[t_qsb], w=[t_w8])
                self.tt("dve", wsg[:], wsg[:], wtmp[:], ALU.subtract, r=[t_w8], w=[t_w8])
                for h in range(8):
                    self.ts("dve", Dg[:, h, :], self.C("ident"), wsg[:, h:h + 1], None, ALU.mult,
                            r=[t_w8, self.t_c32], w=[t_Dg])

            def proc_own_idx(j):
                nk = (j + 1) * 256
                nblk = (nk + 511) // 512
                for b in range(nblk):
                    k0 = b * 512
                    kw = min(512, nk - k0)
                    kg = list(range(k0 // P, (k0 + kw) // P))
                    pend = None
                    for h in range(8):
                        rb = 3 + (h % 2)
                        self.mm(B[rb - 1][:, 0:kw], QiTp[:, h, :], KI2[:, k0:k0 + kw], True, True,
                                r=[t_QiTp] + [t_KI[g] for g in kg], w=[tb[rb]])
                        ri = h % 2
                        self.ts("dve", Rr[ri][:, 0:kw], B[rb - 1][:, 0:kw], 0.0, wab[:, h:h + 1], ALU.max, ALU.mult,
                                r=[tb[rb], t_w8], w=[t_Rr[ri]])
                        if pend is not None:
                            ph_, pri = pend
                            self.mm(B[4][:, 0:kw], Dg[:, ph_, :], Rr[pri][:, 0:kw], ph_ == 0, False,
                                    r=[t_Dg, t_Rr[pri]], w=[tb[5]], sig=False)
                        pend = (h, ri)
                    ph_, pri = pend
                    self.mm(B[4][:, 0:kw], Dg[:, ph_, :], Rr[pri][:, 0:kw], False, True,
                            r=[t_Dg, t_Rr[pri]], w=[tb[5]])
                    self.cp("dve", SC[:, k0:k0 + kw], B[4][:, 0:kw], r=[tb[5]], w=[t_SC])
                self.red("dve", bs[:, 0:1], SC[:, 0:nk], ALU.max, r=[t_SC], w=[t_bs], absval=True)
                self.tt("dve", SC[:, nk - 256:nk], SC[:, nk - 256:nk], self.C("cmask"), ALU.add,
                        r=[t_SC, self.t_c32], w=[t_SC])
                self.ts("dve", bs[:, 1:2], bs[:, 0:1], 2.002, 2e-20, ALU.mult, ALU.add, r=[t_bs], w=[t_bs])
                hw = hwt
                self.ts("dve", hw[:, 0:NBISECT + 2], hk[:, 0:NBISECT + 2], bs[:, 1:2], None, ALU.mult,
                        r=[t_bs], w=[t_hw])
                self.ts("dve", hw[:, 32:32 + NBISECT + 2], hw[:, 0:NBISECT + 2], -0.5, None, ALU.mult,
                        r=[t_hw], w=[t_hw])
                self.memset("dve", bs[:, 3:4], 0.0, w=[t_bs])

            def proc_own_b(j):
                QTp, t_QTp = QTp2[j % 2], t_QTp2[j % 2]
                nk = (j + 1) * 256
                nch = nk // P
                hw = hwt
                bT = self.bankT
                for k in range(NBISECT):
                    self.act(MB[:, 0:nk], SC[:, 0:nk], AF.Sign, r=[t_SC, t_bs], w=[t_MB, t_cnt],
                             bias=bs[:, 3:4], accum_out=cnt[:, 0:1])
                    self.act(bs[:, 4:5], cnt[:, 0:1], AF.Sign, r=[t_cnt], w=[t_bs], bias=float(nk - 512 + 0.5))
                    self.act(bs[:, 3:4], bs[:, 4:5], AF.Identity, r=[t_bs, t_hw], w=[t_bs],
                             scale=hw[:, 32 + k:33 + k], bias=bs[:, 3:4])
                self.stt("dve", bs[:, 5:6], bs[:, 3:4], hw[:, NBISECT:NBISECT + 1], self.C("neghalf", 0, 1),
                         ALU.add, ALU.mult, r=[t_bs, t_hw, self.t_c32], w=[t_bs])
                self.ts("dve", bs[:, 5:6], bs[:, 5:6], 2.0, None, ALU.mult, r=[t_bs], w=[t_bs])
                self.ts("dve", MB[:, 0:nk], SC[:, 0:nk], bs[:, 5:6], MBIAS, ALU.is_lt, ALU.mult,
                        r=[t_SC, t_bs], w=[t_MB])
                def emit_pv(c):
                    pt = c % 2
                    for h in range(4):
                        self.mm(B[4][:, h * 65:(h + 1) * 65], PT[pt][:, h, :], VA[:, c, h, :],
                                (c == 0 and h == 0), (c == nch - 1 and h == 3),
                                r=[t_PT[pt], t_VA[c]], w=[tb[5]], sig=(h == 3))

                for c in range(nch):
                    stb = 6 + (c % 2)
                    pt = c % 2
                    for h in range(4):
                        self.mm(B[stb - 1][:, h * P:(h + 1) * P], KT[:, h // 2, c * P:(c + 1) * P], QTp[:, h, :],
                                h == 0, False, r=[t_KT[c], t_QTp], w=[tb[stb]], sig=False)
                    self.mm(B[stb - 1][:, 0:512], MB[:, c * P:(c + 1) * P], self.irep_bf[:].rearrange("p a t -> p (a t)"),
                            False, True, r=[t_MB, self.t_cbf], w=[tb[stb]])
                    self.act(PT[pt][:].rearrange("p a t -> p (a t)"), B[stb - 1][:, 0:512], AF.Exp,
                             r=[tb[stb]], w=[t_PT[pt]], scale=0.125)
                    if c >= 1:
                        emit_pv(c - 1)
                emit_pv(nch - 1)
                self.cp("act", osb[:].rearrange("p h e -> p (h e)"), B[4][:, 0:260], r=[tb[5]], w=[t_osb])
                self.s.op("dve", lambda e: e.reciprocal(out=rden[:], in_=osb[:, :, 64]), r=[t_osb], w=[t_osb])
                self.tt("dve", yb[:].rearrange("p (h e) -> p h e", h=4), osb[:, :, 0:64],
                        rden[:].unsqueeze(2).to_broadcast([P, 4, 64]), ALU.mult, r=[t_osb], w=[t_yb])
                for pr in range(2):
                    self.tr(bT[:, pr * P:(pr + 1) * P], yb[:, pr * P:(pr + 1) * P], self.ident_bf[:],
                            r=[t_yb, self.t_cbf], w=[tb[0]], sig=(pr == 1))
                self.cp("act", self.yTn[1][:, :, j * P:(j + 1) * P], bT[:, 0:2 * P].rearrange("p (a t) -> p a t", a=2),
                        r=[tb[0]], w=[self.t_yT[1][j]])

            proc_global(0)
            proc_global(1)
            proc_own(0)
            for j in range(NO):
                proc_own_idx(j)
                if j + 1 < NO:
                    proc_global(2 * j + 2)
                    proc_global(2 * j + 3)
                    proc_own(j + 1)
                proc_own_b(j)

    def phase_mlstm(self):
        nc, s = self.nc, self.s
        NO, NT, S = self.NO, self.NT, self.S
        B, tb = self.bank, self.t_bank
        with ExitStack() as es:
            sb = lambda name, shape, dtp: _alloc(nc, es, name, shape, dtp)
            WC = sb("WC", [P, KC, 1032], BF16); t_WC = Tok("WC")
            wv = self.d_w_in.rearrange("(k p) c -> p k c", p=P)
            for (dst0, src0, n) in ((0, OFF["c_k"], 512), (512, OFF["c_i"], 8), (520, OFF["c_q"], 256),
                                    (776, OFF["c_o"], 256)):
                self.load_w(WC[:, :, dst0:dst0 + n], wv[:, :, src0:src0 + n], t_WC)
            xt = [sb(f"mxt{i}", [P, D], F32) for i in range(2)]; t_xt = [Tok() for _ in range(2)]
            hT = [sb(f"mhT{i}", [P, KC, P], BF16) for i in range(2)]; t_hT = [Tok() for _ in range(2)]
            Cst = sb("Cst", [P, 2, 65], F32); t_C = Tok("Cst")
            Ca = sb("Ca", [P, 2, 65], F32); t_Ca = Tok("Ca")
            Csel = sb("Csel", [P, 2, 65], BF16); t_Csel = Tok("Csel")
            kp = [sb(f"kp{i}", [P, 256], BF16) for i in range(3)]
            va = [sb(f"va{i}", [P, 4, 65], BF16) for i in range(3)]
            gsc = [sb(f"gsc{i}", [P, 24], F32) for i in range(3)]
            gsb = [sb(f"gsb{i}", [P, 16], BF16) for i in range(3)]
            t_g = [Tok() for _ in range(3)]
            qb = sb("qb", [P, 256], BF16); t_qb = Tok()
            qTp = sb("mqTp", [P, 4, P], BF16); t_qTp = Tok()
            kT = sb("mkT", [P, 2, P], BF16); t_kT = Tok()
            Sm = sb("Sm", [P, 4, P], BF16); t_Sm = Tok()
            ep = sb("ep", [P, 16], F32); t_ep = Tok()
            hc = sb("hc", [P, 256], F32); t_hc = Tok()
            so = sb("so", [P, 256], F32); t_so = Tok()
            yc = sb("yc", [P, 256], BF16); t_yc = Tok()
            self.memset("pool", Cst[:], 0.0, w=[t_C])
            self.memset("pool", qTp[:], 0.0, w=[t_qTp])

            def kv_gates(hTap, t_h, sl):
                for kc in range(KC):
                    self.mm(B[0][:, 0:512], hTap[:, kc, :], WC[:, kc, 0:512], kc == 0, kc == KC - 1,
                            r=[t_h, t_WC], w=[tb[1]])
                for kc in range(KC):
                    self.mm(B[1][:, 0:8], hTap[:, kc, :], WC[:, kc, 512:520], kc == 0, kc == KC - 1,
                            r=[t_h, t_WC], w=[tb[2]])
                G = gsc[sl]; tg = t_g[sl]
                self.tt("dve", G[:, 0:4], B[1][:, 4:8], self.C("fb"), ALU.add, r=[tb[2], self.t_c32], w=[tg])
                self.act(G[:, 0:4], G[:, 0:4], AF.Exp, r=[tg], w=[tg], scale=-1.0)
                self.act(G[:, 0:4], G[:, 0:4], AF.Ln, r=[tg], w=[tg], bias=1.0)
                self.mm(B[2][:, 0:4], self.C("maskT"), G[:, 0:4], True, False, r=[tg, self.t_c32], w=[tb[3]], sig=False)
                self.mm(B[2][:, 4:8], self.C("ones"), G[:, 0:4], False, True, r=[tg, self.t_c32], w=[tb[3]])
                self.tt("dve", G[:, 4:8], B[1][:, 0:4], self.C("ib"), ALU.add, r=[tb[2], self.t_c32], w=[tg])
                self.tt("dve", G[:, 4:8], G[:, 4:8], B[2][:, 0:4], ALU.add, r=[tg, tb[3]], w=[tg])
                self.act(G[:, 8:12], G[:, 4:8], AF.Exp, r=[tg], w=[tg])
                self.act(G[:, 12:20], B[2][:, 0:8], AF.Exp, r=[tb[3]], w=[tg], scale=-1.0)
                self.act(kp[sl][:], B[0][:, 0:256], AF.Copy, r=[tb[1]], w=[tg], scale=0.125)
                self.tt("dve", va[sl][:, :, 0:64], B[0][:, 256:512].rearrange("p (h e) -> p h e", h=4),
                        G[:, 8:12].unsqueeze(2).to_broadcast([P, 4, 64]), ALU.mult, r=[tb[1], tg], w=[tg])
                self.cp("dve", va[sl][:, :, 64], G[:, 8:12], r=[tg], w=[tg])

            def state_update(sl):
                G = gsc[sl]; tg = t_g[sl]
                for h in range(4):
                    pr = h // 2
                    self.mm(B[3][:, h * 65:(h + 1) * 65], kp[sl][:, pr * P:(pr + 1) * P], va[sl][:, h, :],
                            True, True, r=[tg], w=[tb[4]], sig=(h == 3))
                for h in range(4):
                    lo = (h % 2) * 64
                    self.tt("dve", Cst[lo:lo + 64, h // 2, :], Cst[lo:lo + 64, h // 2, :],
                            B[3][lo:lo + 64, h * 65:(h + 1) * 65], ALU.add, r=[tb[4], t_C], w=[t_C])
                    self.ts("dve", Cst[lo:lo + 64, h // 2, :], Cst[lo:lo + 64, h // 2, :],
                            G[lo:lo + 64, 16 + h:17 + h], None, ALU.mult, r=[tg, t_C], w=[t_C])

            def proc_global(g, sl):
                self.dma("sp", xt[sl][:], self.d_xf[g * P:(g + 1) * P, :], r=(), w=[t_xt[sl]])
                self.norm_transpose(xt[sl][:], t_xt[sl], self.C("lnmixT"), hT[sl][:], t_hT[sl], sl)
                kv_gates(hT[sl], t_hT[sl], sl)
                state_update(sl)

            import os
            KO = int(os.environ.get('KO', '99'))
            def proc_own(j):
                hTo = self.hT_own[:, :, j * P:(j + 1) * P]
                for kc in range(KC):
                    self.mm(B[4][:, 0:512], hTo[:, kc, :], WC[:, kc, 520:1032], kc == 0, kc == KC - 1,
                            r=[self.t_hT[j], t_WC], w=[tb[5]])
                kv_gates(hTo, self.t_hT[j], 2)
                if KO < 1: return
                G = gsc[2]; tg = t_g[2]
                if KO < 2: return
                self.cp("act", qb[:], B[4][:, 0:256], r=[tb[5]], w=[t_qb])
                bT = self.bankT
                for pr in range(2):
                    for kc in range(KC):
                        self.mm(B[3][:, pr * P:(pr + 1) * P], WC[:, kc, 520 + pr * P:520 + (pr + 1) * P], hTo[:, kc, :],
                                kc == 0, kc == KC - 1, r=[self.t_hT[j], t_WC], w=[tb[4]], sig=False)
                for pr in range(2):
                    for kc in range(KC):
                        self.mm(B[3][:, (2 + pr) * P:(3 + pr) * P], WC[:, kc, pr * P:(pr + 1) * P], hTo[:, kc, :],
                                kc == 0, kc == KC - 1, r=[self.t_hT[j], t_WC], w=[tb[4]],
                                sig=(pr == 1 and kc == KC - 1))
                if KO < 3: return
                for h in range(4):
                    lo = (h % 2) * 64
                    self.cp("act", qTp[lo:lo + 64, h, :], B[3][lo:lo + 64, (h // 2) * P:(h // 2 + 1) * P],
                            r=[tb[4]], w=[t_qTp])
                self.act(kT[:].rearrange("p a t -> p (a t)"), B[3][:, 2 * P:4 * P], AF.Copy, r=[tb[4]], w=[t_kT], scale=0.125)
                if KO < 4: return
                for h in range(4):
                    self.mm(B[5][:, h * P:(h + 1) * P], kT[:, h // 2, :], qTp[:, h, :], True, True,
                            r=[t_kT, t_qTp], w=[tb[6]], sig=(h == 3))
                if KO < 5: return
                self.tt("dve", Sm[:], B[5][:, 0:512].rearrange("p (h t) -> p h t", h=4), self.maskT_bf[:], ALU.mult,
                        r=[tb[6], self.t_cbf], w=[t_Sm])
                if KO < 6: return
                for h in range(4):
                    self.mm(B[6][:, h * 65:(h + 1) * 65], qTp[:, h, :], Csel[:, h // 2, :], h == 0, False,
                            r=[t_qTp, t_Csel], w=[tb[7]], sig=False)
                    self.mm(B[6][:, h * 65:(h + 1) * 65], Sm[:, h, :], va[2][:, h, :], False, True,
                            r=[t_Sm, tg], w=[tb[7]], sig=(h == 3))
                if KO < 7: return
                acc = B[6][:, 0:260].rearrange("p (h e) -> p h e", h=4)
                self.tt("dve", ep[:, 0:4], acc[:, :, 64], G[:, 12:16], ALU.mult, r=[tb[7], tg], w=[t_ep])
                self.act(ep[:, 0:4], ep[:, 0:4], AF.Abs, r=[t_ep], w=[t_ep])
                self.ts("dve", ep[:, 0:4], ep[:, 0:4], 1.0, None, ALU.max, r=[t_ep], w=[t_ep])
                self.s.op("dve", lambda e: e.reciprocal(out=ep[:, 4:8], in_=ep[:, 0:4]), r=[t_ep], w=[t_ep])
                self.tt("dve", ep[:, 4:8], ep[:, 4:8], G[:, 12:16], ALU.mult, r=[t_ep, tg], w=[t_ep])
                self.tt("dve", hc[:].rearrange("p (h e) -> p h e", h=4), acc[:, :, 0:64],
                        ep[:, 4:8].unsqueeze(2).to_broadcast([P, 4, 64]), ALU.mult, r=[tb[7], t_ep], w=[t_hc])
                if KO < 8: return
                self.headnorm(hc[:], 4, self.C("mn"), hc[:], r=[t_hc], w=[t_hc])
                if KO < 9: return
                self.act(so[:], B[4][:, 256:512], AF.Exp, r=[tb[5]], w=[t_so], scale=-1.0)
                self.ts("pool", so[:], so[:], 1.0, None, ALU.add, r=[t_so], w=[t_so])
                self.s.op("dve", lambda e: e.reciprocal(out=so[:], in_=so[:]), r=[t_so], w=[t_so])
                self.tt("dve", yc[:], hc[:], so[:], ALU.mult, r=[t_hc, t_so], w=[t_yc])
                if KO < 10: return
                for pr in range(2):
                    self.tr(bT[:, pr * P:(pr + 1) * P], yc[:, pr * P:(pr + 1) * P], self.ident_bf[:],
                            r=[t_yc, self.t_cbf], w=[tb[0]], sig=(pr == 1))
                self.cp("act", self.yTn[2][:, :, j * P:(j + 1) * P], bT[:, 0:2 * P].rearrange("p (a t) -> p a t", a=2),
                        r=[tb[0]], w=[self.t_yT[2][j]])

            pf = self.C("pflag")
            import os
            km = os.environ.get("KM", "gso")
            for j in range(NO):
                self.cp("pool", Ca[:], Cst[:], r=[t_C], w=[t_Ca])
                if "g" in km:
                    proc_global(2 * j, 0)
                if "s" in km:
                    self.tt("dve", hc[:, 0:130].rearrange("p (a e) -> p a e", a=2), Cst[:], Ca[:], ALU.subtract,
                            r=[t_C, t_Ca], w=[t_hc])
                    self.stt("dve", Csel[:], hc[:, 0:130].rearrange("p (a e) -> p a e", a=2), pf, Ca[:], ALU.mult, ALU.add,
                             r=[t_hc, t_Ca, self.t_c32], w=[t_Csel])
                if "g" in km:
                    proc_global(2 * j + 1, 1)
                if "o" in km:
                    proc_own(j)

    def phase_sgu(self):
        nc = self.nc
        NO = self.NO
        B, tb = self.bank, self.t_bank
        with ExitStack() as es:
            sb = lambda name, shape, dtp: _alloc(nc, es, name, shape, dtp)
            WS = sb("WS", [P, KC, 512], BF16); t_WS = Tok()
            wv = self.d_w_in.rearrange("(k p) c -> p k c", p=P)
            self.load_w(WS[:], wv[:, :, 0:512], t_WS)
            WmT = sb("WmT", [P, 4, P], BF16); t_Wm = Tok()
            wT32 = sb("wT32", [P, 512], F32); t_wT32 = Tok()
            o_, w_ = COFF["sgu_wT"]
            self.dma("sp", wT32[:], self.cur_consts[:, o_:o_ + w_], r=(), w=[t_wT32])
            self.tt("dve", WmT[:], wT32[:].rearrange("p (g t) -> p g t", g=4),
                    self.C("maskT").unsqueeze(1).to_broadcast([P, 4, P]), ALU.mult, r=[self.t_c32, t_wT32], w=[t_Wm])
            x2 = sb("sx2", [P, 512], F32); xh = sb("sxh", [P, 512], F32); zz = sb("szz", [P, 512], F32)
            ge = sb("sge", [P, 512], F32); t_s = Tok()
            vn = sb("svn", [P, 256], BF16); t_vn = Tok()
            ssq = sb("sssq", [P, 2], F32)
            ya = sb("sya", [P, 256], BF16); t_ya = Tok()
            for j in range(NO):
                hTo = self.hT_own[:, :, j * P:(j + 1) * P]
                for kc in range(KC):
                    self.mm(B[0][:, 0:512], hTo[:, kc, :], WS[:, kc, :], kc == 0, kc == KC - 1,
                            r=[self.t_hT[j], t_WS], w=[tb[1]])
                xp = B[0][:, 0:512]
                self.act(x2[:], xp, AF.Square, r=[tb[1]], w=[t_s])
                self.act(xh[:], xp, AF.Copy, r=[tb[1]], w=[t_s], scale=0.5)
                self.ts("pool", x2[:], x2[:], 0.044715, 1.0, ALU.mult, ALU.add, r=[t_s], w=[t_s])
                self.tt("dve", zz[:], x2[:], xp, ALU.mult, r=[t_s, tb[1]], w=[t_s])
                self.act(zz[:], zz[:], AF.Tanh, r=[t_s], w=[t_s], scale=0.7978845608028654)
                self.stt("dve", ge[:], zz[:], 1.0, xh[:], ALU.add, ALU.mult, r=[t_s], w=[t_s])
                self.act(x2[:, 0:256], ge[:, 256:512], AF.Square, r=[t_s], w=[t_s], accum_out=ssq[:, 0:1])
                self.rstd(ssq[:, 0:1], 1, 256, EPS, ssq[:, 1:2], r=[t_s], w=[t_s])
                self.stt("dve", vn[:], ge[:, 256:512], ssq[:, 1:2], self.C("sgu_norm"), ALU.mult, ALU.mult,
                         r=[t_s, self.t_c32], w=[t_vn])
                for g in range(4):
                    self.mm(B[1][:, g * 64:(g + 1) * 64], WmT[:, g, :], vn[:, g * 64:(g + 1) * 64], True, True,
                            r=[t_Wm, t_vn], w=[tb[2]], sig=(g == 3))
                for g in range(4):
                    self.stt("dve", ya[:, g * 64:(g + 1) * 64], B[1][:, g * 64:(g + 1) * 64],
                             self.C("sgu_b", g, g + 1), ge[:, g * 64:(g + 1) * 64], ALU.add, ALU.mult,
                             r=[tb[2], t_s, self.t_c32], w=[t_ya])
                bT = self.bankT
                for pr in range(2):
                    self.tr(bT[:, pr * P:(pr + 1) * P], ya[:, pr * P:(pr + 1) * P], self.ident_bf[:],
                            r=[t_ya, self.t_cbf], w=[tb[0]], sig=(pr == 1))
                self.cp("act", self.yTn[0][:, :, j * P:(j + 1) * P], bT[:, 0:2 * P].rearrange("p (a t) -> p a t", a=2),
                        r=[tb[0]], w=[self.t_yT[0][j]])

    def phase_conv(self):
        nc = self.nc
        NO = self.NO
        B, tb = self.bank, self.t_bank
        with ExitStack() as es:
            sb = lambda name, shape, dtp: _alloc(nc, es, name, shape, dtp)
            WD = sb("WD", [P, KC, 768], BF16); t_WD = Tok()
            wv = self.d_w_in.rearrange("(k p) c -> p k c", p=P)
            self.load_w(WD[:], wv[:, :, OFF["d_b"]:OFF["d_b"] + 768], t_WD)
            hz = sb("hz", [P, KC, 2], BF16); t_hz = Tok()
            hd = sb("hd", [P, KC, 2], F32); hsel = sb("hsel", [P, KC, 2], BF16); t_hs = Tok()
            xs = sb("cxs", [P, 256], F32); zf = sb("czf", [P, 256], F32); t_z = Tok()
            z3 = sb("cz3", [P, 3, 256], F32); t_z3 = Tok()
            zp = sb("czp", [P, 2, 256], F32); t_zp = Tok()
            zpx = sb("czpx", [P, 256], F32); zpv = sb("czpv", [P, 256], F32); t_zq = Tok()
            ysb = sb("cys", [P, 256], F32); t_ys = Tok()
            yd = sb("cyd", [P, 256], BF16); t_yd = Tok()
            self.memset("pool", hz[:], 0.0, w=[t_hz])
            self.memset("pool", zp[:], 0.0, w=[t_zp])
            pf = self.C("pflag")
            cw_t = sb("cw_t", [P, 768], F32); t_cw = Tok()
            o_, w_ = COFF["conv"]
            self.dma("sp", cw_t[:], self.cur_consts[:, o_:o_ + w_], r=(), w=[t_cw])
            cw = cw_t
            for j in range(NO):
                hTo = self.hT_own[:, :, j * P:(j + 1) * P]
                if j == 0:
                    h0, t_h0 = hz[:], t_hz
                else:
                    h0, t_h0 = self.hl2[:, 2 * j - 1, :, :], self.t_hl2[2 * j - 1]
                h1, t_h1 = self.hl2[:, 2 * j, :, :], self.t_hl2[2 * j]
                self.tt("dve", hd[:], h1, h0, ALU.subtract, r=[t_h0, t_h1], w=[t_hs])
                self.stt("dve", hsel[:], hd[:], pf, h0, ALU.mult, ALU.add, r=[t_hs, t_h0, self.t_c32], w=[t_hs])
                for kc in range(KC):
                    self.mm(B[0][:, 0:512], hTo[:, kc, :], WD[:, kc, 0:512], kc == 0, kc == KC - 1,
                            r=[self.t_hT[j], t_WD], w=[tb[1]])
                for kc in range(KC):
                    self.mm(B[1][:, 0:256], hTo[:, kc, :], WD[:, kc, 512:768], kc == 0, kc == KC - 1,
                            r=[self.t_hT[j], t_WD], w=[tb[2]])
                for kc in range(KC):
                    self.mm(B[2][0:2, 0:512], hsel[:, kc, :], WD[:, kc, 256:768], kc == 0, kc == KC - 1,
                            r=[t_hs, t_WD], w=[tb[3]])
                self.cp("act", xs[:], B[1][:, 0:256], r=[tb[2]], w=[t_z])
                self.tt("dve", zf[:], B[0][:, 256:512], xs[:], ALU.mult, r=[tb[1], t_z], w=[t_z])
                for k in range(3):
                    self.tt("pool", z3[:, k, :], zf[:], cw[:, k * 256:(k + 1) * 256], ALU.mult,
                            r=[t_z, t_cw], w=[t_z3])
                self.cp("act", zpx[0:2, :], B[2][0:2, 256:512], r=[tb[3]], w=[t_zq])
                self.tt("dve", zpv[0:2, :], B[2][0:2, 0:256], zpx[0:2, :], ALU.mult, r=[tb[3], t_zq], w=[t_zq])
                self.tt("dve", zp[0:2, 0, :], zpv[0:2, :], cw[0:2, 0:256], ALU.mult, r=[t_zq, t_cw], w=[t_zp])
                self.tt("dve", zp[0:2, 1, :], zpv[0:2, :], cw[0:2, 256:512], ALU.mult, r=[t_zq, t_cw], w=[t_zp])
                yb_ = B[3][:, 0:256]
                self.mm(yb_, self.C("ident"), z3[:, 2, :], True, False, r=[t_z3, self.t_c32], w=[tb[4]], sig=False)
                self.mm(yb_, self.C("sh1"), z3[:, 1, :], False, False, r=[t_z3, self.t_c32], w=[tb[4]], sig=False)
                self.mm(yb_, self.C("sh2"), z3[:, 0, :], False, False, r=[t_z3, self.t_c32], w=[tb[4]], sig=False)
                self.mm(yb_, self.C("ba"), zp[:, 1, :], False, False, r=[t_zp, self.t_c32], w=[tb[4]], sig=False)
                self.mm(yb_, self.C("bb"), zp[:, 0, :], False, True, r=[t_zp, self.t_c32], w=[tb[4]])
                self.cp("act", ysb[:], yb_, r=[tb[4]], w=[t_ys])
                self.tt("dve", yd[:], B[0][:, 0:256], ysb[:], ALU.mult, r=[tb[1], t_ys], w=[t_yd])
                bT = self.bankT
                for pr in range(2):
                    self.tr(bT[:, pr * P:(pr + 1) * P], yd[:, pr * P:(pr + 1) * P], self.ident_bf[:],
                            r=[t_yd, self.t_cbf], w=[tb[0]], sig=(pr == 1))
                self.cp("act", self.yTn[3][:, :, j * P:(j + 1) * P], bT[:, 0:2 * P].rearrange("p (a t) -> p a t", a=2),
                        r=[tb[0]], w=[self.t_yT[3][j]])

    def phase_merge(self):
        nc, s = self.nc, self.s
        NO = self.NO
        B, tb = self.bank, self.t_bank
        TG = 4
        halves = [list(range(0, NO // 2)), list(range(NO // 2, NO))] if NO >= 8 else [list(range(NO))]
        wv = self.d_w_in.rearrange("(k p) c -> p k c", p=P)
        with ExitStack() as es0:
            HT = len(halves[0])
            mT = _alloc(nc, es0, "mT", [P, KC, HT * P], BF16)
            t_mT = [Tok() for _ in range(HT)]
            for tiles in halves:
                j0 = tiles[0]
                groups = [tiles[i:i + TG] for i in range(0, len(tiles), TG)]
                with ExitStack() as es:
                    sb = lambda name, shape, dtp: _alloc(nc, es, name, shape, dtp)
                    Wg = [sb(f"Wg{i}", [P, KC, 4, P], BF16) for i in range(2)]; t_Wg = [Tok() for _ in range(2)]
                    Wb = [sb(f"Wb{i}", [P, 2, 4, P], BF16) for i in range(2)]; t_Wb = [Tok() for _ in range(2)]
                    th = [sb(f"th{i}", [P, 512], F32) for i in range(2)]; t_th = [Tok() for _ in range(2)]
                    acc = sb("macc", [P, 512], F32); tmp = sb("mtmp", [P, 512], F32); t_acc = Tok(); t_tmp = Tok()
                    for cc in range(KC):
                        ws = cc % 2
                        for n in range(4):
                            c0 = OFF["g"] + n * D + cc * P
                            self.load_w(Wg[ws][:, :, n, :], wv[:, :, c0:c0 + P], t_Wg[ws])
                            self.load_w(Wb[ws][:, :, n, :],
                                        self.d_w_branch[n].rearrange("(f p) c -> p f c", p=P)[:, :, cc * P:(cc + 1) * P],
                                        t_Wb[ws])
                        for grp in groups:
                            T = len(grp) * P
                            c_lo = grp[0] * P
                            hts = [self.t_hT[j] for j in grp]
                            for n in range(4):
                                gb = n % 2
                                for kc in range(KC):
                                    self.mm(B[gb][:, 0:T], Wg[ws][:, kc, n, :], self.hT_own[:, kc, c_lo:c_lo + T],
                                            kc == 0, kc == KC - 1, r=[t_Wg[ws]] + hts, w=[tb[1 + gb]])
                                self.act(th[gb][:, 0:T], B[gb][:, 0:T], AF.Tanh, r=[tb[1 + gb]], w=[t_th[gb]], scale=0.5)
                                for f in range(2):
                                    self.mm(B[2 + gb][:, 0:T], Wb[ws][:, f, n, :], self.yTn[n][:, f, c_lo:c_lo + T],
                                            f == 0, f == 1, r=[t_Wb[ws]] + [self.t_yT[n][j] for j in grp],
                                            w=[tb[3 + gb]])
                                if n == 0:
                                    self.stt("dve", acc[:, 0:T], th[gb][:, 0:T], 1.0, B[2 + gb][:, 0:T], ALU.add, ALU.mult,
                                             r=[t_th[gb], tb[3 + gb]], w=[t_acc])
                                else:
                                    self.stt("dve", tmp[:, 0:T], th[gb][:, 0:T], 1.0, B[2 + gb][:, 0:T], ALU.add, ALU.mult,
                                             r=[t_th[gb], tb[3 + gb]], w=[t_tmp])
                                    self.tt("dve", acc[:, 0:T], acc[:, 0:T], tmp[:, 0:T], ALU.add,
                                            r=[t_acc, t_tmp], w=[t_acc])
                            m_lo = (grp[0] - j0) * P
                            self.act(mT[:, cc, m_lo:m_lo + T], acc[:, 0:T], AF.Copy, r=[t_acc],
                                     w=[t_mT[j - j0] for j in grp], scale=0.5)
                s.barrier()
                with ExitStack() as es:
                    Wo = _alloc(nc, es, "Wo", [P, KC, D], BF16); t_Wo = Tok()
                    self.load_w(Wo[:], self.d_w_out.rearrange("(k p) c -> p k c", p=P), t_Wo)
                    for j in tiles:
                        for half in range(2):
                            ob = 4 + half
                            for kc in range(KC):
                                self.mm(B[ob][:, 0:512], mT[:, kc, (j - j0) * P:(j - j0 + 1) * P],
                                        Wo[:, kc, half * 512:(half + 1) * 512], kc == 0, kc == KC - 1,
                                        r=[t_mT[j - j0], t_Wo], w=[tb[1 + ob]])
                            xs_ = self.x_own[:, j, half * 512:(half + 1) * 512]
                            self.tt("dve", xs_, xs_, B[ob][:, 0:512], ALU.add, r=[tb[1 + ob], self.t_x[j]], w=[self.t_x[j]])
                s.barrier()

    def phase_ffn(self):
        nc = self.nc
        NO = self.NO
        B, tb = self.bank, self.t_bank
        TG = 4
        groups = [list(range(i, min(i + TG, NO))) for i in range(0, NO, TG)]
        with ExitStack() as es:
            sb = lambda name, shape, dtp: _alloc(nc, es, name, shape, dtp)
            Wu = [sb(f"Wu{i}", [P, KC, 512], BF16) for i in range(2)]; t_Wu = [Tok() for _ in range(2)]
            Wd = [sb(f"Wd{i}", [P, 4, D], BF16) for i in range(2)]; t_Wd = [Tok() for _ in range(2)]
            rr = [sb(f"frr{i}", [P, 512], F32) for i in range(2)]; t_rr = [Tok() for _ in range(2)]
            uT = [sb(f"fuT{i}", [P, 4, 512], BF16) for i in range(2)]; t_uT = [Tok() for _ in range(2)]
            ob_i = 0
            gi = 0
            for slab in range(DFF // 512):
                ws = slab % 2
                self.load_w(Wu[ws][:], self.d_w_up.rearrange("(k p) c -> p k c", p=P)[:, :, slab * 512:(slab + 1) * 512],
                            t_Wu[ws])
                self.load_w(Wd[ws][:], self.d_w_down[slab * 512:(slab + 1) * 512, :].rearrange("(f p) c -> p f c", p=P),
                            t_Wd[ws])
                for grp in groups:
                    T = len(grp) * P
                    c_lo = grp[0] * P
                    hts = [self.t_hT[j] for j in grp]
                    us = gi % 2
                    gi += 1
                    for fc in range(4):
                        ub = fc % 2
                        for kc in range(KC):
                            self.mm(B[ub][:, 0:T], Wu[ws][:, kc, fc * P:(fc + 1) * P], self.hT_own[:, kc, c_lo:c_lo + T],
                                    kc == 0, kc == KC - 1, r=[t_Wu[ws]] + hts, w=[tb[1 + ub]])
                        self.act(rr[ub][:, 0:T], B[ub][:, 0:T], AF.Relu, r=[tb[1 + ub]], w=[t_rr[ub]])
                        self.act(uT[us][:, fc, 0:T], rr[ub][:, 0:T], AF.Square, r=[t_rr[ub]], w=[t_uT[us]])
                    for ti, j in enumerate(grp):
                        for half in range(2):
                            ob = 2 + (ob_i % 4)
                            ob_i += 1
                            for fc in range(4):
                                self.mm(B[ob][:, 0:512], uT[us][:, fc, ti * P:(ti + 1) * P],
                                        Wd[ws][:, fc, half * 512:(half + 1) * 512], fc == 0, fc == 3,
                                        r=[t_uT[us], t_Wd[ws]], w=[tb[1 + ob]])
                            xs_ = self.x_own[:, j, half * 512:(half + 1) * 512]
                            self.tt("dve", xs_, xs_, B[ob][:, 0:512], ALU.add, r=[tb[1 + ob], self.t_x[j]], w=[self.t_x[j]])


def _rope_table(pos):
    half = 32
    inv = np.float32(10000.0) ** (-np.arange(half, dtype=np.float32) * np.float32(2.0) / np.float32(64))
    ang = pos.astype(np.float32)[:, None] * inv[None, :].astype(np.float32)
    return np.concatenate([np.cos(ang), np.sin(ang)], axis=1).astype(np.float32)


def _consts(params, l, parity):
    c = np.zeros((P, NCONST), np.float32)

    def put(name, arr):
        o, w = COFF[name]
        c[:, o:o + w] = np.asarray(arr, np.float32).reshape(P, w) if np.ndim(arr) == 2 else np.asarray(arr, np.float32)

    idx = np.arange(P)
    put("ident", np.eye(P, dtype=np.float32))
    put("maskT", (idx[:, None] <= idx[None, :]).astype(np.float32))
    put("ones", np.ones((P, P), np.float32))
    put("sh1", (idx[:, None] == idx[None, :] - 1).astype(np.float32))
    put("sh2", (idx[:, None] == idx[None, :] - 2).astype(np.float32))
    ba = np.zeros((P, P), np.float32); ba[1, 0] = 1.0
    bb = np.zeros((P, P), np.float32); bb[0, 0] = 1.0; bb[1, 1] = 1.0
    put("ba", ba)
    put("bb", bb)
    put("lnmixT", params["ln_mix"][l].reshape(KC, P).T)
    put("lnmlpT", params["ln_mlp"][l].reshape(KC, P).T)
    rep = lambda v: np.broadcast_to(np.asarray(v, np.float32).reshape(1, -1), (P, np.size(v)))
    put("sgu_norm", rep(params["sgu_norm"][l]))
    put("sgu_b", params["sgu_b"][l].T)
    put("qn", rep(np.tile(params["q_norm"][l], 4)))
    put("kn", rep(np.tile(params["k_norm"][l], 4)))
    put("kidx", rep(params["kidx_norm"][l]))
    put("ib", rep(params["mlstm_i_bias"][l]))
    put("fb", rep(params["mlstm_f_bias"][l]))
    put("mn", rep(params["mlstm_norm"][l]))
    put("conv", rep(params["conv_w"][l].reshape(-1)))
    cm = np.zeros((P, 256), np.float32)
    tri = np.where(idx[None, :] <= idx[:, None], 0.0, NEG).astype(np.float32)
    if parity == 0:
        cm[:, 0:128] = tri
        cm[:, 128:256] = NEG
    else:
        cm[:, 128:256] = tri
    put("cmask", cm)
    put("pflag", np.full((P, 1), float(parity), np.float32))
    put("neghalf", np.full((P, 8), -0.5, np.float32))
    put("sgu_wT", np.transpose(params["sgu_w"][l], (2, 0, 1)).reshape(P, 4 * P))
    return c


_NC_CACHE = {}


def _get_nc(S, dbg=None):
    key = (S, tuple(dbg) if dbg else None)
    if key not in _NC_CACHE:
        _NC_CACHE[key] = Builder(S, dbg).build()
    return _NC_CACHE[key]


def run_model(x, params, dbg=None):
    Bn, S, _ = x.shape
    NT = S // P
    NO = NT // 2
    nc = Builder(S, dbg).build()
    rope_all = np.ascontiguousarray(_rope_table(np.arange(S)).reshape(NT, P, 64))
    c00, c01 = _consts(params, 0, 0), _consts(params, 0, 1)
    wts = {k: np.ascontiguousarray(params[k]) for k in ("w_in", "w_branch", "w_out", "w_up", "w_down")}
    in_maps = []
    for core in range(8):
        b, par = core // 2, core % 2
        own = [2 * j + par for j in range(NO)]
        in_maps.append(dict(
            x=np.ascontiguousarray(x[b]), consts3=np.ascontiguousarray(np.stack([c00, c01, _consts(params, 1, par)])),
            rope_all=rope_all, rope_own=np.ascontiguousarray(rope_all[own]), **wts))
    res = run_bass_kernel_spmd(nc, in_maps, core_ids=list(range(8)))
    out = np.empty_like(x)
    for core in range(8):
        b, par = core // 2, core % 2
        y = np.asarray(res.results[core]["y"]).reshape(NO, P, D)
        ov = out[b].reshape(NT, P, D)
        for j in range(NO):
            ov[2 * j + par] = y[j]
    return out, res


def kernel(**inputs):
    x = np.asarray(inputs["x"], np.float32)
    params = {k: np.asarray(v, np.float32) for k, v in inputs.items() if k != "x"}
    out, _ = run_model(x, params)
    return out
```

```python
import numpy as np
from contextlib import ExitStack
import concourse.bass as bass
import concourse.mybir as mybir
from concourse.bass_utils import run_bass_kernel_spmd

F32 = mybir.dt.float32
BF16 = mybir.dt.bfloat16
AF = mybir.ActivationFunctionType
ALU = mybir.AluOpType
AX = mybir.AxisListType

P = 128
D = 1024
KC = 8
DFF = 4096
IN_W = 7760
EPS = 1e-6
OFF = dict(a_u=0, a_v=256, b_q=512, b_k=768, b_v=1024, b_qi=1280, b_ki=1792, b_wi=1856,
           c_q=1864, c_k=2120, c_v=2376, c_o=2632, c_i=2888, c_f=2892,
           d_b=2896, d_c=3152, d_x=3408, g=3664)
NBISECT = 12
NEG = -1.0e30
MBIAS = -30000.0

CONST_SPEC = [
    ("ident", 128), ("maskT", 128), ("ones", 128), ("sh1", 128), ("sh2", 128), ("ba", 128), ("bb", 128),
    ("lnmixT", 8), ("lnmlpT", 8), ("sgu_norm", 256), ("sgu_b", 4), ("qn", 256), ("kn", 256),
    ("kidx", 64), ("ib", 4), ("fb", 4), ("mn", 256), ("cmask", 256), ("pflag", 1),
    ("neghalf", 8), ("conv", 768), ("sgu_wT", 512),
]
COFF = {}
_o = 0
for _n, _w in CONST_SPEC:
    COFF[_n] = (_o, _w)
    _o += _w
NCONST = _o
NC_TOP = COFF["conv"][0]


def _nbytes(shape, dtp):
    n = 1
    for d in shape[1:]:
        n *= d
    return n * (4 if dtp == F32 else 2)


def _alloc(nc, es, name, shape, dtp):
    _alloc.n += 1
    name = f"{name}_{_alloc.n}"
    t = es.enter_context(nc.sbuf_tensor(name, shape, dtp))
    rem = _nbytes(shape, dtp) % 32
    if rem:
        es.enter_context(nc.sbuf_tensor(name + "_pad", [P, (32 - rem) // 2], BF16))
    return t


_alloc.n = 0


def _unused():
    return None

class Dep:
    __slots__ = ("sem", "val", "eng", "key")

    def __init__(self, sem, val, eng, key):
        self.sem, self.val, self.eng, self.key = sem, val, eng, key


class Tok:
    __slots__ = ("w", "r", "name")

    def __init__(self, name=""):
        self.w = None
        self.r = {}
        self.name = name


class Eng:
    def __init__(self, name, h, sem):
        self.name, self.h, self.sem = name, h, sem
        self.count = 0
        self.waited = {}
        self.n_inst = 0


class Sched:
    NDMA = 8

    def __init__(self, nc):
        self.nc = nc
        self.E = {}
        for name, h in (("pe", nc.tensor), ("act", nc.scalar), ("dve", nc.vector),
                        ("pool", nc.gpsimd), ("sp", nc.sync)):
            self.E[name] = Eng(name, h, nc.alloc_semaphore("s_" + name))
        self.dq = {}
        for q in ("sp", "pool", "act"):
            self.dq[q] = dict(n=0, sems=[nc.alloc_semaphore(f"d_{q}{i}") for i in range(self.NDMA)])

    def _wait(self, E, d):
        if E.waited.get(d.key, 0) >= d.val:
            return
        E.h.wait_ge(d.sem, d.val)
        E.n_inst += 1
        E.waited[d.key] = d.val

    def op(self, eng, fn, r=(), w=(), dma=False, sig=True):
        E = self.E[eng]
        deps = []
        for t in r:
            if t.w is not None:
                deps.append(t.w)
        for t in w:
            if t.w is not None:
                deps.append(t.w)
            deps.extend(t.r.values())
        if dma:
            q = self.dq[eng]
            i = q["n"]
            slot = i % self.NDMA
            sem = q["sems"][slot]
            key = f"d_{eng}{slot}"
            if i >= self.NDMA:
                deps.append(Dep(sem, 16 * (i // self.NDMA), None, key))
            comp = Dep(sem, 16 * (i // self.NDMA + 1), None, key)
            q["n"] += 1
        else:
            if sig:
                E.count += 1
                comp = Dep(E.sem, E.count, eng, "e_" + eng)
            else:
                comp = Dep(E.sem, E.count + 1, eng, "e_" + eng)
        for d in deps:
            if d.eng == eng and not dma and eng == "pe":
                continue
            self._wait(E, d)
        inst = fn(E.h)
        E.n_inst += 1
        if dma:
            inst.then_inc(comp.sem, 16)
        elif sig:
            inst.then_inc(comp.sem, 1)
        for t in r:
            old = t.r.get(comp.key)
            if old is None or old.val < comp.val:
                t.r[comp.key] = comp
        for t in w:
            t.w = comp
            t.r = {}
        return comp

    def barrier(self):
        deps = [Dep(F.sem, F.count, F.name, "e_" + F.name) for F in self.E.values() if F.count > 0]
        for qn, q in self.dq.items():
            for slot in range(min(q["n"], self.NDMA)):
                last = ((q["n"] - 1 - slot) // self.NDMA) + 1
                deps.append(Dep(q["sems"][slot], 16 * last, None, f"d_{qn}{slot}"))
        for E in self.E.values():
            for d in deps:
                if d.eng == E.name:
                    continue
                self._wait(E, d)

    def final_wait(self, eng, deps):
        E = self.E[eng]
        for d in deps:
            self._wait(E, d)


class _Idx:
    def __init__(self, fn):
        self.fn = fn

    def __getitem__(self, j):
        return self.fn(j)


class Builder:
    def __init__(self, S, dbg=None):
        self.S = S
        self.NT = S // P
        self.NO = self.NT // 2
        self.dbg = dbg or ()
        nc = bass.Bass("TRN2", target_bir_lowering=False)
        self.nc = nc
        self.s = Sched(nc)
        NO, NT = self.NO, self.NT
        dt = nc.dram_tensor
        self.d_x = dt("x", [S, D], F32, kind="ExternalInput").ap()
        self.d_consts3 = dt("consts3", [3, P, NCONST], F32, kind="ExternalInput").ap()
        self.d_rope_all = dt("rope_all", [NT, P, 64], F32, kind="ExternalInput").ap()
        self.d_rope_own_in = dt("rope_own", [NO, P, 64], F32, kind="ExternalInput").ap()
        self.D_w_in = dt("w_in", [2, D, IN_W], F32, kind="ExternalInput").ap()
        self.D_w_branch = dt("w_branch", [2, 4, 256, D], F32, kind="ExternalInput").ap()
        self.D_w_out = dt("w_out", [2, D, D], F32, kind="ExternalInput").ap()
        self.D_w_up = dt("w_up", [2, D, DFF], F32, kind="ExternalInput").ap()
        self.D_w_down = dt("w_down", [2, DFF, D], F32, kind="ExternalInput").ap()
        self.d_y = dt("y", [NO * P, D], F32, kind="ExternalOutput").ap()
        self.d_x1 = dt("x1_scratch", [S, D], F32).ap()
        self.d_hTs = dt("hT_scratch", [NT, P, KC * P], BF16).ap()
        self.d_dbg = {}
        for name, shape in self.dbg:
            self.d_dbg[name] = dt("dbg_" + name, list(shape), F32, kind="ExternalOutput").ap()

    def mm(self, out, lhsT, rhs, start, stop, r, w, sig=None):
        sig = stop if sig is None else sig
        return self.s.op("pe", lambda e: e.matmul(out, lhsT, rhs, start=start, stop=stop,
                                                  skip_group_check=True), r=r, w=w, sig=sig)

    def tr(self, out, in_, ident, r, w, sig=True):
        return self.s.op("pe", lambda e: e.transpose(out, in_, ident), r=r, w=w, sig=sig)

    def act(self, out, in_, func, r, w, **kw):
        return self.s.op("act", lambda e: e.activation(out=out, in_=in_, func=func, **kw), r=r, w=w)

    def tt(self, eng, out, in0, in1, op, r, w):
        return self.s.op(eng, lambda e: e.tensor_tensor(out=out, in0=in0, in1=in1, op=op), r=r, w=w)

    def ts(self, eng, out, in0, s1, s2, op0, op1=None, r=(), w=(), accum_out=None):
        def f(e):
            kw = {}
            if op1 is not None:
                kw["op1"] = op1
            if accum_out is not None:
                kw["accum_out"] = accum_out
            return e.tensor_scalar(out=out, in0=in0, scalar1=s1, scalar2=s2, op0=op0, **kw)
        return self.s.op(eng, f, r=r, w=w)

    def stt(self, eng, out, in0, scalar, in1, op0, op1, r, w):
        return self.s.op(eng, lambda e: e.scalar_tensor_tensor(out=out, in0=in0, scalar=scalar, in1=in1,
                                                               op0=op0, op1=op1), r=r, w=w)

    def cp(self, eng, out, in_, r, w):
        if eng == "act":
            return self.s.op("act", lambda e: e.copy(out=out, in_=in_), r=r, w=w)
        return self.s.op(eng, lambda e: e.tensor_copy(out=out, in_=in_), r=r, w=w)

    def red(self, eng, out, in_, op, r, w, absval=False):
        return self.s.op(eng, lambda e: e.tensor_reduce(out=out, in_=in_, axis=AX.X, op=op,
                                                        apply_absolute_value=absval), r=r, w=w)

    def memset(self, eng, ap, val, w):
        return self.s.op(eng, lambda e: e.memset(ap, val), r=(), w=w)

    def dma(self, q, out, in_, r, w):
        h = {"sp": self.nc.sync, "pool": self.nc.gpsimd, "act": self.nc.scalar}[q]
        return self.s.op(q, lambda e: h.dma_start(out=out, in_=in_), r=r, w=w, dma=True)

    def C(self, name, a=0, b=None):
        o, wd = COFF[name]
        b = wd if b is None else b
        return self.c32[:, o + a:o + b]

    def rstd(self, ss, n, width, eps, out, r, w):
        tv = self.t_rs
        self.ts("pool", self.rs_tmp[:, 0:n], ss, 1.0 / width, eps, ALU.mult, ALU.add, r=r, w=[tv])
        self.tt("pool", out, self.rs_tmp[:, 0:n], self.C("neghalf", 0, n), ALU.pow, r=[tv, self.t_c32], w=w)

    def norm_transpose(self, x_ap, t_x, gainT, hT, t_hT, slot):
        xn, t_xn = self.xn[slot], self.t_xn[slot]
        ss, t_ss = self.nt_ss[slot], self.t_nt_ss[slot]
        rs, t_rsd = self.nt_rs[slot], self.t_nt_rs[slot]
        self.act(xn[:], x_ap, AF.Square, r=[t_x], w=[t_xn, t_ss], accum_out=ss[:, 0:1])
        self.rstd(ss[:, 0:1], 1, D, EPS, rs[:, 0:1], r=[t_ss], w=[t_rsd])
        self.ts("dve", xn[:], x_ap, rs[:, 0:1], None, ALU.mult, r=[t_x, t_rsd], w=[t_xn])
        bT, t_bT = self.bankT, self.t_bank[0]
        for kc in range(KC):
            self.tr(bT[:, kc * P:(kc + 1) * P], xn[:, kc * P:(kc + 1) * P], self.ident_bf[:],
                    r=[t_xn, self.t_cbf], w=[t_bT], sig=(kc == KC - 1))
        self.tt("dve", hT, bT[:].rearrange("p (k t) -> p k t", k=KC),
                gainT.unsqueeze(2).to_broadcast([P, KC, P]), ALU.mult, r=[t_bT, self.t_c32], w=[t_hT])

    def headnorm(self, src, H, gain, out, r, w, eng="dve"):
        t = self.t_hn
        sq = self.hn_sq[:, 0:H * 64]
        self.tt(eng, sq, src, src, ALU.mult, r=r, w=[t])
        self.red(eng, self.hn_ss[:, 0:H], sq.rearrange("p (h e) -> p h e", h=H), ALU.add, r=[t], w=[t])
        self.rstd(self.hn_ss[:, 0:H], H, 64, EPS, self.hn_rs[:, 0:H], r=[t], w=[t])
        self.tt(eng, out.rearrange("p (h e) -> p h e", h=H), src.rearrange("p (h e) -> p h e", h=H),
                self.hn_rs[:, 0:H].unsqueeze(2).to_broadcast([P, H, 64]), ALU.mult, r=list(r) + [t], w=w)
        if gain is not None:
            self.tt(eng, out, out, gain, ALU.mult, r=list(w) + [self.t_c32], w=w)

    def rotary(self, src, H, rope, t_rope, out, r, w, eng="pool", scratch=None):
        ro_a, ro_b, t = scratch if scratch is not None else (self.ro_a, self.ro_b, self.t_ro)
        s4 = src.rearrange("p (h two e) -> p h two e", h=H, two=2)
        a4 = ro_a[:, 0:H * 64].rearrange("p (h two e) -> p h two e", h=H, two=2)
        b4 = ro_b[:, 0:H * 64].rearrange("p (h two e) -> p h two e", h=H, two=2)
        o4 = out.rearrange("p (h two e) -> p h two e", h=H, two=2)
        cosb = rope[:, 0:32].unsqueeze(1).unsqueeze(1).to_broadcast([P, H, 2, 32])
        sinb = rope[:, 32:64].unsqueeze(1).to_broadcast([P, H, 32])
        rr = list(r) + [t_rope]
        self.tt(eng, a4, s4, cosb, ALU.mult, r=rr, w=[t])
        self.tt(eng, b4[:, :, 0, :], s4[:, :, 1, :], sinb, ALU.mult, r=rr, w=[t])
        self.tt(eng, b4[:, :, 1, :], s4[:, :, 0, :], sinb, ALU.mult, r=rr, w=[t])
        self.tt(eng, o4[:, :, 0, :], a4[:, :, 0, :], b4[:, :, 0, :], ALU.subtract, r=[t], w=w)
        self.tt(eng, o4[:, :, 1, :], a4[:, :, 1, :], b4[:, :, 1, :], ALU.add, r=[t], w=w)

    def load_w(self, dst, src, t_dst):
        return self.dma("pool", dst, src, r=(), w=[t_dst])

    def build(self):
        nc, s = self.nc, self.s
        NO, NT, S = self.NO, self.NT, self.S
        with ExitStack() as top:
            sb = lambda name, shape, dtp: _alloc(nc, top, name, shape, dtp)
            self.bank = [top.enter_context(nc.psum_tensor(f"bank{i}", [P, 512], F32)) for i in range(1, 8)]
            self.bankT = top.enter_context(nc.psum_tensor("bankT", [P, 1024], BF16))
            self.t_bank = [Tok(f"bank{i}") for i in range(8)]
            self.c32 = sb("c32", [P, NC_TOP], F32)
            self.t_c32 = Tok("c32")
            self.ident_bf = sb("ident_bf", [P, P], BF16)
            self.irep_bf = sb("irep_bf", [P, 4, P], BF16)
            self.maskT_bf = sb("maskT_bf", [P, 4, P], BF16)
            self.ones_bf = sb("ones_bf", [P, P], BF16)
            self.t_cbf = Tok("cbf")
            self.x_own = sb("x_own", [P, NO, D], F32)
            self.t_x = [Tok(f"x{j}") for j in range(NO)]
            self.yTn = [None] * 4
            self.yTn[1] = sb("yT1", [P, 2, NO * P], BF16)
            self.t_yT = [[Tok(f"yT{n}_{j}") for j in range(NO)] for n in range(4)]
            self.hl2 = sb("hl2", [P, NT, KC, 2], BF16)
            self.t_hl2 = [Tok(f"hl2_{g}") for g in range(NT)]
            xn0 = sb("xn0", [P, D], BF16)
            self.xn = [xn0, xn0]
            t_xn0 = Tok()
            self.t_xn = [t_xn0, t_xn0]
            self.nt_ss = [sb(f"ntss{i}", [P, 1], F32) for i in range(2)]
            self.t_nt_ss = [Tok() for _ in range(2)]
            self.nt_rs = [sb(f"ntrs{i}", [P, 1], F32) for i in range(2)]
            self.t_nt_rs = [Tok() for _ in range(2)]
            self.t_junk_act = Tok()
            self.t_junk_dve = Tok()
            self.rs_tmp = sb("rs_tmp", [P, 8], F32)
            self.t_rs = Tok()
            self.hn_sq = sb("hn_sq", [P, 512], F32)
            self.hn_ss = sb("hn_ss", [P, 8], F32)
            self.hn_rs = sb("hn_rs", [P, 8], F32)
            self.t_hn = Tok()
            self.t_ro = Tok()

            import os
            ph = os.environ.get("KPH", "att,mlstm,sgu,conv,merge,ffn").split(",")
            npass = int(os.environ.get("KNP", "3"))
            passes = [(0, 0), (0, 1), (1, None)][:npass]
            outs = []
            for pi, (l, q) in enumerate(passes):
                s.barrier()
                self.cur_consts = self.d_consts3[pi]
                self.dma("sp", self.c32[:], self.cur_consts[:, 0:NC_TOP], r=(), w=[self.t_c32])
                if pi == 0:
                    self.cp("dve", self.ident_bf[:], self.C("ident"), r=[self.t_c32], w=[self.t_cbf])
                    self.cp("dve", self.ones_bf[:], self.C("ones"), r=[self.t_c32], w=[self.t_cbf])
                    for h in range(4):
                        self.cp("dve", self.irep_bf[:, h, :], self.C("ident"), r=[self.t_c32], w=[self.t_cbf])
                        self.cp("dve", self.maskT_bf[:, h, :], self.C("maskT"), r=[self.t_c32], w=[self.t_cbf])
                self.d_w_in, self.d_w_branch, self.d_w_out = self.D_w_in[l], self.D_w_branch[l], self.D_w_out[l]
                self.d_w_up, self.d_w_down = self.D_w_up[l], self.D_w_down[l]
                if q is not None:
                    self.d_xf = self.d_x
                    self.d_rope_own = _Idx(lambda j, q=q: self.d_rope_all[2 * j + q])
                    for j in range(NO):
                        g = 2 * j + q
                        self.dma("sp", self.x_own[:, j, :], self.d_x[g * P:(g + 1) * P, :], r=(), w=[self.t_x[j]])
                else:
                    self.d_xf = self.d_x1
                    self.d_rope_own = self.d_rope_own_in
                    with ExitStack() as es:
                        xb = [_alloc(nc, es, f"xblend{i}", [P, D], F32) for i in range(2)]
                        t_xb = [Tok() for _ in range(2)]
                        for j in range(NO):
                            sl = j % 2
                            self.dma("sp", self.x_own[:, j, :], self.d_x1[(2 * j) * P:(2 * j + 1) * P, :], r=(), w=[self.t_x[j]])
                            self.dma("sp", xb[sl][:], self.d_x1[(2 * j + 1) * P:(2 * j + 2) * P, :], r=(), w=[t_xb[sl]])
                            self.tt("dve", xb[sl][:], xb[sl][:], self.x_own[:, j, :], ALU.subtract,
                                    r=[t_xb[sl], self.t_x[j]], w=[t_xb[sl]])
                            self.stt("dve", self.x_own[:, j, :], xb[sl][:], self.C("pflag"), self.x_own[:, j, :],
                                     ALU.mult, ALU.add, r=[t_xb[sl], self.t_x[j], self.t_c32], w=[self.t_x[j]])
                        s.barrier()
                if "att" in ph:
                    self.phase_attention()
                s.barrier()
                with ExitStack() as mid:
                    self.hT_own = _alloc(nc, mid, "hT_own", [P, KC, NO * P], BF16)
                    for n in (0, 2, 3):
                        self.yTn[n] = _alloc(nc, mid, f"yT{n}", [P, 2, NO * P], BF16)
                    self.t_hT = [Tok(f"hT{j}") for j in range(NO)]
                    self.phase_hT(self.C("lnmixT"))
                    if "mlstm" in ph:
                        self.phase_mlstm()
                    s.barrier()
                    if "sgu" in ph:
                        self.phase_sgu()
                    s.barrier()
                    if "conv" in ph:
                        self.phase_conv()
                    s.barrier()
                    if "merge" in ph:
                        self.phase_merge()
                    s.barrier()
                    if "ffn" in ph:
                        self.phase_hT(self.C("lnmlpT"))
                        self.phase_ffn()
                    s.barrier()
                last = (pi == len(passes) - 1)
                for j in range(NO):
                    if last:
                        dst = self.d_y[j * P:(j + 1) * P, :]
                    else:
                        g = 2 * j + q
                        dst = self.d_x1[g * P:(g + 1) * P, :]
                    outs.append(self.dma("sp", dst, self.x_own[:, j, :], r=[self.t_x[j]], w=()))
            s.barrier()
            s.final_wait("sp", outs + self.dbg_deps)
        return nc

    dbg_deps = []

    def dump(self, name, src_ap, toks, dst_slice=None):
        if name not in self.d_dbg:
            return
        dst = self.d_dbg[name] if dst_slice is None else dst_slice(self.d_dbg[name])
        self.dbg_deps = self.dbg_deps + [self.dma("sp", dst, src_ap, r=toks, w=())]

    def phase_hT(self, gainT):
        for j in range(self.NO):
            self.norm_transpose(self.x_own[:, j, :], self.t_x[j], gainT,
                                self.hT_own[:, :, j * P:(j + 1) * P], self.t_hT[j], j % 2)

    def phase_attention(self):
        nc, s = self.nc, self.s
        NO, NT, S = self.NO, self.NT, self.S
        B = self.bank
        tb = self.t_bank
        with ExitStack() as es:
            sb = lambda name, shape, dtp: _alloc(nc, es, name, shape, dtp)
            WA = sb("WA", [P, KC, 1352], BF16)
            t_WA = Tok("WA")
            wv = self.d_w_in.rearrange("(k p) c -> p k c", p=P)
            for (dst0, src0, n) in ((0, OFF["b_q"], 256), (256, OFF["b_wi"], 8), (264, OFF["b_qi"], 512),
                                    (776, OFF["b_k"], 512), (1288, OFF["b_ki"], 64)):
                self.load_w(WA[:, :, dst0:dst0 + n], wv[:, :, src0:src0 + n], t_WA)
            KT = sb("KT", [P, 2, S], BF16)
            t_KT = [Tok(f"KT{g}") for g in range(NT)]
            VA = sb("VA", [P, NT, 4, 65], BF16)
            t_VA = [Tok(f"VA{g}") for g in range(NT)]
            KI2 = sb("KI2", [P, S], BF16)
            t_KI = [Tok(f"KI{g}") for g in range(NT)]
            xt0 = sb("xt0", [P, D], F32)
            xt = [xt0, xt0]
            t_xt0 = Tok()
            t_xt = [t_xt0, t_xt0]
            hTg0 = sb("hTg0", [P, KC, P], BF16)
            hT = [hTg0, hTg0]
            t_hTg0 = Tok()
            t_hT = [t_hTg0, t_hTg0]
            rope_g = [sb(f"ropeg{i}", [P, 64], F32) for i in range(2)]
            t_rope_g = [Tok() for _ in range(2)]
            rope_o = sb("ropeo", [P, 64], F32)
            t_rope_o = Tok()
            ksb = sb("ksb", [P, 256], F32); t_ksb = Tok()
            kisb = sb("kisb", [P, 64], F32); t_kisb = Tok()
            kr = sb("kr", [P, 256], BF16); t_kr = Tok()
            ki2 = sb("ki2", [P, 128], BF16); t_ki2 = Tok()
            bn6 = sb("bn6", [P, 8], F32); t_bn = Tok()
            SC = sb("SC", [P, S], F32); t_SC = Tok("SC")
            self.ro_a = SC[:, 0:512]
            self.ro_b = SC[:, 512:1024]
            self.t_ro = t_SC
            rog = sb("rog", [P, 1024], F32)
            sc_g = (rog[:, 0:512], rog[:, 512:1024], Tok("rog"))
            sc_o = sc_g
            MB = sb("MB", [P, S], BF16); t_MB = Tok("MB")
            qsb = sb("qsb", [P, 264], F32); t_qsb = Tok()
            qisb = sb("qisb", [P, 512], F32); t_qisb = Tok()
            qr = sb("qr", [P, 256], BF16); t_qr = Tok()
            qir = sb("qir", [P, 512], BF16); t_qir = Tok()
            QTp2 = [sb(f"QTp{i}", [P, 4, P], BF16) for i in range(2)]; t_QTp2 = [Tok() for _ in range(2)]

            QiTp = sb("QiTp", [P, 8, P], BF16); t_QiTp = Tok()
            Dg = sb("Dg", [P, 8, P], BF16); t_Dg = Tok()
            wab = sb("wab", [P, 8], F32); wsg = sb("wsg", [P, 8], F32); wtmp = sb("wtmp", [P, 8], F32); t_w8 = Tok()
            Rr = [sb(f"Rr{i}", [P, 512], BF16) for i in range(2)]
            t_Rr = [Tok() for _ in range(2)]
            PT = [sb(f"PT{i}", [P, 4, P], BF16) for i in range(2)]
            t_PT = [Tok() for _ in range(2)]
            bs = sb("bs", [P, 8], F32); t_bs = Tok()
            hwt = sb("hwt", [P, 64], F32); t_hw = Tok()
            t_cnt = Tok()
            hk = sb("hk", [P, NBISECT + 2], F32)
            cnt = sb("cnt", [P, 1], F32)
            osb = sb("osb", [P, 4, 65], F32); t_osb = Tok()
            rden = sb("rden", [P, 4], F32)
            yb = sb("yb", [P, 256], BF16); t_yb = Tok()

            self.memset("pool", VA[:], 1.0, w=t_VA)
            for i in range(2):
                self.memset("pool", QTp2[i][:], 0.0, w=[t_QTp2[i]])
            self.memset("pool", QiTp[:], 0.0, w=[t_QiTp])
            for k in range(NBISECT + 2):
                self.memset("pool", hk[:, k:k + 1], 2.0 ** (-(k + 1)), w=[t_bs])

            def proc_global(g):
                sl = g % 2
                self.dma("sp", xt[sl][:], self.d_xf[g * P:(g + 1) * P, :], r=(), w=[t_xt[sl]])
                self.dma("sp", rope_g[sl][:], self.d_rope_all[g], r=(), w=[t_rope_g[sl]])
                self.norm_transpose(xt[sl][:], t_xt[sl], self.C("lnmixT"), hT[sl][:], t_hT[sl], sl)
                self.cp("pool", self.hl2[:, g, :, :], hT[sl][:, :, P - 2:P], r=[t_hT[sl]], w=[self.t_hl2[g]])
                self.dma("pool", self.d_hTs[g], hT[sl][:].rearrange("p k t -> p (k t)"), r=[t_hT[sl]], w=())
                for kc in range(KC):
                    self.mm(B[0][:, 0:512], hT[sl][:, kc, :], WA[:, kc, 776:1288], kc == 0, kc == KC - 1,
                            r=[t_hT[sl], t_WA], w=[tb[1]])
                for kc in range(KC):
                    self.mm(B[1][:, 0:64], hT[sl][:, kc, :], WA[:, kc, 1288:1352], kc == 0, kc == KC - 1,
                            r=[t_hT[sl], t_WA], w=[tb[2]])
                self.cp("dve", ksb[:], B[0][:, 0:256], r=[tb[1]], w=[t_ksb])
                self.cp("dve", VA[:, g, :, 0:64], B[0][:, 256:512].rearrange("p (h e) -> p h e", h=4),
                        r=[tb[1]], w=[t_VA[g]])
                self.cp("dve", kisb[:], B[1][:, 0:64], r=[tb[2]], w=[t_kisb])
                self.headnorm(ksb[:], 4, self.C("kn"), ksb[:], r=[t_ksb], w=[t_ksb])
                self.rotary(ksb[:], 4, rope_g[sl], t_rope_g[sl], kr[:], r=[t_ksb], w=[t_kr], eng="dve", scratch=sc_g)
                bT = self.bankT
                for pr in range(2):
                    self.tr(bT[:, pr * P:(pr + 1) * P], kr[:, pr * P:(pr + 1) * P], self.ident_bf[:],
                            r=[t_kr, self.t_cbf], w=[tb[0]], sig=False)
                self.red("dve", bn6[:, 0:1], kisb[:], ALU.add, r=[t_kisb], w=[t_bn])
                self.ts("dve", bn6[:, 1:2], bn6[:, 0:1], 1.0 / 64, None, ALU.mult, r=[t_bn], w=[t_bn])
                self.ts("dve", kisb[:], kisb[:], bn6[:, 1:2], None, ALU.subtract, r=[t_bn, t_kisb], w=[t_kisb])
                self.tt("dve", self.hn_sq[:, 0:64], kisb[:], kisb[:], ALU.mult, r=[t_kisb], w=[self.t_hn])
                self.red("dve", bn6[:, 2:3], self.hn_sq[:, 0:64], ALU.add, r=[self.t_hn], w=[t_bn])
                self.rstd(bn6[:, 2:3], 1, 64, EPS, self.hn_rs[:, 0:1], r=[t_bn], w=[self.t_hn])
                self.ts("dve", kisb[:], kisb[:], self.hn_rs[:, 0:1], None, ALU.mult, r=[self.t_hn, t_kisb], w=[t_kisb])
                self.tt("dve", kisb[:], kisb[:], self.C("kidx"), ALU.mult, r=[t_kisb, self.t_c32], w=[t_kisb])
                self.rotary(kisb[:], 1, rope_g[sl], t_rope_g[sl], ki2[:, 0:64], r=[t_kisb], w=[t_ki2], eng="dve", scratch=sc_g)
                self.cp("dve", ki2[:, 64:128], ki2[:, 0:64], r=[t_ki2], w=[t_ki2])
                self.tr(bT[:, 2 * P:3 * P], ki2[:], self.ident_bf[:], r=[t_ki2, self.t_cbf], w=[tb[0]], sig=True)
                self.cp("dve", KT[:, :, g * P:(g + 1) * P], bT[:, 0:2 * P].rearrange("p (a t) -> p a t", a=2),
                        r=[tb[0]], w=[t_KT[g]])
                self.cp("dve", KI2[:, g * P:(g + 1) * P], bT[:, 2 * P:3 * P], r=[tb[0]], w=[t_KI[g]])

            def proc_own(j):
                QTp, t_QTp = QTp2[j % 2], t_QTp2[j % 2]
                sl = j % 2
                xj = self.x_own[:, j, :]
                self.dma("sp", rope_o[:], self.d_rope_own[j], r=(), w=[t_rope_o])
                self.norm_transpose(xj, self.t_x[j], self.C("lnmixT"), hT[sl][:], t_hT[sl], sl)
                for kc in range(KC):
                    self.mm(B[0][:, 0:264], hT[sl][:, kc, :], WA[:, kc, 0:264], kc == 0, kc == KC - 1,
                            r=[t_hT[sl], t_WA], w=[tb[1]])
                for kc in range(KC):
                    self.mm(B[1][:, 0:512], hT[sl][:, kc, :], WA[:, kc, 264:776], kc == 0, kc == KC - 1,
                            r=[t_hT[sl], t_WA], w=[tb[2]])
                self.cp("dve", qsb[:], B[0][:, 0:264], r=[tb[1]], w=[t_qsb])
                self.cp("dve", qisb[:], B[1][:, 0:512], r=[tb[2]], w=[t_qisb])
                self.headnorm(qsb[:, 0:256], 4, self.C("qn"), qsb[:, 0:256], r=[t_qsb], w=[t_qsb])
                self.rotary(qsb[:, 0:256], 4, rope_o, t_rope_o, qr[:], r=[t_qsb], w=[t_qr], eng="dve", scratch=sc_g)
                self.rotary(qisb[:], 8, rope_o, t_rope_o, qir[:], r=[t_qisb], w=[t_qir], eng="dve", scratch=sc_o)
                bT = self.bankT
                for pr in range(2):
                    self.tr(bT[:, pr * P:(pr + 1) * P], qr[:, pr * P:(pr + 1) * P], self.ident_bf[:],
                            r=[t_qr, self.t_cbf], w=[tb[0]], sig=False)
                for pr in range(4):
                    self.tr(bT[:, (2 + pr) * P:(3 + pr) * P], qir[:, pr * P:(pr + 1) * P], self.ident_bf[:],
                            r=[t_qir, self.t_cbf], w=[tb[0]], sig=(pr == 3))
                for h in range(4):
                    lo = (h % 2) * 64
                    self.cp("dve", QTp[lo:lo + 64, h, :], bT[lo:lo + 64, (h // 2) * P:(h // 2 + 1) * P],
                            r=[tb[0]], w=[t_QTp])
                for h in range(8):
                    lo = (h % 2) * 64
                    self.cp("dve", QiTp[lo:lo + 64, h, :],
                            bT[lo:lo + 64, (2 + h // 2) * P:(3 + h // 2) * P], r=[tb[0]], w=[t_QiTp])
                wsrc = qsb[:, 256:264]
                self.ts("dve", wsg[:], wsrc, -1.0, None, ALU.mult, r=[t_qsb], w=[t_w8])
                self.tt("dve", wab[:], wsrc, wsg[:], ALU.max, r=[t_qsb, t_w8], w=[t_w8])
                self.ts("dve", wab[:], wab[:], float(8 ** -0.5 * 64 ** -0.5), None, ALU.mult, r=[t_w8], w=[t_w8])
                self.ts("dve", wsg[:], wsrc, 0.0, None, ALU.is_gt, r=[t_qsb, t_w8], w=[t_w8])
                self.ts("dve", wtmp[:], wsrc, 0.0, None, ALU.is_lt, r=[t_qsb], w=[t_w8])
                self.tt("dve", wsg[:], wsg[:], wtmp[:], ALU.subtract, r=[t_w8], w=[t_w8])
                for h in range(8):
                    self.ts("dve", Dg[:, h, :], self.C("ident"), wsg[:, h:h + 1], None, ALU.mult,
                            r=[t_w8, self.t_c32], w=[t_Dg])

            def proc_own_idx(j):
                nk = (j + 1) * 256
                nblk = (nk + 511) // 512
                for b in range(nblk):
                    k0 = b * 512
                    kw = min(512, nk - k0)
                    kg = list(range(k0 // P, (k0 + kw) // P))
                    pend = None
                    for h in range(8):
                        rb = 3 + (h % 2)
                        self.mm(B[rb - 1][:, 0:kw], QiTp[:, h, :], KI2[:, k0:k0 + kw], True, True,
                                r=[t_QiTp] + [t_KI[g] for g in kg], w=[tb[rb]])
                        ri = h % 2
                        self.ts("dve", Rr[ri][:, 0:kw], B[rb - 1][:, 0:kw], 0.0, wab[:, h:h + 1], ALU.max, ALU.mult,
                                r=[tb[rb], t_w8], w=[t_Rr[ri]])
                        if pend is not None:
                            ph_, pri = pend
                            self.mm(B[4][:, 0:kw], Dg[:, ph_, :], Rr[pri][:, 0:kw], ph_ == 0, False,
                                    r=[t_Dg, t_Rr[pri]], w=[tb[5]], sig=False)
                        pend = (h, ri)
                    ph_, pri = pend
                    self.mm(B[4][:, 0:kw], Dg[:, ph_, :], Rr[pri][:, 0:kw], False, True,
                            r=[t_Dg, t_Rr[pri]], w=[tb[5]])
                    self.cp("dve", SC[:, k0:k0 + kw], B[4][:, 0:kw], r=[tb[5]], w=[t_SC])
                self.red("dve", bs[:, 0:1], SC[:, 0:nk], ALU.max, r=[t_SC], w=[t_bs], absval=True)
                self.tt("dve", SC[:, nk - 256:nk], SC[:, nk - 256:nk], self.C("cmask"), ALU.add,
                        r=[t_SC, self.t_c32], w=[t_SC])
                self.ts("dve", bs[:, 1:2], bs[:, 0:1], 2.002, 2e-20, ALU.mult, ALU.add, r=[t_bs], w=[t_bs])
                hw = hwt
                self.ts("dve", hw[:, 0:NBISECT + 2], hk[:, 0:NBISECT + 2], bs[:, 1:2], None, ALU.mult,
                        r=[t_bs], w=[t_hw])
                self.ts("dve", hw[:, 32:32 + NBISECT + 2], hw[:, 0:NBISECT + 2], -0.5, None, ALU.mult,
                        r=[t_hw], w=[t_hw])
                self.memset("dve", bs[:, 3:4], 0.0, w=[t_bs])

            def proc_own_b(j):
                QTp, t_QTp = QTp2[j % 2], t_QTp2[j % 2]
                nk = (j + 1) * 256
                nch = nk // P
                hw = hwt
                bT = self.bankT
                for k in range(NBISECT):
                    self.act(MB[:, 0:nk], SC[:, 0:nk], AF.Sign, r=[t_SC, t_bs], w=[t_MB, t_cnt],
                             bias=bs[:, 3:4], accum_out=cnt[:, 0:1])
                    self.act(bs[:, 4:5], cnt[:, 0:1], AF.Sign, r=[t_cnt], w=[t_bs], bias=float(nk - 512 + 0.5))
                    self.act(bs[:, 3:4], bs[:, 4:5], AF.Identity, r=[t_bs, t_hw], w=[t_bs],
                             scale=hw[:, 32 + k:33 + k], bias=bs[:, 3:4])
                self.stt("dve", bs[:, 5:6], bs[:, 3:4], hw[:, NBISECT:NBISECT + 1], self.C("neghalf", 0, 1),
                         ALU.add, ALU.mult, r=[t_bs, t_hw, self.t_c32], w=[t_bs])
                self.ts("dve", bs[:, 5:6], bs[:, 5:6], 2.0, None, ALU.mult, r=[t_bs], w=[t_bs])
                self.ts("dve", MB[:, 0:nk], SC[:, 0:nk], bs[:, 5:6], MBIAS, ALU.is_lt, ALU.mult,
                        r=[t_SC, t_bs], w=[t_MB])
                def emit_pv(c):
                    pt = c % 2
                    for h in range(4):
                        self.mm(B[4][:, h * 65:(h + 1) * 65], PT[pt][:, h, :], VA[:, c, h, :],
                                (c == 0 and h == 0), (c == nch - 1 and h == 3),
                                r=[t_PT[pt], t_VA[c]], w=[tb[5]], sig=(h == 3))

                for c in range(nch):
                    stb = 6 + (c % 2)
                    pt = c % 2
                    for h in range(4):
                        self.mm(B[stb - 1][:, h * P:(h + 1) * P], KT[:, h // 2, c * P:(c + 1) * P], QTp[:, h, :],
                                h == 0, False, r=[t_KT[c], t_QTp], w=[tb[stb]], sig=False)
                    self.mm(B[stb - 1][:, 0:512], MB[:, c * P:(c + 1) * P], self.irep_bf[:].rearrange("p a t -> p (a t)"),
                            False, True, r=[t_MB, self.t_cbf], w=[tb[stb]])
                    self.act(PT[pt][:].rearrange("p a t -> p (a t)"), B[stb - 1][:, 0:512], AF.Exp,
                             r=[tb[stb]], w=[t_PT[pt]], scale=0.125)
                    if c >= 1:
                        emit_pv(c - 1)
                emit_pv(nch - 1)
                self.cp("act", osb[:].rearrange("p h e -> p (h e)"), B[4][:, 0:260], r=[tb[5]], w=[t_osb])
                self.s.op("dve", lambda e: e.reciprocal(out=rden[:], in_=osb[:, :, 64]), r=[t_osb], w=[t_osb])
                self.tt("dve", yb[:].rearrange("p (h e) -> p h e", h=4), osb[:, :, 0:64],
                        rden[:].unsqueeze(2).to_broadcast([P, 4, 64]), ALU.mult, r=[t_osb], w=[t_yb])
                for pr in range(2):
                    self.tr(bT[:, pr * P:(pr + 1) * P], yb[:, pr * P:(pr + 1) * P], self.ident_bf[:],
                            r=[t_yb, self.t_cbf], w=[tb[0]], sig=(pr == 1))
                self.cp("act", self.yTn[1][:, :, j * P:(j + 1) * P], bT[:, 0:2 * P].rearrange("p (a t) -> p a t", a=2),
                        r=[tb[0]], w=[self.t_yT[1][j]])

            proc_global(0)
            proc_global(1)
            proc_own(0)
            for j in range(NO):
                proc_own_idx(j)
                if j + 1 < NO:
                    proc_global(2 * j + 2)
                    proc_global(2 * j + 3)
                    proc_own(j + 1)
                proc_own_b(j)

    def phase_mlstm(self):
        nc, s = self.nc, self.s
        NO, NT, S = self.NO, self.NT, self.S
        B, tb = self.bank, self.t_bank
        with ExitStack() as es:
            sb = lambda name, shape, dtp: _alloc(nc, es, name, shape, dtp)
            WC = sb("WC", [P, KC, 1032], BF16); t_WC = Tok("WC")
            wv = self.d_w_in.rearrange("(k p) c -> p k c", p=P)
            for (dst0, src0, n) in ((0, OFF["c_k"], 512), (512, OFF["c_i"], 8), (520, OFF["c_q"], 256),
                                    (776, OFF["c_o"], 256)):
                self.load_w(WC[:, :, dst0:dst0 + n], wv[:, :, src0:src0 + n], t_WC)
            xt = [sb(f"mxt{i}", [P, D], F32) for i in range(2)]; t_xt = [Tok() for _ in range(2)]
            hT = [sb(f"mhT{i}", [P, KC, P], BF16) for i in range(2)]; t_hT = [Tok() for _ in range(2)]
            Cst = sb("Cst", [P, 2, 65], F32); t_C = Tok("Cst")
            Ca = sb("Ca", [P, 2, 65], F32); t_Ca = Tok("Ca")
            Csel = sb("Csel", [P, 2, 65], BF16); t_Csel = Tok("Csel")
            kp = [sb(f"kp{i}", [P, 256], BF16) for i in range(3)]
            va = [sb(f"va{i}", [P, 4, 65], BF16) for i in range(3)]
            gsc = [sb(f"gsc{i}", [P, 24], F32) for i in range(3)]
            gsb = [sb(f"gsb{i}", [P, 16], BF16) for i in range(3)]
            t_g = [Tok() for _ in range(3)]
            qb = sb("qb", [P, 256], BF16); t_qb = Tok()
            qTp = sb("mqTp", [P, 4, P], BF16); t_qTp = Tok()
            kT = sb("mkT", [P, 2, P], BF16); t_kT = Tok()
            Sm = sb("Sm", [P, 4, P], BF16); t_Sm = Tok()
            ep = sb("ep", [P, 16], F32); t_ep = Tok()
            hc = sb("hc", [P, 256], F32); t_hc = Tok()
            so = sb("so", [P, 256], F32); t_so = Tok()
            yc = sb("yc", [P, 256], BF16); t_yc = Tok()
            self.memset("pool", Cst[:], 0.0, w=[t_C])
            self.memset("pool", qTp[:], 0.0, w=[t_qTp])

            def kv_gates(hTap, t_h, sl):
                for kc in range(KC):
                    self.mm(B[0][:, 0:512], hTap[:, kc, :], WC[:, kc, 0:512], kc == 0, kc == KC - 1,
                            r=[t_h, t_WC], w=[tb[1]])
                for kc in range(KC):
                    self.mm(B[1][:, 0:8], hTap[:, kc, :], WC[:, kc, 512:520], kc == 0, kc == KC - 1,
                            r=[t_h, t_WC], w=[tb[2]])
                G = gsc[sl]; tg = t_g[sl]
                self.tt("dve", G[:, 0:4], B[1][:, 4:8], self.C("fb"), ALU.add, r=[tb[2], self.t_c32], w=[tg])
                self.act(G[:, 0:4], G[:, 0:4], AF.Exp, r=[tg], w=[tg], scale=-1.0)
                self.act(G[:, 0:4], G[:, 0:4], AF.Ln, r=[tg], w=[tg], bias=1.0)
                self.mm(B[2][:, 0:4], self.C("maskT"), G[:, 0:4], True, False, r=[tg, self.t_c32], w=[tb[3]], sig=False)
                self.mm(B[2][:, 4:8], self.C("ones"), G[:, 0:4], False, True, r=[tg, self.t_c32], w=[tb[3]])
                self.tt("dve", G[:, 4:8], B[1][:, 0:4], self.C("ib"), ALU.add, r=[tb[2], self.t_c32], w=[tg])
                self.tt("dve", G[:, 4:8], G[:, 4:8], B[2][:, 0:4], ALU.add, r=[tg, tb[3]], w=[tg])
                self.act(G[:, 8:12], G[:, 4:8], AF.Exp, r=[tg], w=[tg])
                self.act(G[:, 12:20], B[2][:, 0:8], AF.Exp, r=[tb[3]], w=[tg], scale=-1.0)
                self.act(kp[sl][:], B[0][:, 0:256], AF.Copy, r=[tb[1]], w=[tg], scale=0.125)
                self.tt("dve", va[sl][:, :, 0:64], B[0][:, 256:512].rearrange("p (h e) -> p h e", h=4),
                        G[:, 8:12].unsqueeze(2).to_broadcast([P, 4, 64]), ALU.mult, r=[tb[1], tg], w=[tg])
                self.cp("dve", va[sl][:, :, 64], G[:, 8:12], r=[tg], w=[tg])

            def state_update(sl):
                G = gsc[sl]; tg = t_g[sl]
                for h in range(4):
                    pr = h // 2
                    self.mm(B[3][:, h * 65:(h + 1) * 65], kp[sl][:, pr * P:(pr + 1) * P], va[sl][:, h, :],
                            True, True, r=[tg], w=[tb[4]], sig=(h == 3))
                for h in range(4):
                    lo = (h % 2) * 64
                    self.tt("dve", Cst[lo:lo + 64, h // 2, :], Cst[lo:lo + 64, h // 2, :],
                            B[3][lo:lo + 64, h * 65:(h + 1) * 65], ALU.add, r=[tb[4], t_C], w=[t_C])
                    self.ts("dve", Cst[lo:lo + 64, h // 2, :], Cst[lo:lo + 64, h // 2, :],
                            G[lo:lo + 64, 16 + h:17 + h], None, ALU.mult, r=[tg, t_C], w=[t_C])

            def proc_global(g, sl):
                self.dma("sp", hT[sl][:].rearrange("p k t -> p (k t)"), self.d_hTs[g], r=(), w=[t_hT[sl]])
                kv_gates(hT[sl], t_hT[sl], sl)
                state_update(sl)

            import os
            KO = int(os.environ.get('KO', '99'))
            def proc_own(j):
                hTo = self.hT_own[:, :, j * P:(j + 1) * P]
                for kc in range(KC):
                    self.mm(B[4][:, 0:512], hTo[:, kc, :], WC[:, kc, 520:1032], kc == 0, kc == KC - 1,
                            r=[self.t_hT[j], t_WC], w=[tb[5]])
                kv_gates(hTo, self.t_hT[j], 2)
                if KO < 1: return
                G = gsc[2]; tg = t_g[2]
                if KO < 2: return
                self.cp("act", qb[:], B[4][:, 0:256], r=[tb[5]], w=[t_qb])
                bT = self.bankT
                for pr in range(2):
                    for kc in range(KC):
                        self.mm(B[3][:, pr * P:(pr + 1) * P], WC[:, kc, 520 + pr * P:520 + (pr + 1) * P], hTo[:, kc, :],
                                kc == 0, kc == KC - 1, r=[self.t_hT[j], t_WC], w=[tb[4]], sig=False)
                for pr in range(2):
                    for kc in range(KC):
                        self.mm(B[3][:, (2 + pr) * P:(3 + pr) * P], WC[:, kc, pr * P:(pr + 1) * P], hTo[:, kc, :],
                                kc == 0, kc == KC - 1, r=[self.t_hT[j], t_WC], w=[tb[4]],
                                sig=(pr == 1 and kc == KC - 1))
                if KO < 3: return
                for h in range(4):
                    lo = (h % 2) * 64
                    self.cp("act", qTp[lo:lo + 64, h, :], B[3][lo:lo + 64, (h // 2) * P:(h // 2 + 1) * P],
                            r=[tb[4]], w=[t_qTp])
                self.act(kT[:].rearrange("p a t -> p (a t)"), B[3][:, 2 * P:4 * P], AF.Copy, r=[tb[4]], w=[t_kT], scale=0.125)
                if KO < 4: return
                for h in range(4):
                    self.mm(B[5][:, h * P:(h + 1) * P], kT[:, h // 2, :], qTp[:, h, :], True, True,
                            r=[t_kT, t_qTp], w=[tb[6]], sig=(h == 3))
                if KO < 5: return
                self.tt("dve", Sm[:], B[5][:, 0:512].rearrange("p (h t) -> p h t", h=4), self.maskT_bf[:], ALU.mult,
                        r=[tb[6], self.t_cbf], w=[t_Sm])
                if KO < 6: return
                for h in range(4):
                    self.mm(B[6][:, h * 65:(h + 1) * 65], qTp[:, h, :], Csel[:, h // 2, :], h == 0, False,
                            r=[t_qTp, t_Csel], w=[tb[7]], sig=False)
                    self.mm(B[6][:, h * 65:(h + 1) * 65], Sm[:, h, :], va[2][:, h, :], False, True,
                            r=[t_Sm, tg], w=[tb[7]], sig=(h == 3))
                if KO < 7: return
                acc = B[6][:, 0:260].rearrange("p (h e) -> p h e", h=4)
                self.tt("dve", ep[:, 0:4], acc[:, :, 64], G[:, 12:16], ALU.mult, r=[tb[7], tg], w=[t_ep])
                self.act(ep[:, 0:4], ep[:, 0:4], AF.Abs, r=[t_ep], w=[t_ep])
                self.ts("dve", ep[:, 0:4], ep[:, 0:4], 1.0, None, ALU.max, r=[t_ep], w=[t_ep])
                self.s.op("dve", lambda e: e.reciprocal(out=ep[:, 4:8], in_=ep[:, 0:4]), r=[t_ep], w=[t_ep])
                self.tt("dve", ep[:, 4:8], ep[:, 4:8], G[:, 12:16], ALU.mult, r=[t_ep, tg], w=[t_ep])
                self.tt("dve", hc[:].rearrange("p (h e) -> p h e", h=4), acc[:, :, 0:64],
                        ep[:, 4:8].unsqueeze(2).to_broadcast([P, 4, 64]), ALU.mult, r=[tb[7], t_ep], w=[t_hc])
                if KO < 8: return
                self.headnorm(hc[:], 4, self.C("mn"), hc[:], r=[t_hc], w=[t_hc])
                if KO < 9: return
                self.act(so[:], B[4][:, 256:512], AF.Exp, r=[tb[5]], w=[t_so], scale=-1.0)
                self.ts("pool", so[:], so[:], 1.0, None, ALU.add, r=[t_so], w=[t_so])
                self.s.op("dve", lambda e: e.reciprocal(out=so[:], in_=so[:]), r=[t_so], w=[t_so])
                self.tt("dve", yc[:], hc[:], so[:], ALU.mult, r=[t_hc, t_so], w=[t_yc])
                if KO < 10: return
                for pr in range(2):
                    self.tr(bT[:, pr * P:(pr + 1) * P], yc[:, pr * P:(pr + 1) * P], self.ident_bf[:],
                            r=[t_yc, self.t_cbf], w=[tb[0]], sig=(pr == 1))
                self.cp("act", self.yTn[2][:, :, j * P:(j + 1) * P], bT[:, 0:2 * P].rearrange("p (a t) -> p a t", a=2),
                        r=[tb[0]], w=[self.t_yT[2][j]])

            pf = self.C("pflag")
            import os
            km = os.environ.get("KM", "gso")
            for j in range(NO):
                self.cp("pool", Ca[:], Cst[:], r=[t_C], w=[t_Ca])
                if "g" in km:
                    proc_global(2 * j, 0)
                if "s" in km:
                    self.tt("dve", hc[:, 0:130].rearrange("p (a e) -> p a e", a=2), Cst[:], Ca[:], ALU.subtract,
                            r=[t_C, t_Ca], w=[t_hc])
                    self.stt("dve", Csel[:], hc[:, 0:130].rearrange("p (a e) -> p a e", a=2), pf, Ca[:], ALU.mult, ALU.add,
                             r=[t_hc, t_Ca, self.t_c32], w=[t_Csel])
                if "g" in km:
                    proc_global(2 * j + 1, 1)
                if "o" in km:
                    proc_own(j)

    def phase_sgu(self):
        nc = self.nc
        NO = self.NO
        B, tb = self.bank, self.t_bank
        with ExitStack() as es:
            sb = lambda name, shape, dtp: _alloc(nc, es, name, shape, dtp)
            WS = sb("WS", [P, KC, 512], BF16); t_WS = Tok()
            wv = self.d_w_in.rearrange("(k p) c -> p k c", p=P)
            self.load_w(WS[:], wv[:, :, 0:512], t_WS)
            WmT = sb("WmT", [P, 4, P], BF16); t_Wm = Tok()
            wT32 = sb("wT32", [P, 512], F32); t_wT32 = Tok()
            o_, w_ = COFF["sgu_wT"]
            self.dma("sp", wT32[:], self.cur_consts[:, o_:o_ + w_], r=(), w=[t_wT32])
            self.tt("dve", WmT[:], wT32[:].rearrange("p (g t) -> p g t", g=4),
                    self.C("maskT").unsqueeze(1).to_broadcast([P, 4, P]), ALU.mult, r=[self.t_c32, t_wT32], w=[t_Wm])
            x2 = sb("sx2", [P, 512], F32); xh = sb("sxh", [P, 512], F32); zz = sb("szz", [P, 512], F32)
            ge = sb("sge", [P, 512], F32); t_s = Tok()
            vn = sb("svn", [P, 256], BF16); t_vn = Tok()
            ssq = sb("sssq", [P, 2], F32)
            ya = sb("sya", [P, 256], BF16); t_ya = Tok()
            for j in range(NO):
                hTo = self.hT_own[:, :, j * P:(j + 1) * P]
                for kc in range(KC):
                    self.mm(B[0][:, 0:512], hTo[:, kc, :], WS[:, kc, :], kc == 0, kc == KC - 1,
                            r=[self.t_hT[j], t_WS], w=[tb[1]])
                xp = B[0][:, 0:512]
                self.act(x2[:], xp, AF.Square, r=[tb[1]], w=[t_s])
                self.act(xh[:], xp, AF.Copy, r=[tb[1]], w=[t_s], scale=0.5)
                self.ts("pool", x2[:], x2[:], 0.044715, 1.0, ALU.mult, ALU.add, r=[t_s], w=[t_s])
                self.tt("dve", zz[:], x2[:], xp, ALU.mult, r=[t_s, tb[1]], w=[t_s])
                self.act(zz[:], zz[:], AF.Tanh, r=[t_s], w=[t_s], scale=0.7978845608028654)
                self.stt("dve", ge[:], zz[:], 1.0, xh[:], ALU.add, ALU.mult, r=[t_s], w=[t_s])
                self.act(x2[:, 0:256], ge[:, 256:512], AF.Square, r=[t_s], w=[t_s], accum_out=ssq[:, 0:1])
                self.rstd(ssq[:, 0:1], 1, 256, EPS, ssq[:, 1:2], r=[t_s], w=[t_s])
                self.stt("dve", vn[:], ge[:, 256:512], ssq[:, 1:2], self.C("sgu_norm"), ALU.mult, ALU.mult,
                         r=[t_s, self.t_c32], w=[t_vn])
                for g in range(4):
                    self.mm(B[1][:, g * 64:(g + 1) * 64], WmT[:, g, :], vn[:, g * 64:(g + 1) * 64], True, True,
                            r=[t_Wm, t_vn], w=[tb[2]], sig=(g == 3))
                for g in range(4):
                    self.stt("dve", ya[:, g * 64:(g + 1) * 64], B[1][:, g * 64:(g + 1) * 64],
                             self.C("sgu_b", g, g + 1), ge[:, g * 64:(g + 1) * 64], ALU.add, ALU.mult,
                             r=[tb[2], t_s, self.t_c32], w=[t_ya])
                bT = self.bankT
                for pr in range(2):
                    self.tr(bT[:, pr * P:(pr + 1) * P], ya[:, pr * P:(pr + 1) * P], self.ident_bf[:],
                            r=[t_ya, self.t_cbf], w=[tb[0]], sig=(pr == 1))
                self.cp("act", self.yTn[0][:, :, j * P:(j + 1) * P], bT[:, 0:2 * P].rearrange("p (a t) -> p a t", a=2),
                        r=[tb[0]], w=[self.t_yT[0][j]])

    def phase_conv(self):
        nc = self.nc
        NO = self.NO
        B, tb = self.bank, self.t_bank
        with ExitStack() as es:
            sb = lambda name, shape, dtp: _alloc(nc, es, name, shape, dtp)
            WD = sb("WD", [P, KC, 768], BF16); t_WD = Tok()
            wv = self.d_w_in.rearrange("(k p) c -> p k c", p=P)
            self.load_w(WD[:], wv[:, :, OFF["d_b"]:OFF["d_b"] + 768], t_WD)
            hz = sb("hz", [P, KC, 2], BF16); t_hz = Tok()
            hd = sb("hd", [P, KC, 2], F32); hsel = sb("hsel", [P, KC, 2], BF16); t_hs = Tok()
            xs = sb("cxs", [P, 256], F32); zf = sb("czf", [P, 256], F32); t_z = Tok()
            z3 = sb("cz3", [P, 3, 256], F32); t_z3 = Tok()
            zp = sb("czp", [P, 2, 256], F32); t_zp = Tok()
            zpx = sb("czpx", [P, 256], F32); zpv = sb("czpv", [P, 256], F32); t_zq = Tok()
            ysb = sb("cys", [P, 256], F32); t_ys = Tok()
            yd = sb("cyd", [P, 256], BF16); t_yd = Tok()
            self.memset("pool", hz[:], 0.0, w=[t_hz])
            self.memset("pool", zp[:], 0.0, w=[t_zp])
            pf = self.C("pflag")
            cw_t = sb("cw_t", [P, 768], F32); t_cw = Tok()
            o_, w_ = COFF["conv"]
            self.dma("sp", cw_t[:], self.cur_consts[:, o_:o_ + w_], r=(), w=[t_cw])
            cw = cw_t
            for j in range(NO):
                hTo = self.hT_own[:, :, j * P:(j + 1) * P]
                if j == 0:
                    h0, t_h0 = hz[:], t_hz
                else:
                    h0, t_h0 = self.hl2[:, 2 * j - 1, :, :], self.t_hl2[2 * j - 1]
                h1, t_h1 = self.hl2[:, 2 * j, :, :], self.t_hl2[2 * j]
                self.tt("dve", hd[:], h1, h0, ALU.subtract, r=[t_h0, t_h1], w=[t_hs])
                self.stt("dve", hsel[:], hd[:], pf, h0, ALU.mult, ALU.add, r=[t_hs, t_h0, self.t_c32], w=[t_hs])
                for kc in range(KC):
                    self.mm(B[0][:, 0:512], hTo[:, kc, :], WD[:, kc, 0:512], kc == 0, kc == KC - 1,
                            r=[self.t_hT[j], t_WD], w=[tb[1]])
                for kc in range(KC):
                    self.mm(B[1][:, 0:256], hTo[:, kc, :], WD[:, kc, 512:768], kc == 0, kc == KC - 1,
                            r=[self.t_hT[j], t_WD], w=[tb[2]])
                for kc in range(KC):
                    self.mm(B[2][0:2, 0:512], hsel[:, kc, :], WD[:, kc, 256:768], kc == 0, kc == KC - 1,
                            r=[t_hs, t_WD], w=[tb[3]])
                self.cp("act", xs[:], B[1][:, 0:256], r=[tb[2]], w=[t_z])
                self.tt("dve", zf[:], B[0][:, 256:512], xs[:], ALU.mult, r=[tb[1], t_z], w=[t_z])
                for k in range(3):
                    self.tt("pool", z3[:, k, :], zf[:], cw[:, k * 256:(k + 1) * 256], ALU.mult,
                            r=[t_z, t_cw], w=[t_z3])
                self.cp("act", zpx[0:2, :], B[2][0:2, 256:512], r=[tb[3]], w=[t_zq])
                self.tt("dve", zpv[0:2, :], B[2][0:2, 0:256], zpx[0:2, :], ALU.mult, r=[tb[3], t_zq], w=[t_zq])
                self.tt("dve", zp[0:2, 0, :], zpv[0:2, :], cw[0:2, 0:256], ALU.mult, r=[t_zq, t_cw], w=[t_zp])
                self.tt("dve", zp[0:2, 1, :], zpv[0:2, :], cw[0:2, 256:512], ALU.mult, r=[t_zq, t_cw], w=[t_zp])
                yb_ = B[3][:, 0:256]
                self.mm(yb_, self.C("ident"), z3[:, 2, :], True, False, r=[t_z3, self.t_c32], w=[tb[4]], sig=False)
                self.mm(yb_, self.C("sh1"), z3[:, 1, :], False, False, r=[t_z3, self.t_c32], w=[tb[4]], sig=False)
                self.mm(yb_, self.C("sh2"), z3[:, 0, :], False, False, r=[t_z3, self.t_c32], w=[tb[4]], sig=False)
                self.mm(yb_, self.C("ba"), zp[:, 1, :], False, False, r=[t_zp, self.t_c32], w=[tb[4]], sig=False)
                self.mm(yb_, self.C("bb"), zp[:, 0, :], False, True, r=[t_zp, self.t_c32], w=[tb[4]])
                self.cp("act", ysb[:], yb_, r=[tb[4]], w=[t_ys])
                self.tt("dve", yd[:], B[0][:, 0:256], ysb[:], ALU.mult, r=[tb[1], t_ys], w=[t_yd])
                bT = self.bankT
                for pr in range(2):
                    self.tr(bT[:, pr * P:(pr + 1) * P], yd[:, pr * P:(pr + 1) * P], self.ident_bf[:],
                            r=[t_yd, self.t_cbf], w=[tb[0]], sig=(pr == 1))
                self.cp("act", self.yTn[3][:, :, j * P:(j + 1) * P], bT[:, 0:2 * P].rearrange("p (a t) -> p a t", a=2),
                        r=[tb[0]], w=[self.t_yT[3][j]])

    def phase_merge(self):
        nc, s = self.nc, self.s
        NO = self.NO
        B, tb = self.bank, self.t_bank
        TG = 4
        halves = [list(range(0, NO // 2)), list(range(NO // 2, NO))] if NO >= 8 else [list(range(NO))]
        wv = self.d_w_in.rearrange("(k p) c -> p k c", p=P)
        with ExitStack() as es0:
            HT = len(halves[0])
            mT = _alloc(nc, es0, "mT", [P, KC, HT * P], BF16)
            t_mT = [Tok() for _ in range(HT)]
            for tiles in halves:
                j0 = tiles[0]
                groups = [tiles[i:i + TG] for i in range(0, len(tiles), TG)]
                with ExitStack() as es:
                    sb = lambda name, shape, dtp: _alloc(nc, es, name, shape, dtp)
                    Wg = [sb(f"Wg{i}", [P, KC, 4, P], BF16) for i in range(2)]; t_Wg = [Tok() for _ in range(2)]
                    Wb = [sb(f"Wb{i}", [P, 2, 4, P], BF16) for i in range(2)]; t_Wb = [Tok() for _ in range(2)]
                    th = [sb(f"th{i}", [P, 512], F32) for i in range(2)]; t_th = [Tok() for _ in range(2)]
                    acc = sb("macc", [P, 512], F32); tmp = sb("mtmp", [P, 512], F32); t_acc = Tok(); t_tmp = Tok()
                    for cc in range(KC):
                        ws = cc % 2
                        for n in range(4):
                            c0 = OFF["g"] + n * D + cc * P
                            self.load_w(Wg[ws][:, :, n, :], wv[:, :, c0:c0 + P], t_Wg[ws])
                            self.load_w(Wb[ws][:, :, n, :],
                                        self.d_w_branch[n].rearrange("(f p) c -> p f c", p=P)[:, :, cc * P:(cc + 1) * P],
                                        t_Wb[ws])
                        for grp in groups:
                            T = len(grp) * P
                            c_lo = grp[0] * P
                            hts = [self.t_hT[j] for j in grp]
                            for n in range(4):
                                gb = n % 2
                                for kc in range(KC):
                                    self.mm(B[gb][:, 0:T], Wg[ws][:, kc, n, :], self.hT_own[:, kc, c_lo:c_lo + T],
                                            kc == 0, kc == KC - 1, r=[t_Wg[ws]] + hts, w=[tb[1 + gb]])
                                self.act(th[gb][:, 0:T], B[gb][:, 0:T], AF.Tanh, r=[tb[1 + gb]], w=[t_th[gb]], scale=0.5)
                                for f in range(2):
                                    self.mm(B[2 + gb][:, 0:T], Wb[ws][:, f, n, :], self.yTn[n][:, f, c_lo:c_lo + T],
                                            f == 0, f == 1, r=[t_Wb[ws]] + [self.t_yT[n][j] for j in grp],
                                            w=[tb[3 + gb]])
                                if n == 0:
                                    self.stt("dve", acc[:, 0:T], th[gb][:, 0:T], 1.0, B[2 + gb][:, 0:T], ALU.add, ALU.mult,
                                             r=[t_th[gb], tb[3 + gb]], w=[t_acc])
                                else:
                                    self.stt("dve", tmp[:, 0:T], th[gb][:, 0:T], 1.0, B[2 + gb][:, 0:T], ALU.add, ALU.mult,
                                             r=[t_th[gb], tb[3 + gb]], w=[t_tmp])
                                    self.tt("dve", acc[:, 0:T], acc[:, 0:T], tmp[:, 0:T], ALU.add,
                                            r=[t_acc, t_tmp], w=[t_acc])
                            m_lo = (grp[0] - j0) * P
                            self.act(mT[:, cc, m_lo:m_lo + T], acc[:, 0:T], AF.Copy, r=[t_acc],
                                     w=[t_mT[j - j0] for j in grp], scale=0.5)
                s.barrier()
                with ExitStack() as es:
                    Wo = _alloc(nc, es, "Wo", [P, KC, D], BF16); t_Wo = Tok()
                    self.load_w(Wo[:], self.d_w_out.rearrange("(k p) c -> p k c", p=P), t_Wo)
                    for j in tiles:
                        for half in range(2):
                            ob = 4 + half
                            for kc in range(KC):
                                self.mm(B[ob][:, 0:512], mT[:, kc, (j - j0) * P:(j - j0 + 1) * P],
                                        Wo[:, kc, half * 512:(half + 1) * 512], kc == 0, kc == KC - 1,
                                        r=[t_mT[j - j0], t_Wo], w=[tb[1 + ob]])
                            xs_ = self.x_own[:, j, half * 512:(half + 1) * 512]
                            self.tt("dve", xs_, xs_, B[ob][:, 0:512], ALU.add, r=[tb[1 + ob], self.t_x[j]], w=[self.t_x[j]])
                s.barrier()

    def phase_ffn(self):
        nc = self.nc
        NO = self.NO
        B, tb = self.bank, self.t_bank
        TG = 4
        groups = [list(range(i, min(i + TG, NO))) for i in range(0, NO, TG)]
        with ExitStack() as es:
            sb = lambda name, shape, dtp: _alloc(nc, es, name, shape, dtp)
            Wu = [sb(f"Wu{i}", [P, KC, 512], BF16) for i in range(2)]; t_Wu = [Tok() for _ in range(2)]
            Wd = [sb(f"Wd{i}", [P, 4, D], BF16) for i in range(2)]; t_Wd = [Tok() for _ in range(2)]
            rr = [sb(f"frr{i}", [P, 512], F32) for i in range(2)]; t_rr = [Tok() for _ in range(2)]
            uT = [sb(f"fuT{i}", [P, 4, 512], BF16) for i in range(2)]; t_uT = [Tok() for _ in range(2)]
            ob_i = 0
            gi = 0
            for slab in range(DFF // 512):
                ws = slab % 2
                self.load_w(Wu[ws][:], self.d_w_up.rearrange("(k p) c -> p k c", p=P)[:, :, slab * 512:(slab + 1) * 512],
                            t_Wu[ws])
                self.load_w(Wd[ws][:], self.d_w_down[slab * 512:(slab + 1) * 512, :].rearrange("(f p) c -> p f c", p=P),
                            t_Wd[ws])
                for grp in groups:
                    T = len(grp) * P
                    c_lo = grp[0] * P
                    hts = [self.t_hT[j] for j in grp]
                    us = gi % 2
                    gi += 1
                    for fc in range(4):
                        ub = fc % 2
                        for kc in range(KC):
                            self.mm(B[ub][:, 0:T], Wu[ws][:, kc, fc * P:(fc + 1) * P], self.hT_own[:, kc, c_lo:c_lo + T],
                                    kc == 0, kc == KC - 1, r=[t_Wu[ws]] + hts, w=[tb[1 + ub]])
                        self.act(rr[ub][:, 0:T], B[ub][:, 0:T], AF.Relu, r=[tb[1 + ub]], w=[t_rr[ub]])
                        self.act(uT[us][:, fc, 0:T], rr[ub][:, 0:T], AF.Square, r=[t_rr[ub]], w=[t_uT[us]])
                    for ti, j in enumerate(grp):
                        for half in range(2):
                            ob = 2 + (ob_i % 4)
                            ob_i += 1
                            for fc in range(4):
                                self.mm(B[ob][:, 0:512], uT[us][:, fc, ti * P:(ti + 1) * P],
                                        Wd[ws][:, fc, half * 512:(half + 1) * 512], fc == 0, fc == 3,
                                        r=[t_uT[us], t_Wd[ws]], w=[tb[1 + ob]])
                            xs_ = self.x_own[:, j, half * 512:(half + 1) * 512]
                            self.tt("dve", xs_, xs_, B[ob][:, 0:512], ALU.add, r=[tb[1 + ob], self.t_x[j]], w=[self.t_x[j]])


def _rope_table(pos):
    half = 32
    inv = np.float32(10000.0) ** (-np.arange(half, dtype=np.float32) * np.float32(2.0) / np.float32(64))
    ang = pos.astype(np.float32)[:, None] * inv[None, :].astype(np.float32)
    return np.concatenate([np.cos(ang), np.sin(ang)], axis=1).astype(np.float32)


def _consts(params, l, parity):
    c = np.zeros((P, NCONST), np.float32)

    def put(name, arr):
        o, w = COFF[name]
        c[:, o:o + w] = np.asarray(arr, np.float32).reshape(P, w) if np.ndim(arr) == 2 else np.asarray(arr, np.float32)

    idx = np.arange(P)
    put("ident", np.eye(P, dtype=np.float32))
    put("maskT", (idx[:, None] <= idx[None, :]).astype(np.float32))
    put("ones", np.ones((P, P), np.float32))
    put("sh1", (idx[:, None] == idx[None, :] - 1).astype(np.float32))
    put("sh2", (idx[:, None] == idx[None, :] - 2).astype(np.float32))
    ba = np.zeros((P, P), np.float32); ba[1, 0] = 1.0
    bb = np.zeros((P, P), np.float32); bb[0, 0] = 1.0; bb[1, 1] = 1.0
    put("ba", ba)
    put("bb", bb)
    put("lnmixT", params["ln_mix"][l].reshape(KC, P).T)
    put("lnmlpT", params["ln_mlp"][l].reshape(KC, P).T)
    rep = lambda v: np.broadcast_to(np.asarray(v, np.float32).reshape(1, -1), (P, np.size(v)))
    put("sgu_norm", rep(params["sgu_norm"][l]))
    put("sgu_b", params["sgu_b"][l].T)
    put("qn", rep(np.tile(params["q_norm"][l], 4)))
    put("kn", rep(np.tile(params["k_norm"][l], 4)))
    put("kidx", rep(params["kidx_norm"][l]))
    put("ib", rep(params["mlstm_i_bias"][l]))
    put("fb", rep(params["mlstm_f_bias"][l]))
    put("mn", rep(params["mlstm_norm"][l]))
    put("conv", rep(params["conv_w"][l].reshape(-1)))
    cm = np.zeros((P, 256), np.float32)
    tri = np.where(idx[None, :] <= idx[:, None], 0.0, NEG).astype(np.float32)
    if parity == 0:
        cm[:, 0:128] = tri
        cm[:, 128:256] = NEG
    else:
        cm[:, 128:256] = tri
    put("cmask", cm)
    put("pflag", np.full((P, 1), float(parity), np.float32))
    put("neghalf", np.full((P, 8), -0.5, np.float32))
    put("sgu_wT", np.transpose(params["sgu_w"][l], (2, 0, 1)).reshape(P, 4 * P))
    return c


_NC_CACHE = {}


def _get_nc(S, dbg=None):
    key = (S, tuple(dbg) if dbg else None)
    if key not in _NC_CACHE:
        _NC_CACHE[key] = Builder(S, dbg).build()
    return _NC_CACHE[key]


def run_model(x, params, dbg=None):
    Bn, S, _ = x.shape
    NT = S // P
    NO = NT // 2
    nc = Builder(S, dbg).build()
    rope_all = np.ascontiguousarray(_rope_table(np.arange(S)).reshape(NT, P, 64))
    c00, c01 = _consts(params, 0, 0), _consts(params, 0, 1)
    wts = {k: np.ascontiguousarray(params[k]) for k in ("w_in", "w_branch", "w_out", "w_up", "w_down")}
    in_maps = []
    for core in range(8):
        b, par = core // 2, core % 2
        own = [2 * j + par for j in range(NO)]
        in_maps.append(dict(
            x=np.ascontiguousarray(x[b]), consts3=np.ascontiguousarray(np.stack([c00, c01, _consts(params, 1, par)])),
            rope_all=rope_all, rope_own=np.ascontiguousarray(rope_all[own]), **wts))
    res = run_bass_kernel_spmd(nc, in_maps, core_ids=list(range(8)))
    out = np.empty_like(x)
    for core in range(8):
        b, par = core // 2, core % 2
        y = np.asarray(res.results[core]["y"]).reshape(NO, P, D)
        ov = out[b].reshape(NT, P, D)
        for j in range(NO):
            ov[2 * j + par] = y[j]
    return out, res


def kernel(**inputs):
    x = np.asarray(inputs["x"], np.float32)
    params = {k: np.asarray(v, np.float32) for k, v in inputs.items() if k != "x"}
    out, _ = run_model(x, params)
    return out
```

```python
import numpy as np
from contextlib import ExitStack
import concourse.bass as bass
import concourse.mybir as mybir
from concourse.bass_utils import run_bass_kernel_spmd

F32 = mybir.dt.float32
BF16 = mybir.dt.bfloat16
AF = mybir.ActivationFunctionType
ALU = mybir.AluOpType
AX = mybir.AxisListType

P = 128
D = 1024
KC = 8
DFF = 4096
IN_W = 7760
EPS = 1e-6
OFF = dict(a_u=0, a_v=256, b_q=512, b_k=768, b_v=1024, b_qi=1280, b_ki=1792, b_wi=1856,
           c_q=1864, c_k=2120, c_v=2376, c_o=2632, c_i=2888, c_f=2892,
           d_b=2896, d_c=3152, d_x=3408, g=3664)
NBISECT = 12
NEG = -1.0e30
MBIAS = -30000.0

CONST_SPEC = [
    ("ident", 128), ("maskT", 128), ("ones", 128), ("sh1", 128), ("sh2", 128), ("ba", 128), ("bb", 128),
    ("lnmixT", 8), ("lnmlpT", 8), ("sgu_norm", 256), ("sgu_b", 4), ("qn", 256), ("kn", 256),
    ("kidx", 64), ("ib", 4), ("fb", 4), ("mn", 256), ("cmask", 256), ("pflag", 1),
    ("neghalf", 8), ("conv", 768), ("sgu_wT", 512),
]
COFF = {}
_o = 0
for _n, _w in CONST_SPEC:
    COFF[_n] = (_o, _w)
    _o += _w
NCONST = _o
NC_TOP = COFF["conv"][0]


def _nbytes(shape, dtp):
    n = 1
    for d in shape[1:]:
        n *= d
    return n * (4 if dtp == F32 else 2)


def _alloc(nc, es, name, shape, dtp):
    _alloc.n += 1
    name = f"{name}_{_alloc.n}"
    t = es.enter_context(nc.sbuf_tensor(name, shape, dtp))
    rem = _nbytes(shape, dtp) % 32
    if rem:
        es.enter_context(nc.sbuf_tensor(name + "_pad", [P, (32 - rem) // 2], BF16))
    return t


_alloc.n = 0


def _unused():
    return None

class Dep:
    __slots__ = ("sem", "val", "eng", "key")

    def __init__(self, sem, val, eng, key):
        self.sem, self.val, self.eng, self.key = sem, val, eng, key


class Tok:
    __slots__ = ("w", "r", "name")

    def __init__(self, name=""):
        self.w = None
        self.r = {}
        self.name = name


class Eng:
    def __init__(self, name, h, sem):
        self.name, self.h, self.sem = name, h, sem
        self.count = 0
        self.waited = {}
        self.n_inst = 0


class Sched:
    NDMA = 8

    def __init__(self, nc):
        self.nc = nc
        self.E = {}
        for name, h in (("pe", nc.tensor), ("act", nc.scalar), ("dve", nc.vector),
                        ("pool", nc.gpsimd), ("sp", nc.sync)):
            self.E[name] = Eng(name, h, nc.alloc_semaphore("s_" + name))
        self.dq = {}
        for q in ("sp", "pool", "act"):
            self.dq[q] = dict(n=0, sems=[nc.alloc_semaphore(f"d_{q}{i}") for i in range(self.NDMA)])

    def _wait(self, E, d):
        if E.waited.get(d.key, 0) >= d.val:
            return
        E.h.wait_ge(d.sem, d.val)
        E.n_inst += 1
        E.waited[d.key] = d.val

    def op(self, eng, fn, r=(), w=(), dma=False, sig=True):
        E = self.E[eng]
        deps = []
        for t in r:
            if t.w is not None:
                deps.append(t.w)
        for t in w:
            if t.w is not None:
                deps.append(t.w)
            deps.extend(t.r.values())
        if dma:
            q = self.dq[eng]
            i = q["n"]
            slot = i % self.NDMA
            sem = q["sems"][slot]
            key = f"d_{eng}{slot}"
            if i >= self.NDMA:
                deps.append(Dep(sem, 16 * (i // self.NDMA), None, key))
            comp = Dep(sem, 16 * (i // self.NDMA + 1), None, key)
            q["n"] += 1
        else:
            if sig:
                E.count += 1
                comp = Dep(E.sem, E.count, eng, "e_" + eng)
            else:
                comp = Dep(E.sem, E.count + 1, eng, "e_" + eng)
        for d in deps:
            if d.eng == eng and not dma and eng == "pe":
                continue
            self._wait(E, d)
        inst = fn(E.h)
        E.n_inst += 1
        if dma:
            inst.then_inc(comp.sem, 16)
        elif sig:
            inst.then_inc(comp.sem, 1)
        for t in r:
            old = t.r.get(comp.key)
            if old is None or old.val < comp.val:
                t.r[comp.key] = comp
        for t in w:
            t.w = comp
            t.r = {}
        return comp

    def barrier(self):
        deps = [Dep(F.sem, F.count, F.name, "e_" + F.name) for F in self.E.values() if F.count > 0]
        for qn, q in self.dq.items():
            for slot in range(min(q["n"], self.NDMA)):
                last = ((q["n"] - 1 - slot) // self.NDMA) + 1
                deps.append(Dep(q["sems"][slot], 16 * last, None, f"d_{qn}{slot}"))
        for E in self.E.values():
            for d in deps:
                if d.eng == E.name:
                    continue
                self._wait(E, d)

    def final_wait(self, eng, deps):
        E = self.E[eng]
        for d in deps:
            self._wait(E, d)


class _Idx:
    def __init__(self, fn):
        self.fn = fn

    def __getitem__(self, j):
        return self.fn(j)


class Builder:
    def __init__(self, S, dbg=None):
        self.S = S
        self.NT = S // P
        self.NO = self.NT // 2
        self.dbg = dbg or ()
        nc = bass.Bass("TRN2", target_bir_lowering=False)
        self.nc = nc
        self.s = Sched(nc)
        NO, NT = self.NO, self.NT
        dt = nc.dram_tensor
        self.d_x = dt("x", [S, D], F32, kind="ExternalInput").ap()
        self.d_consts3 = dt("consts3", [3, P, NCONST], F32, kind="ExternalInput").ap()
        self.d_rope_all = dt("rope_all", [NT, P, 64], F32, kind="ExternalInput").ap()
        self.d_rope_own_in = dt("rope_own", [NO, P, 64], F32, kind="ExternalInput").ap()
        self.D_w_in = dt("w_in", [2, D, IN_W], F32, kind="ExternalInput").ap()
        self.D_w_branch = dt("w_branch", [2, 4, 256, D], F32, kind="ExternalInput").ap()
        self.D_w_out = dt("w_out", [2, D, D], F32, kind="ExternalInput").ap()
        self.D_w_up = dt("w_up", [2, D, DFF], F32, kind="ExternalInput").ap()
        self.D_w_down = dt("w_down", [2, DFF, D], F32, kind="ExternalInput").ap()
        self.d_y = dt("y", [NO * P, D], F32, kind="ExternalOutput").ap()
        self.d_x1 = dt("x1_scratch", [S, D], F32).ap()
        self.d_hTs = dt("hT_scratch", [NT, P, KC * P], BF16).ap()
        self.d_hTo = dt("hTo_scratch", [NO, P, KC * P], BF16).ap()
        self.d_dbg = {}
        for name, shape in self.dbg:
            self.d_dbg[name] = dt("dbg_" + name, list(shape), F32, kind="ExternalOutput").ap()

    def mm(self, out, lhsT, rhs, start, stop, r, w, sig=None):
        sig = stop if sig is None else sig
        return self.s.op("pe", lambda e: e.matmul(out, lhsT, rhs, start=start, stop=stop,
                                                  skip_group_check=True), r=r, w=w, sig=sig)

    def tr(self, out, in_, ident, r, w, sig=True):
        return self.s.op("pe", lambda e: e.transpose(out, in_, ident), r=r, w=w, sig=sig)

    def act(self, out, in_, func, r, w, **kw):
        return self.s.op("act", lambda e: e.activation(out=out, in_=in_, func=func, **kw), r=r, w=w)

    def tt(self, eng, out, in0, in1, op, r, w):
        return self.s.op(eng, lambda e: e.tensor_tensor(out=out, in0=in0, in1=in1, op=op), r=r, w=w)

    def ts(self, eng, out, in0, s1, s2, op0, op1=None, r=(), w=(), accum_out=None):
        def f(e):
            kw = {}
            if op1 is not None:
                kw["op1"] = op1
            if accum_out is not None:
                kw["accum_out"] = accum_out
            return e.tensor_scalar(out=out, in0=in0, scalar1=s1, scalar2=s2, op0=op0, **kw)
        return self.s.op(eng, f, r=r, w=w)

    def stt(self, eng, out, in0, scalar, in1, op0, op1, r, w):
        return self.s.op(eng, lambda e: e.scalar_tensor_tensor(out=out, in0=in0, scalar=scalar, in1=in1,
                                                               op0=op0, op1=op1), r=r, w=w)

    def cp(self, eng, out, in_, r, w):
        if eng == "act":
            return self.s.op("act", lambda e: e.copy(out=out, in_=in_), r=r, w=w)
        return self.s.op(eng, lambda e: e.tensor_copy(out=out, in_=in_), r=r, w=w)

    def red(self, eng, out, in_, op, r, w, absval=False):
        return self.s.op(eng, lambda e: e.tensor_reduce(out=out, in_=in_, axis=AX.X, op=op,
                                                        apply_absolute_value=absval), r=r, w=w)

    def memset(self, eng, ap, val, w):
        return self.s.op(eng, lambda e: e.memset(ap, val), r=(), w=w)

    def dma(self, q, out, in_, r, w):
        h = {"sp": self.nc.sync, "pool": self.nc.gpsimd, "act": self.nc.scalar}[q]
        return self.s.op(q, lambda e: h.dma_start(out=out, in_=in_), r=r, w=w, dma=True)

    def C(self, name, a=0, b=None):
        o, wd = COFF[name]
        b = wd if b is None else b
        return self.c32[:, o + a:o + b]

    def rstd(self, ss, n, width, eps, out, r, w):
        tv = self.t_rs
        self.ts("pool", self.rs_tmp[:, 0:n], ss, 1.0 / width, eps, ALU.mult, ALU.add, r=r, w=[tv])
        self.tt("pool", out, self.rs_tmp[:, 0:n], self.C("neghalf", 0, n), ALU.pow, r=[tv, self.t_c32], w=w)

    def norm_transpose(self, x_ap, t_x, gainT, hT, t_hT, slot):
        xn, t_xn = self.xn[slot], self.t_xn[slot]
        ss, t_ss = self.nt_ss[slot], self.t_nt_ss[slot]
        rs, t_rsd = self.nt_rs[slot], self.t_nt_rs[slot]
        self.act(xn[:], x_ap, AF.Square, r=[t_x], w=[t_xn, t_ss], accum_out=ss[:, 0:1])
        self.rstd(ss[:, 0:1], 1, D, EPS, rs[:, 0:1], r=[t_ss], w=[t_rsd])
        self.ts("dve", xn[:], x_ap, rs[:, 0:1], None, ALU.mult, r=[t_x, t_rsd], w=[t_xn])
        bT, t_bT = self.bankT, self.t_bank[0]
        for kc in range(KC):
            self.tr(bT[:, kc * P:(kc + 1) * P], xn[:, kc * P:(kc + 1) * P], self.ident_bf[:],
                    r=[t_xn, self.t_cbf], w=[t_bT], sig=(kc == KC - 1))
        self.tt("dve", hT, bT[:].rearrange("p (k t) -> p k t", k=KC),
                gainT.unsqueeze(2).to_broadcast([P, KC, P]), ALU.mult, r=[t_bT, self.t_c32], w=[t_hT])

    def headnorm(self, src, H, gain, out, r, w, eng="dve"):
        t = self.t_hn
        sq = self.hn_sq[:, 0:H * 64]
        self.tt(eng, sq, src, src, ALU.mult, r=r, w=[t])
        self.red(eng, self.hn_ss[:, 0:H], sq.rearrange("p (h e) -> p h e", h=H), ALU.add, r=[t], w=[t])
        self.rstd(self.hn_ss[:, 0:H], H, 64, EPS, self.hn_rs[:, 0:H], r=[t], w=[t])
        self.tt(eng, out.rearrange("p (h e) -> p h e", h=H), src.rearrange("p (h e) -> p h e", h=H),
                self.hn_rs[:, 0:H].unsqueeze(2).to_broadcast([P, H, 64]), ALU.mult, r=list(r) + [t], w=w)
        if gain is not None:
            self.tt(eng, out, out, gain, ALU.mult, r=list(w) + [self.t_c32], w=w)

    def rotary(self, src, H, rope, t_rope, out, r, w, eng="pool", scratch=None):
        ro_a, ro_b, t = scratch if scratch is not None else (self.ro_a, self.ro_b, self.t_ro)
        s4 = src.rearrange("p (h two e) -> p h two e", h=H, two=2)
        a4 = ro_a[:, 0:H * 64].rearrange("p (h two e) -> p h two e", h=H, two=2)
        b4 = ro_b[:, 0:H * 64].rearrange("p (h two e) -> p h two e", h=H, two=2)
        o4 = out.rearrange("p (h two e) -> p h two e", h=H, two=2)
        cosb = rope[:, 0:32].unsqueeze(1).unsqueeze(1).to_broadcast([P, H, 2, 32])
        sinb = rope[:, 32:64].unsqueeze(1).to_broadcast([P, H, 32])
        rr = list(r) + [t_rope]
        self.tt(eng, a4, s4, cosb, ALU.mult, r=rr, w=[t])
        self.tt(eng, b4[:, :, 0, :], s4[:, :, 1, :], sinb, ALU.mult, r=rr, w=[t])
        self.tt(eng, b4[:, :, 1, :], s4[:, :, 0, :], sinb, ALU.mult, r=rr, w=[t])
        self.tt(eng, o4[:, :, 0, :], a4[:, :, 0, :], b4[:, :, 0, :], ALU.subtract, r=[t], w=w)
        self.tt(eng, o4[:, :, 1, :], a4[:, :, 1, :], b4[:, :, 1, :], ALU.add, r=[t], w=w)

    def load_w(self, dst, src, t_dst):
        return self.dma("pool", dst, src, r=(), w=[t_dst])

    def build(self):
        nc, s = self.nc, self.s
        NO, NT, S = self.NO, self.NT, self.S
        with ExitStack() as top:
            sb = lambda name, shape, dtp: _alloc(nc, top, name, shape, dtp)
            self.bank = [top.enter_context(nc.psum_tensor(f"bank{i}", [P, 512], F32)) for i in range(1, 8)]
            self.bankT = top.enter_context(nc.psum_tensor("bankT", [P, 1024], BF16))
            self.t_bank = [Tok(f"bank{i}") for i in range(8)]
            self.c32 = sb("c32", [P, NC_TOP], F32)
            self.t_c32 = Tok("c32")
            self.ident_bf = sb("ident_bf", [P, P], BF16)
            self.irep_bf = sb("irep_bf", [P, 4, P], BF16)
            self.maskT_bf = sb("maskT_bf", [P, 4, P], BF16)
            self.ones_bf = sb("ones_bf", [P, P], BF16)
            self.t_cbf = Tok("cbf")
            self.x_own = sb("x_own", [P, NO, D], F32)
            self.t_x = [Tok(f"x{j}") for j in range(NO)]
            self.yTn = [None] * 4
            self.yTn[1] = sb("yT1", [P, 2, NO * P], BF16)
            self.t_yT = [[Tok(f"yT{n}_{j}") for j in range(NO)] for n in range(4)]
            self.hl2 = sb("hl2", [P, NT, KC, 2], BF16)
            self.t_hl2 = [Tok(f"hl2_{g}") for g in range(NT)]
            xn0 = sb("xn0", [P, D], BF16)
            self.xn = [xn0, xn0]
            t_xn0 = Tok()
            self.t_xn = [t_xn0, t_xn0]
            self.nt_ss = [sb(f"ntss{i}", [P, 1], F32) for i in range(2)]
            self.t_nt_ss = [Tok() for _ in range(2)]
            self.nt_rs = [sb(f"ntrs{i}", [P, 1], F32) for i in range(2)]
            self.t_nt_rs = [Tok() for _ in range(2)]
            self.t_junk_act = Tok()
            self.t_junk_dve = Tok()
            self.rs_tmp = sb("rs_tmp", [P, 8], F32)
            self.t_rs = Tok()
            self.hn_sq = sb("hn_sq", [P, 512], F32)
            self.hn_ss = sb("hn_ss", [P, 8], F32)
            self.hn_rs = sb("hn_rs", [P, 8], F32)
            self.t_hn = Tok()
            self.t_ro = Tok()

            import os
            ph = os.environ.get("KPH", "att,mlstm,sgu,conv,merge,ffn").split(",")
            npass = int(os.environ.get("KNP", "3"))
            passes = [(0, 0), (0, 1), (1, None)][:npass]
            outs = []
            for pi, (l, q) in enumerate(passes):
                s.barrier()
                self.cur_consts = self.d_consts3[pi]
                self.dma("sp", self.c32[:], self.cur_consts[:, 0:NC_TOP], r=(), w=[self.t_c32])
                if pi == 0:
                    self.cp("dve", self.ident_bf[:], self.C("ident"), r=[self.t_c32], w=[self.t_cbf])
                    self.cp("dve", self.ones_bf[:], self.C("ones"), r=[self.t_c32], w=[self.t_cbf])
                    for h in range(4):
                        self.cp("dve", self.irep_bf[:, h, :], self.C("ident"), r=[self.t_c32], w=[self.t_cbf])
                        self.cp("dve", self.maskT_bf[:, h, :], self.C("maskT"), r=[self.t_c32], w=[self.t_cbf])
                self.d_w_in, self.d_w_branch, self.d_w_out = self.D_w_in[l], self.D_w_branch[l], self.D_w_out[l]
                self.d_w_up, self.d_w_down = self.D_w_up[l], self.D_w_down[l]
                if q is not None:
                    self.d_xf = self.d_x
                    self.d_rope_own = _Idx(lambda j, q=q: self.d_rope_all[2 * j + q])
                    for j in range(NO):
                        g = 2 * j + q
                        self.dma("sp", self.x_own[:, j, :], self.d_x[g * P:(g + 1) * P, :], r=(), w=[self.t_x[j]])
                else:
                    self.d_xf = self.d_x1
                    self.d_rope_own = self.d_rope_own_in
                    with ExitStack() as es:
                        xb = [_alloc(nc, es, f"xblend{i}", [P, D], F32) for i in range(2)]
                        t_xb = [Tok() for _ in range(2)]
                        for j in range(NO):
                            sl = j % 2
                            self.dma("sp", self.x_own[:, j, :], self.d_x1[(2 * j) * P:(2 * j + 1) * P, :], r=(), w=[self.t_x[j]])
                            self.dma("sp", xb[sl][:], self.d_x1[(2 * j + 1) * P:(2 * j + 2) * P, :], r=(), w=[t_xb[sl]])
                            self.tt("dve", xb[sl][:], xb[sl][:], self.x_own[:, j, :], ALU.subtract,
                                    r=[t_xb[sl], self.t_x[j]], w=[t_xb[sl]])
                            self.stt("dve", self.x_own[:, j, :], xb[sl][:], self.C("pflag"), self.x_own[:, j, :],
                                     ALU.mult, ALU.add, r=[t_xb[sl], self.t_x[j], self.t_c32], w=[self.t_x[j]])
                        s.barrier()
                if "att" in ph:
                    self.phase_attention()
                s.barrier()
                with ExitStack() as mid:
                    self.hT_own = _alloc(nc, mid, "hT_own", [P, KC, NO * P], BF16)
                    for n in (0, 2, 3):
                        self.yTn[n] = _alloc(nc, mid, f"yT{n}", [P, 2, NO * P], BF16)
                    self.t_hT = [Tok(f"hT{j}") for j in range(NO)]
                    if "att" in ph:
                        for j in range(NO):
                            self.dma("sp", self.hT_own[:, :, j * P:(j + 1) * P],
                                     self.d_hTo[j].rearrange("p (k t) -> p k t", k=KC), r=(), w=[self.t_hT[j]])
                    else:
                        self.phase_hT(self.C("lnmixT"))
                    if "mlstm" in ph:
                        self.phase_mlstm()
                    s.barrier()
                    if "sgu" in ph:
                        self.phase_sgu()
                    s.barrier()
                    if "conv" in ph:
                        self.phase_conv()
                    s.barrier()
                    if "merge" in ph:
                        self.phase_merge()
                    s.barrier()
                    if "ffn" in ph:
                        self.phase_hT(self.C("lnmlpT"))
                        self.phase_ffn()
                    s.barrier()
                last = (pi == len(passes) - 1)
                for j in range(NO):
                    if last:
                        dst = self.d_y[j * P:(j + 1) * P, :]
                    else:
                        g = 2 * j + q
                        dst = self.d_x1[g * P:(g + 1) * P, :]
                    outs.append(self.dma("sp", dst, self.x_own[:, j, :], r=[self.t_x[j]], w=()))
            s.barrier()
            s.final_wait("sp", outs + self.dbg_deps)
        return nc

    dbg_deps = []

    def dump(self, name, src_ap, toks, dst_slice=None):
        if name not in self.d_dbg:
            return
        dst = self.d_dbg[name] if dst_slice is None else dst_slice(self.d_dbg[name])
        self.dbg_deps = self.dbg_deps + [self.dma("sp", dst, src_ap, r=toks, w=())]

    def phase_hT(self, gainT):
        for j in range(self.NO):
            self.norm_transpose(self.x_own[:, j, :], self.t_x[j], gainT,
                                self.hT_own[:, :, j * P:(j + 1) * P], self.t_hT[j], j % 2)

    def phase_attention(self):
        nc, s = self.nc, self.s
        NO, NT, S = self.NO, self.NT, self.S
        B = self.bank
        tb = self.t_bank
        with ExitStack() as es:
            sb = lambda name, shape, dtp: _alloc(nc, es, name, shape, dtp)
            WA = sb("WA", [P, KC, 1352], BF16)
            t_WA = Tok("WA")
            wv = self.d_w_in.rearrange("(k p) c -> p k c", p=P)
            for (dst0, src0, n) in ((0, OFF["b_q"], 256), (256, OFF["b_wi"], 8), (264, OFF["b_qi"], 512),
                                    (776, OFF["b_k"], 512), (1288, OFF["b_ki"], 64)):
                self.load_w(WA[:, :, dst0:dst0 + n], wv[:, :, src0:src0 + n], t_WA)
            KT = sb("KT", [P, 2, S], BF16)
            t_KT = [Tok(f"KT{g}") for g in range(NT)]
            VA = sb("VA", [P, NT, 4, 65], BF16)
            t_VA = [Tok(f"VA{g}") for g in range(NT)]
            KI2 = sb("KI2", [P, S], BF16)
            t_KI = [Tok(f"KI{g}") for g in range(NT)]
            xt0 = sb("xt0", [P, D], F32)
            xt = [xt0, xt0]
            t_xt0 = Tok()
            t_xt = [t_xt0, t_xt0]
            hTg0 = sb("hTg0", [P, KC, P], BF16)
            hT = [hTg0, hTg0]
            t_hTg0 = Tok()
            t_hT = [t_hTg0, t_hTg0]
            rope_g = [sb(f"ropeg{i}", [P, 64], F32) for i in range(2)]
            t_rope_g = [Tok() for _ in range(2)]
            rope_o = sb("ropeo", [P, 64], F32)
            t_rope_o = Tok()
            ksb = sb("ksb", [P, 256], F32); t_ksb = Tok()
            kisb = sb("kisb", [P, 64], F32); t_kisb = Tok()
            kr = sb("kr", [P, 256], BF16); t_kr = Tok()
            ki2 = sb("ki2", [P, 128], BF16); t_ki2 = Tok()
            bn6 = sb("bn6", [P, 8], F32); t_bn = Tok()
            SC = sb("SC", [P, S], F32); t_SC = Tok("SC")
            self.ro_a = SC[:, 0:512]
            self.ro_b = SC[:, 512:1024]
            self.t_ro = t_SC
            rog = sb("rog", [P, 1024], F32)
            sc_g = (rog[:, 0:512], rog[:, 512:1024], Tok("rog"))
            sc_o = sc_g
            MB = sb("MB", [P, S], BF16); t_MB = Tok("MB")
            qsb = sb("qsb", [P, 264], F32); t_qsb = Tok()
            qisb = sb("qisb", [P, 512], F32); t_qisb = Tok()
            qr = sb("qr", [P, 256], BF16); t_qr = Tok()
            qir = sb("qir", [P, 512], BF16); t_qir = Tok()
            QTp2 = [sb(f"QTp{i}", [P, 4, P], BF16) for i in range(2)]; t_QTp2 = [Tok() for _ in range(2)]

            QiTp = sb("QiTp", [P, 8, P], BF16); t_QiTp = Tok()
            Dg = sb("Dg", [P, 8, P], BF16); t_Dg = Tok()
            wab = sb("wab", [P, 8], F32); wsg = sb("wsg", [P, 8], F32); wtmp = sb("wtmp", [P, 8], F32); t_w8 = Tok()
            Rr = [sb(f"Rr{i}", [P, 512], BF16) for i in range(2)]
            t_Rr = [Tok() for _ in range(2)]
            PT = [sb(f"PT{i}", [P, 4, P], BF16) for i in range(2)]
            t_PT = [Tok() for _ in range(2)]
            bs = sb("bs", [P, 8], F32); t_bs = Tok()
            hwt = sb("hwt", [P, 64], F32); t_hw = Tok()
            t_cnt = Tok()
            hk = sb("hk", [P, NBISECT + 2], F32)
            cnt = sb("cnt", [P, 1], F32)
            osb = sb("osb", [P, 4, 65], F32); t_osb = Tok()
            rden = sb("rden", [P, 4], F32)
            yb = sb("yb", [P, 256], BF16); t_yb = Tok()

            self.memset("pool", VA[:], 1.0, w=t_VA)
            for i in range(2):
                self.memset("pool", QTp2[i][:], 0.0, w=[t_QTp2[i]])
            self.memset("pool", QiTp[:], 0.0, w=[t_QiTp])
            for k in range(NBISECT + 2):
                self.memset("pool", hk[:, k:k + 1], 2.0 ** (-(k + 1)), w=[t_bs])

            def proc_global(g):
                sl = g % 2
                self.dma("sp", xt[sl][:], self.d_xf[g * P:(g + 1) * P, :], r=(), w=[t_xt[sl]])
                self.dma("sp", rope_g[sl][:], self.d_rope_all[g], r=(), w=[t_rope_g[sl]])
                self.norm_transpose(xt[sl][:], t_xt[sl], self.C("lnmixT"), hT[sl][:], t_hT[sl], sl)
                self.cp("pool", self.hl2[:, g, :, :], hT[sl][:, :, P - 2:P], r=[t_hT[sl]], w=[self.t_hl2[g]])
                self.dma("pool", self.d_hTs[g], hT[sl][:].rearrange("p k t -> p (k t)"), r=[t_hT[sl]], w=())
                for kc in range(KC):
                    self.mm(B[0][:, 0:512], hT[sl][:, kc, :], WA[:, kc, 776:1288], kc == 0, kc == KC - 1,
                            r=[t_hT[sl], t_WA], w=[tb[1]])
                for kc in range(KC):
                    self.mm(B[1][:, 0:64], hT[sl][:, kc, :], WA[:, kc, 1288:1352], kc == 0, kc == KC - 1,
                            r=[t_hT[sl], t_WA], w=[tb[2]])
                self.cp("dve", ksb[:], B[0][:, 0:256], r=[tb[1]], w=[t_ksb])
                self.cp("dve", VA[:, g, :, 0:64], B[0][:, 256:512].rearrange("p (h e) -> p h e", h=4),
                        r=[tb[1]], w=[t_VA[g]])
                self.cp("dve", kisb[:], B[1][:, 0:64], r=[tb[2]], w=[t_kisb])
                self.headnorm(ksb[:], 4, self.C("kn"), ksb[:], r=[t_ksb], w=[t_ksb])
                self.rotary(ksb[:], 4, rope_g[sl], t_rope_g[sl], kr[:], r=[t_ksb], w=[t_kr], eng="dve", scratch=sc_g)
                bT = self.bankT
                for pr in range(2):
                    self.tr(bT[:, pr * P:(pr + 1) * P], kr[:, pr * P:(pr + 1) * P], self.ident_bf[:],
                            r=[t_kr, self.t_cbf], w=[tb[0]], sig=False)
                self.red("dve", bn6[:, 0:1], kisb[:], ALU.add, r=[t_kisb], w=[t_bn])
                self.ts("dve", bn6[:, 1:2], bn6[:, 0:1], 1.0 / 64, None, ALU.mult, r=[t_bn], w=[t_bn])
                self.ts("dve", kisb[:], kisb[:], bn6[:, 1:2], None, ALU.subtract, r=[t_bn, t_kisb], w=[t_kisb])
                self.tt("dve", self.hn_sq[:, 0:64], kisb[:], kisb[:], ALU.mult, r=[t_kisb], w=[self.t_hn])
                self.red("dve", bn6[:, 2:3], self.hn_sq[:, 0:64], ALU.add, r=[self.t_hn], w=[t_bn])
                self.rstd(bn6[:, 2:3], 1, 64, EPS, self.hn_rs[:, 0:1], r=[t_bn], w=[self.t_hn])
                self.ts("dve", kisb[:], kisb[:], self.hn_rs[:, 0:1], None, ALU.mult, r=[self.t_hn, t_kisb], w=[t_kisb])
                self.tt("dve", kisb[:], kisb[:], self.C("kidx"), ALU.mult, r=[t_kisb, self.t_c32], w=[t_kisb])
                self.rotary(kisb[:], 1, rope_g[sl], t_rope_g[sl], ki2[:, 0:64], r=[t_kisb], w=[t_ki2], eng="dve", scratch=sc_g)
                self.cp("dve", ki2[:, 64:128], ki2[:, 0:64], r=[t_ki2], w=[t_ki2])
                self.tr(bT[:, 2 * P:3 * P], ki2[:], self.ident_bf[:], r=[t_ki2, self.t_cbf], w=[tb[0]], sig=True)
                self.cp("dve", KT[:, :, g * P:(g + 1) * P], bT[:, 0:2 * P].rearrange("p (a t) -> p a t", a=2),
                        r=[tb[0]], w=[t_KT[g]])
                self.cp("dve", KI2[:, g * P:(g + 1) * P], bT[:, 2 * P:3 * P], r=[tb[0]], w=[t_KI[g]])

            def proc_own(j):
                QTp, t_QTp = QTp2[j % 2], t_QTp2[j % 2]
                sl = j % 2
                xj = self.x_own[:, j, :]
                self.dma("sp", rope_o[:], self.d_rope_own[j], r=(), w=[t_rope_o])
                self.norm_transpose(xj, self.t_x[j], self.C("lnmixT"), hT[sl][:], t_hT[sl], sl)
                self.dma("pool", self.d_hTo[j], hT[sl][:].rearrange("p k t -> p (k t)"), r=[t_hT[sl]], w=())
                for kc in range(KC):
                    self.mm(B[0][:, 0:264], hT[sl][:, kc, :], WA[:, kc, 0:264], kc == 0, kc == KC - 1,
                            r=[t_hT[sl], t_WA], w=[tb[1]])
                for kc in range(KC):
                    self.mm(B[1][:, 0:512], hT[sl][:, kc, :], WA[:, kc, 264:776], kc == 0, kc == KC - 1,
                            r=[t_hT[sl], t_WA], w=[tb[2]])
                self.cp("dve", qsb[:], B[0][:, 0:264], r=[tb[1]], w=[t_qsb])
                self.cp("dve", qisb[:], B[1][:, 0:512], r=[tb[2]], w=[t_qisb])
                self.headnorm(qsb[:, 0:256], 4, self.C("qn"), qsb[:, 0:256], r=[t_qsb], w=[t_qsb])
                self.rotary(qsb[:, 0:256], 4, rope_o, t_rope_o, qr[:], r=[t_qsb], w=[t_qr], eng="dve", scratch=sc_g)
                self.rotary(qisb[:], 8, rope_o, t_rope_o, qir[:], r=[t_qisb], w=[t_qir], eng="dve", scratch=sc_o)
                bT = self.bankT
                for pr in range(2):
                    self.tr(bT[:, pr * P:(pr + 1) * P], qr[:, pr * P:(pr + 1) * P], self.ident_bf[:],
                            r=[t_qr, self.t_cbf], w=[tb[0]], sig=False)
                for pr in range(4):
                    self.tr(bT[:, (2 + pr) * P:(3 + pr) * P], qir[:, pr * P:(pr + 1) * P], self.ident_bf[:],
                            r=[t_qir, self.t_cbf], w=[tb[0]], sig=(pr == 3))
                for h in range(4):
                    lo = (h % 2) * 64
                    self.cp("dve", QTp[lo:lo + 64, h, :], bT[lo:lo + 64, (h // 2) * P:(h // 2 + 1) * P],
                            r=[tb[0]], w=[t_QTp])
                for h in range(8):
                    lo = (h % 2) * 64
                    self.cp("dve", QiTp[lo:lo + 64, h, :],
                            bT[lo:lo + 64, (2 + h // 2) * P:(3 + h // 2) * P], r=[tb[0]], w=[t_QiTp])
                wsrc = qsb[:, 256:264]
                self.ts("dve", wsg[:], wsrc, -1.0, None, ALU.mult, r=[t_qsb], w=[t_w8])
                self.tt("dve", wab[:], wsrc, wsg[:], ALU.max, r=[t_qsb, t_w8], w=[t_w8])
                self.ts("dve", wab[:], wab[:], float(8 ** -0.5 * 64 ** -0.5), None, ALU.mult, r=[t_w8], w=[t_w8])
                self.ts("dve", wsg[:], wsrc, 0.0, None, ALU.is_gt, r=[t_qsb, t_w8], w=[t_w8])
                self.ts("dve", wtmp[:], wsrc, 0.0, None, ALU.is_lt, r=[t_qsb], w=[t_w8])
                self.tt("dve", wsg[:], wsg[:], wtmp[:], ALU.subtract, r=[t_w8], w=[t_w8])
                for h in range(8):
                    self.ts("dve", Dg[:, h, :], self.C("ident"), wsg[:, h:h + 1], None, ALU.mult,
                            r=[t_w8, self.t_c32], w=[t_Dg])

            def proc_own_idx(j):
                nk = (j + 1) * 256
                nblk = (nk + 511) // 512
                for b in range(nblk):
                    k0 = b * 512
                    kw = min(512, nk - k0)
                    kg = list(range(k0 // P, (k0 + kw) // P))
                    pend = None
                    for h in range(8):
                        rb = 3 + (h % 2)
                        self.mm(B[rb - 1][:, 0:kw], QiTp[:, h, :], KI2[:, k0:k0 + kw], True, True,
                                r=[t_QiTp] + [t_KI[g] for g in kg], w=[tb[rb]])
                        ri = h % 2
                        self.ts("dve", Rr[ri][:, 0:kw], B[rb - 1][:, 0:kw], 0.0, wab[:, h:h + 1], ALU.max, ALU.mult,
                                r=[tb[rb], t_w8], w=[t_Rr[ri]])
                        if pend is not None:
                            ph_, pri = pend
                            self.mm(B[4][:, 0:kw], Dg[:, ph_, :], Rr[pri][:, 0:kw], ph_ == 0, False,
                                    r=[t_Dg, t_Rr[pri]], w=[tb[5]], sig=False)
                        pend = (h, ri)
                    ph_, pri = pend
                    self.mm(B[4][:, 0:kw], Dg[:, ph_, :], Rr[pri][:, 0:kw], False, True,
                            r=[t_Dg, t_Rr[pri]], w=[tb[5]])
                    self.cp("dve", SC[:, k0:k0 + kw], B[4][:, 0:kw], r=[tb[5]], w=[t_SC])
                self.red("dve", bs[:, 0:1], SC[:, 0:nk], ALU.max, r=[t_SC], w=[t_bs], absval=True)
                self.tt("dve", SC[:, nk - 256:nk], SC[:, nk - 256:nk], self.C("cmask"), ALU.add,
                        r=[t_SC, self.t_c32], w=[t_SC])
                self.ts("dve", bs[:, 1:2], bs[:, 0:1], 2.002, 2e-20, ALU.mult, ALU.add, r=[t_bs], w=[t_bs])
                hw = hwt
                self.ts("dve", hw[:, 0:NBISECT + 2], hk[:, 0:NBISECT + 2], bs[:, 1:2], None, ALU.mult,
                        r=[t_bs], w=[t_hw])
                self.ts("dve", hw[:, 32:32 + NBISECT + 2], hw[:, 0:NBISECT + 2], -0.5, None, ALU.mult,
                        r=[t_hw], w=[t_hw])
                self.memset("dve", bs[:, 3:4], 0.0, w=[t_bs])

            def proc_own_b(j):
                QTp, t_QTp = QTp2[j % 2], t_QTp2[j % 2]
                nk = (j + 1) * 256
                nch = nk // P
                hw = hwt
                bT = self.bankT
                for k in range(NBISECT):
                    self.act(MB[:, 0:nk], SC[:, 0:nk], AF.Sign, r=[t_SC, t_bs], w=[t_MB, t_cnt],
                             bias=bs[:, 3:4], accum_out=cnt[:, 0:1])
                    self.act(bs[:, 4:5], cnt[:, 0:1], AF.Sign, r=[t_cnt], w=[t_bs], bias=float(nk - 512 + 0.5))
                    self.act(bs[:, 3:4], bs[:, 4:5], AF.Identity, r=[t_bs, t_hw], w=[t_bs],
                             scale=hw[:, 32 + k:33 + k], bias=bs[:, 3:4])
                self.stt("dve", bs[:, 5:6], bs[:, 3:4], hw[:, NBISECT:NBISECT + 1], self.C("neghalf", 0, 1),
                         ALU.add, ALU.mult, r=[t_bs, t_hw, self.t_c32], w=[t_bs])
                self.ts("dve", bs[:, 5:6], bs[:, 5:6], 2.0, None, ALU.mult, r=[t_bs], w=[t_bs])
                self.ts("dve", MB[:, 0:nk], SC[:, 0:nk], bs[:, 5:6], MBIAS, ALU.is_lt, ALU.mult,
                        r=[t_SC, t_bs], w=[t_MB])
                def emit_pv(c):
                    pt = c % 2
                    for h in range(4):
                        self.mm(B[4][:, h * 65:(h + 1) * 65], PT[pt][:, h, :], VA[:, c, h, :],
                                (c == 0 and h == 0), (c == nch - 1 and h == 3),
                                r=[t_PT[pt], t_VA[c]], w=[tb[5]], sig=(h == 3))

                for c in range(nch):
                    stb = 6 + (c % 2)
                    pt = c % 2
                    for h in range(4):
                        self.mm(B[stb - 1][:, h * P:(h + 1) * P], KT[:, h // 2, c * P:(c + 1) * P], QTp[:, h, :],
                                h == 0, False, r=[t_KT[c], t_QTp], w=[tb[stb]], sig=False)
                    self.mm(B[stb - 1][:, 0:512], MB[:, c * P:(c + 1) * P], self.irep_bf[:].rearrange("p a t -> p (a t)"),
                            False, True, r=[t_MB, self.t_cbf], w=[tb[stb]])
                    self.act(PT[pt][:].rearrange("p a t -> p (a t)"), B[stb - 1][:, 0:512], AF.Exp,
                             r=[tb[stb]], w=[t_PT[pt]], scale=0.125)
                    if c >= 1:
                        emit_pv(c - 1)
                emit_pv(nch - 1)
                self.cp("act", osb[:].rearrange("p h e -> p (h e)"), B[4][:, 0:260], r=[tb[5]], w=[t_osb])
                self.s.op("dve", lambda e: e.reciprocal(out=rden[:], in_=osb[:, :, 64]), r=[t_osb], w=[t_osb])
                self.tt("dve", yb[:].rearrange("p (h e) -> p h e", h=4), osb[:, :, 0:64],
                        rden[:].unsqueeze(2).to_broadcast([P, 4, 64]), ALU.mult, r=[t_osb], w=[t_yb])
                for pr in range(2):
                    self.tr(bT[:, pr * P:(pr + 1) * P], yb[:, pr * P:(pr + 1) * P], self.ident_bf[:],
                            r=[t_yb, self.t_cbf], w=[tb[0]], sig=(pr == 1))
                self.cp("act", self.yTn[1][:, :, j * P:(j + 1) * P], bT[:, 0:2 * P].rearrange("p (a t) -> p a t", a=2),
                        r=[tb[0]], w=[self.t_yT[1][j]])

            proc_global(0)
            proc_global(1)
            proc_own(0)
            for j in range(NO):
                proc_own_idx(j)
                if j + 1 < NO:
                    proc_global(2 * j + 2)
                    proc_global(2 * j + 3)
                    proc_own(j + 1)
                proc_own_b(j)

    def phase_mlstm(self):
        nc, s = self.nc, self.s
        NO, NT, S = self.NO, self.NT, self.S
        B, tb = self.bank, self.t_bank
        with ExitStack() as es:
            sb = lambda name, shape, dtp: _alloc(nc, es, name, shape, dtp)
            WC = sb("WC", [P, KC, 1032], BF16); t_WC = Tok("WC")
            wv = self.d_w_in.rearrange("(k p) c -> p k c", p=P)
            for (dst0, src0, n) in ((0, OFF["c_k"], 512), (512, OFF["c_i"], 8), (520, OFF["c_q"], 256),
                                    (776, OFF["c_o"], 256)):
                self.load_w(WC[:, :, dst0:dst0 + n], wv[:, :, src0:src0 + n], t_WC)
            xt = [sb(f"mxt{i}", [P, D], F32) for i in range(2)]; t_xt = [Tok() for _ in range(2)]
            hT = [sb(f"mhT{i}", [P, KC, P], BF16) for i in range(2)]; t_hT = [Tok() for _ in range(2)]
            Cst = sb("Cst", [P, 2, 65], F32); t_C = Tok("Cst")
            Ca = sb("Ca", [P, 2, 65], F32); t_Ca = Tok("Ca")
            Csel = sb("Csel", [P, 2, 65], BF16); t_Csel = Tok("Csel")
            kp = [sb(f"kp{i}", [P, 256], BF16) for i in range(3)]
            va = [sb(f"va{i}", [P, 4, 65], BF16) for i in range(3)]
            gsc = [sb(f"gsc{i}", [P, 24], F32) for i in range(3)]
            gsb = [sb(f"gsb{i}", [P, 16], BF16) for i in range(3)]
            t_g = [Tok() for _ in range(3)]
            qb = sb("qb", [P, 256], BF16); t_qb = Tok()
            qTp = sb("mqTp", [P, 4, P], BF16); t_qTp = Tok()
            kT = sb("mkT", [P, 2, P], BF16); t_kT = Tok()
            Sm = sb("Sm", [P, 4, P], BF16); t_Sm = Tok()
            ep = sb("ep", [P, 16], F32); t_ep = Tok()
            hc = sb("hc", [P, 256], F32); t_hc = Tok()
            so = sb("so", [P, 256], F32); t_so = Tok()
            yc = sb("yc", [P, 256], BF16); t_yc = Tok()
            self.memset("pool", Cst[:], 0.0, w=[t_C])
            self.memset("pool", qTp[:], 0.0, w=[t_qTp])

            def kv_gates(hTap, t_h, sl):
                for kc in range(KC):
                    self.mm(B[0][:, 0:512], hTap[:, kc, :], WC[:, kc, 0:512], kc == 0, kc == KC - 1,
                            r=[t_h, t_WC], w=[tb[1]])
                for kc in range(KC):
                    self.mm(B[1][:, 0:8], hTap[:, kc, :], WC[:, kc, 512:520], kc == 0, kc == KC - 1,
                            r=[t_h, t_WC], w=[tb[2]])
                G = gsc[sl]; tg = t_g[sl]
                self.tt("dve", G[:, 0:4], B[1][:, 4:8], self.C("fb"), ALU.add, r=[tb[2], self.t_c32], w=[tg])
                self.act(G[:, 0:4], G[:, 0:4], AF.Exp, r=[tg], w=[tg], scale=-1.0)
                self.act(G[:, 0:4], G[:, 0:4], AF.Ln, r=[tg], w=[tg], bias=1.0)
                self.mm(B[2][:, 0:4], self.C("maskT"), G[:, 0:4], True, False, r=[tg, self.t_c32], w=[tb[3]], sig=False)
                self.mm(B[2][:, 4:8], self.C("ones"), G[:, 0:4], False, True, r=[tg, self.t_c32], w=[tb[3]])
                self.tt("dve", G[:, 4:8], B[1][:, 0:4], self.C("ib"), ALU.add, r=[tb[2], self.t_c32], w=[tg])
                self.tt("dve", G[:, 4:8], G[:, 4:8], B[2][:, 0:4], ALU.add, r=[tg, tb[3]], w=[tg])
                self.act(G[:, 8:12], G[:, 4:8], AF.Exp, r=[tg], w=[tg])
                self.act(G[:, 12:20], B[2][:, 0:8], AF.Exp, r=[tb[3]], w=[tg], scale=-1.0)
                self.act(kp[sl][:], B[0][:, 0:256], AF.Copy, r=[tb[1]], w=[tg], scale=0.125)
                self.tt("dve", va[sl][:, :, 0:64], B[0][:, 256:512].rearrange("p (h e) -> p h e", h=4),
                        G[:, 8:12].unsqueeze(2).to_broadcast([P, 4, 64]), ALU.mult, r=[tb[1], tg], w=[tg])
                self.cp("dve", va[sl][:, :, 64], G[:, 8:12], r=[tg], w=[tg])

            def state_update(sl):
                G = gsc[sl]; tg = t_g[sl]
                for h in range(4):
                    pr = h // 2
                    self.mm(B[3][:, h * 65:(h + 1) * 65], kp[sl][:, pr * P:(pr + 1) * P], va[sl][:, h, :],
                            True, True, r=[tg], w=[tb[4]], sig=(h == 3))
                for h in range(4):
                    lo = (h % 2) * 64
                    self.tt("dve", Cst[lo:lo + 64, h // 2, :], Cst[lo:lo + 64, h // 2, :],
                            B[3][lo:lo + 64, h * 65:(h + 1) * 65], ALU.add, r=[tb[4], t_C], w=[t_C])
                    self.ts("dve", Cst[lo:lo + 64, h // 2, :], Cst[lo:lo + 64, h // 2, :],
                            G[lo:lo + 64, 16 + h:17 + h], None, ALU.mult, r=[tg, t_C], w=[t_C])

            def proc_global(g, sl):
                self.dma("sp", hT[sl][:].rearrange("p k t -> p (k t)"), self.d_hTs[g], r=(), w=[t_hT[sl]])
                kv_gates(hT[sl], t_hT[sl], sl)
                state_update(sl)

            import os
            KO = int(os.environ.get('KO', '99'))
            def proc_own(j):
                hTo = self.hT_own[:, :, j * P:(j + 1) * P]
                for kc in range(KC):
                    self.mm(B[4][:, 0:512], hTo[:, kc, :], WC[:, kc, 520:1032], kc == 0, kc == KC - 1,
                            r=[self.t_hT[j], t_WC], w=[tb[5]])
                kv_gates(hTo, self.t_hT[j], 2)
                if KO < 1: return
                G = gsc[2]; tg = t_g[2]
                if KO < 2: return
                self.cp("act", qb[:], B[4][:, 0:256], r=[tb[5]], w=[t_qb])
                bT = self.bankT
                for pr in range(2):
                    for kc in range(KC):
                        self.mm(B[3][:, pr * P:(pr + 1) * P], WC[:, kc, 520 + pr * P:520 + (pr + 1) * P], hTo[:, kc, :],
                                kc == 0, kc == KC - 1, r=[self.t_hT[j], t_WC], w=[tb[4]], sig=False)
                for pr in range(2):
                    for kc in range(KC):
                        self.mm(B[3][:, (2 + pr) * P:(3 + pr) * P], WC[:, kc, pr * P:(pr + 1) * P], hTo[:, kc, :],
                                kc == 0, kc == KC - 1, r=[self.t_hT[j], t_WC], w=[tb[4]],
                                sig=(pr == 1 and kc == KC - 1))
                if KO < 3: return
                for h in range(4):
                    lo = (h % 2) * 64
                    self.cp("act", qTp[lo:lo + 64, h, :], B[3][lo:lo + 64, (h // 2) * P:(h // 2 + 1) * P],
                            r=[tb[4]], w=[t_qTp])
                self.act(kT[:].rearrange("p a t -> p (a t)"), B[3][:, 2 * P:4 * P], AF.Copy, r=[tb[4]], w=[t_kT], scale=0.125)
                if KO < 4: return
                for h in range(4):
                    self.mm(B[5][:, h * P:(h + 1) * P], kT[:, h // 2, :], qTp[:, h, :], True, True,
                            r=[t_kT, t_qTp], w=[tb[6]], sig=(h == 3))
                if KO < 5: return
                self.tt("dve", Sm[:], B[5][:, 0:512].rearrange("p (h t) -> p h t", h=4), self.maskT_bf[:], ALU.mult,
                        r=[tb[6], self.t_cbf], w=[t_Sm])
                if KO < 6: return
                for h in range(4):
                    self.mm(B[6][:, h * 65:(h + 1) * 65], qTp[:, h, :], Csel[:, h // 2, :], h == 0, False,
                            r=[t_qTp, t_Csel], w=[tb[7]], sig=False)
                    self.mm(B[6][:, h * 65:(h + 1) * 65], Sm[:, h, :], va[2][:, h, :], False, True,
                            r=[t_Sm, tg], w=[tb[7]], sig=(h == 3))
                if KO < 7: return
                acc = B[6][:, 0:260].rearrange("p (h e) -> p h e", h=4)
                self.tt("dve", ep[:, 0:4], acc[:, :, 64], G[:, 12:16], ALU.mult, r=[tb[7], tg], w=[t_ep])
                self.act(ep[:, 0:4], ep[:, 0:4], AF.Abs, r=[t_ep], w=[t_ep])
                self.ts("dve", ep[:, 0:4], ep[:, 0:4], 1.0, None, ALU.max, r=[t_ep], w=[t_ep])
                self.s.op("dve", lambda e: e.reciprocal(out=ep[:, 4:8], in_=ep[:, 0:4]), r=[t_ep], w=[t_ep])
                self.tt("dve", ep[:, 4:8], ep[:, 4:8], G[:, 12:16], ALU.mult, r=[t_ep, tg], w=[t_ep])
                self.tt("dve", hc[:].rearrange("p (h e) -> p h e", h=4), acc[:, :, 0:64],
                        ep[:, 4:8].unsqueeze(2).to_broadcast([P, 4, 64]), ALU.mult, r=[tb[7], t_ep], w=[t_hc])
                if KO < 8: return
                self.headnorm(hc[:], 4, self.C("mn"), hc[:], r=[t_hc], w=[t_hc])
                if KO < 9: return
                self.act(so[:], B[4][:, 256:512], AF.Exp, r=[tb[5]], w=[t_so], scale=-1.0)
                self.ts("pool", so[:], so[:], 1.0, None, ALU.add, r=[t_so], w=[t_so])
                self.s.op("dve", lambda e: e.reciprocal(out=so[:], in_=so[:]), r=[t_so], w=[t_so])
                self.tt("dve", yc[:], hc[:], so[:], ALU.mult, r=[t_hc, t_so], w=[t_yc])
                if KO < 10: return
                for pr in range(2):
                    self.tr(bT[:, pr * P:(pr + 1) * P], yc[:, pr * P:(pr + 1) * P], self.ident_bf[:],
                            r=[t_yc, self.t_cbf], w=[tb[0]], sig=(pr == 1))
                self.cp("act", self.yTn[2][:, :, j * P:(j + 1) * P], bT[:, 0:2 * P].rearrange("p (a t) -> p a t", a=2),
                        r=[tb[0]], w=[self.t_yT[2][j]])

            pf = self.C("pflag")
            import os
            km = os.environ.get("KM", "gso")
            for j in range(NO):
                self.cp("pool", Ca[:], Cst[:], r=[t_C], w=[t_Ca])
                if "g" in km:
                    proc_global(2 * j, 0)
                if "s" in km:
                    self.tt("dve", hc[:, 0:130].rearrange("p (a e) -> p a e", a=2), Cst[:], Ca[:], ALU.subtract,
                            r=[t_C, t_Ca], w=[t_hc])
                    self.stt("dve", Csel[:], hc[:, 0:130].rearrange("p (a e) -> p a e", a=2), pf, Ca[:], ALU.mult, ALU.add,
                             r=[t_hc, t_Ca, self.t_c32], w=[t_Csel])
                if "g" in km:
                    proc_global(2 * j + 1, 1)
                if "o" in km:
                    proc_own(j)

    def phase_sgu(self):
        nc = self.nc
        NO = self.NO
        B, tb = self.bank, self.t_bank
        with ExitStack() as es:
            sb = lambda name, shape, dtp: _alloc(nc, es, name, shape, dtp)
            WS = sb("WS", [P, KC, 512], BF16); t_WS = Tok()
            wv = self.d_w_in.rearrange("(k p) c -> p k c", p=P)
            self.load_w(WS[:], wv[:, :, 0:512], t_WS)
            WmT = sb("WmT", [P, 4, P], BF16); t_Wm = Tok()
            wT32 = sb("wT32", [P, 512], F32); t_wT32 = Tok()
            o_, w_ = COFF["sgu_wT"]
            self.dma("sp", wT32[:], self.cur_consts[:, o_:o_ + w_], r=(), w=[t_wT32])
            self.tt("dve", WmT[:], wT32[:].rearrange("p (g t) -> p g t", g=4),
                    self.C("maskT").unsqueeze(1).to_broadcast([P, 4, P]), ALU.mult, r=[self.t_c32, t_wT32], w=[t_Wm])
            x2 = sb("sx2", [P, 512], F32); xh = sb("sxh", [P, 512], F32); zz = sb("szz", [P, 512], F32)
            ge = sb("sge", [P, 512], F32); t_s = Tok()
            vn = sb("svn", [P, 256], BF16); t_vn = Tok()
            ssq = sb("sssq", [P, 2], F32)
            ya = sb("sya", [P, 256], BF16); t_ya = Tok()
            for j in range(NO):
                hTo = self.hT_own[:, :, j * P:(j + 1) * P]
                for kc in range(KC):
                    self.mm(B[0][:, 0:512], hTo[:, kc, :], WS[:, kc, :], kc == 0, kc == KC - 1,
                            r=[self.t_hT[j], t_WS], w=[tb[1]])
                xp = B[0][:, 0:512]
                self.act(x2[:], xp, AF.Square, r=[tb[1]], w=[t_s])
                self.act(xh[:], xp, AF.Copy, r=[tb[1]], w=[t_s], scale=0.5)
                self.ts("pool", x2[:], x2[:], 0.044715, 1.0, ALU.mult, ALU.add, r=[t_s], w=[t_s])
                self.tt("dve", zz[:], x2[:], xp, ALU.mult, r=[t_s, tb[1]], w=[t_s])
                self.act(zz[:], zz[:], AF.Tanh, r=[t_s], w=[t_s], scale=0.7978845608028654)
                self.stt("dve", ge[:], zz[:], 1.0, xh[:], ALU.add, ALU.mult, r=[t_s], w=[t_s])
                self.act(x2[:, 0:256], ge[:, 256:512], AF.Square, r=[t_s], w=[t_s], accum_out=ssq[:, 0:1])
                self.rstd(ssq[:, 0:1], 1, 256, EPS, ssq[:, 1:2], r=[t_s], w=[t_s])
                self.stt("dve", vn[:], ge[:, 256:512], ssq[:, 1:2], self.C("sgu_norm"), ALU.mult, ALU.mult,
                         r=[t_s, self.t_c32], w=[t_vn])
                for g in range(4):
                    self.mm(B[1][:, g * 64:(g + 1) * 64], WmT[:, g, :], vn[:, g * 64:(g + 1) * 64], True, True,
                            r=[t_Wm, t_vn], w=[tb[2]], sig=(g == 3))
                for g in range(4):
                    self.stt("dve", ya[:, g * 64:(g + 1) * 64], B[1][:, g * 64:(g + 1) * 64],
                             self.C("sgu_b", g, g + 1), ge[:, g * 64:(g + 1) * 64], ALU.add, ALU.mult,
                             r=[tb[2], t_s, self.t_c32], w=[t_ya])
                bT = self.bankT
                for pr in range(2):
                    self.tr(bT[:, pr * P:(pr + 1) * P], ya[:, pr * P:(pr + 1) * P], self.ident_bf[:],
                            r=[t_ya, self.t_cbf], w=[tb[0]], sig=(pr == 1))
                self.cp("act", self.yTn[0][:, :, j * P:(j + 1) * P], bT[:, 0:2 * P].rearrange("p (a t) -> p a t", a=2),
                        r=[tb[0]], w=[self.t_yT[0][j]])

    def phase_conv(self):
        nc = self.nc
        NO = self.NO
        B, tb = self.bank, self.t_bank
        with ExitStack() as es:
            sb = lambda name, shape, dtp: _alloc(nc, es, name, shape, dtp)
            WD = sb("WD", [P, KC, 768], BF16); t_WD = Tok()
            wv = self.d_w_in.rearrange("(k p) c -> p k c", p=P)
            self.load_w(WD[:], wv[:, :, OFF["d_b"]:OFF["d_b"] + 768], t_WD)
            hz = sb("hz", [P, KC, 2], BF16); t_hz = Tok()
            hd = sb("hd", [P, KC, 2], F32); hsel = sb("hsel", [P, KC, 2], BF16); t_hs = Tok()
            xs = sb("cxs", [P, 256], F32); zf = sb("czf", [P, 256], F32); t_z = Tok()
            z3 = sb("cz3", [P, 3, 256], F32); t_z3 = Tok()
            zp = sb("czp", [P, 2, 256], F32); t_zp = Tok()
            zpx = sb("czpx", [P, 256], F32); zpv = sb("czpv", [P, 256], F32); t_zq = Tok()
            ysb = sb("cys", [P, 256], F32); t_ys = Tok()
            yd = sb("cyd", [P, 256], BF16); t_yd = Tok()
            self.memset("pool", hz[:], 0.0, w=[t_hz])
            self.memset("pool", zp[:], 0.0, w=[t_zp])
            pf = self.C("pflag")
            cw_t = sb("cw_t", [P, 768], F32); t_cw = Tok()
            o_, w_ = COFF["conv"]
            self.dma("sp", cw_t[:], self.cur_consts[:, o_:o_ + w_], r=(), w=[t_cw])
            cw = cw_t
            for j in range(NO):
                hTo = self.hT_own[:, :, j * P:(j + 1) * P]
                if j == 0:
                    h0, t_h0 = hz[:], t_hz
                else:
                    h0, t_h0 = self.hl2[:, 2 * j - 1, :, :], self.t_hl2[2 * j - 1]
                h1, t_h1 = self.hl2[:, 2 * j, :, :], self.t_hl2[2 * j]
                self.tt("dve", hd[:], h1, h0, ALU.subtract, r=[t_h0, t_h1], w=[t_hs])
                self.stt("dve", hsel[:], hd[:], pf, h0, ALU.mult, ALU.add, r=[t_hs, t_h0, self.t_c32], w=[t_hs])
                for kc in range(KC):
                    self.mm(B[0][:, 0:512], hTo[:, kc, :], WD[:, kc, 0:512], kc == 0, kc == KC - 1,
                            r=[self.t_hT[j], t_WD], w=[tb[1]])
                for kc in range(KC):
                    self.mm(B[1][:, 0:256], hTo[:, kc, :], WD[:, kc, 512:768], kc == 0, kc == KC - 1,
                            r=[self.t_hT[j], t_WD], w=[tb[2]])
                for kc in range(KC):
                    self.mm(B[2][0:2, 0:512], hsel[:, kc, :], WD[:, kc, 256:768], kc == 0, kc == KC - 1,
                            r=[t_hs, t_WD], w=[tb[3]])
                self.cp("act", xs[:], B[1][:, 0:256], r=[tb[2]], w=[t_z])
                self.tt("dve", zf[:], B[0][:, 256:512], xs[:], ALU.mult, r=[tb[1], t_z], w=[t_z])
                for k in range(3):
                    self.tt("pool", z3[:, k, :], zf[:], cw[:, k * 256:(k + 1) * 256], ALU.mult,
                            r=[t_z, t_cw], w=[t_z3])
                self.cp("act", zpx[0:2, :], B[2][0:2, 256:512], r=[tb[3]], w=[t_zq])
                self.tt("dve", zpv[0:2, :], B[2][0:2, 0:256], zpx[0:2, :], ALU.mult, r=[tb[3], t_zq], w=[t_zq])
                self.tt("dve", zp[0:2, 0, :], zpv[0:2, :], cw[0:2, 0:256], ALU.mult, r=[t_zq, t_cw], w=[t_zp])
                self.tt("dve", zp[0:2, 1, :], zpv[0:2, :], cw[0:2, 256:512], ALU.mult, r=[t_zq, t_cw], w=[t_zp])
                yb_ = B[3][:, 0:256]
                self.mm(yb_, self.C("ident"), z3[:, 2, :], True, False, r=[t_z3, self.t_c32], w=[tb[4]], sig=False)
                self.mm(yb_, self.C("sh1"), z3[:, 1, :], False, False, r=[t_z3, self.t_c32], w=[tb[4]], sig=False)
                self.mm(yb_, self.C("sh2"), z3[:, 0, :], False, False, r=[t_z3, self.t_c32], w=[tb[4]], sig=False)
                self.mm(yb_, self.C("ba"), zp[:, 1, :], False, False, r=[t_zp, self.t_c32], w=[tb[4]], sig=False)
                self.mm(yb_, self.C("bb"), zp[:, 0, :], False, True, r=[t_zp, self.t_c32], w=[tb[4]])
                self.cp("act", ysb[:], yb_, r=[tb[4]], w=[t_ys])
                self.tt("dve", yd[:], B[0][:, 0:256], ysb[:], ALU.mult, r=[tb[1], t_ys], w=[t_yd])
                bT = self.bankT
                for pr in range(2):
                    self.tr(bT[:, pr * P:(pr + 1) * P], yd[:, pr * P:(pr + 1) * P], self.ident_bf[:],
                            r=[t_yd, self.t_cbf], w=[tb[0]], sig=(pr == 1))
                self.cp("act", self.yTn[3][:, :, j * P:(j + 1) * P], bT[:, 0:2 * P].rearrange("p (a t) -> p a t", a=2),
                        r=[tb[0]], w=[self.t_yT[3][j]])

    def phase_merge(self):
        nc, s = self.nc, self.s
        NO = self.NO
        B, tb = self.bank, self.t_bank
        TG = 4
        halves = [list(range(0, NO // 2)), list(range(NO // 2, NO))] if NO >= 8 else [list(range(NO))]
        wv = self.d_w_in.rearrange("(k p) c -> p k c", p=P)
        with ExitStack() as es0:
            HT = len(halves[0])
            mT = _alloc(nc, es0, "mT", [P, KC, HT * P], BF16)
            t_mT = [Tok() for _ in range(HT)]
            for tiles in halves:
                j0 = tiles[0]
                groups = [tiles[i:i + TG] for i in range(0, len(tiles), TG)]
                with ExitStack() as es:
                    sb = lambda name, shape, dtp: _alloc(nc, es, name, shape, dtp)
                    Wg = [sb(f"Wg{i}", [P, KC, 4, P], BF16) for i in range(2)]; t_Wg = [Tok() for _ in range(2)]
                    Wb = [sb(f"Wb{i}", [P, 2, 4, P], BF16) for i in range(2)]; t_Wb = [Tok() for _ in range(2)]
                    th = [sb(f"th{i}", [P, 512], F32) for i in range(2)]; t_th = [Tok() for _ in range(2)]
                    acc = sb("macc", [P, 512], F32); tmp = sb("mtmp", [P, 512], F32); t_acc = Tok(); t_tmp = Tok()
                    for cc in range(KC):
                        ws = cc % 2
                        for n in range(4):
                            c0 = OFF["g"] + n * D + cc * P
                            self.load_w(Wg[ws][:, :, n, :], wv[:, :, c0:c0 + P], t_Wg[ws])
                            self.load_w(Wb[ws][:, :, n, :],
                                        self.d_w_branch[n].rearrange("(f p) c -> p f c", p=P)[:, :, cc * P:(cc + 1) * P],
                                        t_Wb[ws])
                        for grp in groups:
                            T = len(grp) * P
                            c_lo = grp[0] * P
                            hts = [self.t_hT[j] for j in grp]
                            for n in range(4):
                                gb = n % 2
                                for kc in range(KC):
                                    self.mm(B[gb][:, 0:T], Wg[ws][:, kc, n, :], self.hT_own[:, kc, c_lo:c_lo + T],
                                            kc == 0, kc == KC - 1, r=[t_Wg[ws]] + hts, w=[tb[1 + gb]])
                                self.act(th[gb][:, 0:T], B[gb][:, 0:T], AF.Tanh, r=[tb[1 + gb]], w=[t_th[gb]], scale=0.5)
                                for f in range(2):
                                    self.mm(B[2 + gb][:, 0:T], Wb[ws][:, f, n, :], self.yTn[n][:, f, c_lo:c_lo + T],
                                            f == 0, f == 1, r=[t_Wb[ws]] + [self.t_yT[n][j] for j in grp],
                                            w=[tb[3 + gb]])
                                if n == 0:
                                    self.stt("dve", acc[:, 0:T], th[gb][:, 0:T], 1.0, B[2 + gb][:, 0:T], ALU.add, ALU.mult,
                                             r=[t_th[gb], tb[3 + gb]], w=[t_acc])
                                else:
                                    self.stt("dve", tmp[:, 0:T], th[gb][:, 0:T], 1.0, B[2 + gb][:, 0:T], ALU.add, ALU.mult,
                                             r=[t_th[gb], tb[3 + gb]], w=[t_tmp])
                                    self.tt("dve", acc[:, 0:T], acc[:, 0:T], tmp[:, 0:T], ALU.add,
                                            r=[t_acc, t_tmp], w=[t_acc])
                            m_lo = (grp[0] - j0) * P
                            self.act(mT[:, cc, m_lo:m_lo + T], acc[:, 0:T], AF.Copy, r=[t_acc],
                                     w=[t_mT[j - j0] for j in grp], scale=0.5)
                s.barrier()
                with ExitStack() as es:
                    Wo = _alloc(nc, es, "Wo", [P, KC, D], BF16); t_Wo = Tok()
                    self.load_w(Wo[:], self.d_w_out.rearrange("(k p) c -> p k c", p=P), t_Wo)
                    for j in tiles:
                        for half in range(2):
                            ob = 4 + half
                            for kc in range(KC):
                                self.mm(B[ob][:, 0:512], mT[:, kc, (j - j0) * P:(j - j0 + 1) * P],
                                        Wo[:, kc, half * 512:(half + 1) * 512], kc == 0, kc == KC - 1,
                                        r=[t_mT[j - j0], t_Wo], w=[tb[1 + ob]])
                            xs_ = self.x_own[:, j, half * 512:(half + 1) * 512]
                            self.tt("dve", xs_, xs_, B[ob][:, 0:512], ALU.add, r=[tb[1 + ob], self.t_x[j]], w=[self.t_x[j]])
                s.barrier()

    def phase_ffn(self):
        nc = self.nc
        NO = self.NO
        B, tb = self.bank, self.t_bank
        TG = 4
        groups = [list(range(i, min(i + TG, NO))) for i in range(0, NO, TG)]
        with ExitStack() as es:
            sb = lambda name, shape, dtp: _alloc(nc, es, name, shape, dtp)
            Wu = [sb(f"Wu{i}", [P, KC, 512], BF16) for i in range(2)]; t_Wu = [Tok() for _ in range(2)]
            Wd = [sb(f"Wd{i}", [P, 4, D], BF16) for i in range(2)]; t_Wd = [Tok() for _ in range(2)]
            rr = [sb(f"frr{i}", [P, 512], F32) for i in range(2)]; t_rr = [Tok() for _ in range(2)]
            uT = [sb(f"fuT{i}", [P, 4, 512], BF16) for i in range(2)]; t_uT = [Tok() for _ in range(2)]
            ob_i = 0
            gi = 0
            for slab in range(DFF // 512):
                ws = slab % 2
                self.load_w(Wu[ws][:], self.d_w_up.rearrange("(k p) c -> p k c", p=P)[:, :, slab * 512:(slab + 1) * 512],
                            t_Wu[ws])
                self.load_w(Wd[ws][:], self.d_w_down[slab * 512:(slab + 1) * 512, :].rearrange("(f p) c -> p f c", p=P),
                            t_Wd[ws])
                for grp in groups:
                    T = len(grp) * P
                    c_lo = grp[0] * P
                    hts = [self.t_hT[j] for j in grp]
                    us = gi % 2
                    gi += 1
                    for fc in range(4):
                        ub = fc % 2
                        for kc in range(KC):
                            self.mm(B[ub][:, 0:T], Wu[ws][:, kc, fc * P:(fc + 1) * P], self.hT_own[:, kc, c_lo:c_lo + T],
                                    kc == 0, kc == KC - 1, r=[t_Wu[ws]] + hts, w=[tb[1 + ub]])
                        self.act(rr[ub][:, 0:T], B[ub][:, 0:T], AF.Relu, r=[tb[1 + ub]], w=[t_rr[ub]])
                        self.act(uT[us][:, fc, 0:T], rr[ub][:, 0:T], AF.Square, r=[t_rr[ub]], w=[t_uT[us]])
                    for ti, j in enumerate(grp):
                        for half in range(2):
                            ob = 2 + (ob_i % 4)
                            ob_i += 1
                            for fc in range(4):
                                self.mm(B[ob][:, 0:512], uT[us][:, fc, ti * P:(ti + 1) * P],
                                        Wd[ws][:, fc, half * 512:(half + 1) * 512], fc == 0, fc == 3,
                                        r=[t_uT[us], t_Wd[ws]], w=[tb[1 + ob]])
                            xs_ = self.x_own[:, j, half * 512:(half + 1) * 512]
                            self.tt("dve", xs_, xs_, B[ob][:, 0:512], ALU.add, r=[tb[1 + ob], self.t_x[j]], w=[self.t_x[j]])


def _rope_table(pos):
    half = 32
    inv = np.float32(10000.0) ** (-np.arange(half, dtype=np.float32) * np.float32(2.0) / np.float32(64))
    ang = pos.astype(np.float32)[:, None] * inv[None, :].astype(np.float32)
    return np.concatenate([np.cos(ang), np.sin(ang)], axis=1).astype(np.float32)


def _consts(params, l, parity):
    c = np.zeros((P, NCONST), np.float32)

    def put(name, arr):
        o, w = COFF[name]
        c[:, o:o + w] = np.asarray(arr, np.float32).reshape(P, w) if np.ndim(arr) == 2 else np.asarray(arr, np.float32)

    idx = np.arange(P)
    put("ident", np.eye(P, dtype=np.float32))
    put("maskT", (idx[:, None] <= idx[None, :]).astype(np.float32))
    put("ones", np.ones((P, P), np.float32))
    put("sh1", (idx[:, None] == idx[None, :] - 1).astype(np.float32))
    put("sh2", (idx[:, None] == idx[None, :] - 2).astype(np.float32))
    ba = np.zeros((P, P), np.float32); ba[1, 0] = 1.0
    bb = np.zeros((P, P), np.float32); bb[0, 0] = 1.0; bb[1, 1] = 1.0
    put("ba", ba)
    put("bb", bb)
    put("lnmixT", params["ln_mix"][l].reshape(KC, P).T)
    put("lnmlpT", params["ln_mlp"][l].reshape(KC, P).T)
    rep = lambda v: np.broadcast_to(np.asarray(v, np.float32).reshape(1, -1), (P, np.size(v)))
    put("sgu_norm", rep(params["sgu_norm"][l]))
    put("sgu_b", params["sgu_b"][l].T)
    put("qn", rep(np.tile(params["q_norm"][l], 4)))
    put("kn", rep(np.tile(params["k_norm"][l], 4)))
    put("kidx", rep(params["kidx_norm"][l]))
    put("ib", rep(params["mlstm_i_bias"][l]))
    put("fb", rep(params["mlstm_f_bias"][l]))
    put("mn", rep(params["mlstm_norm"][l]))
    put("conv", rep(params["conv_w"][l].reshape(-1)))
    cm = np.zeros((P, 256), np.float32)
    tri = np.where(idx[None, :] <= idx[:, None], 0.0, NEG).astype(np.float32)
    if parity == 0:
        cm[:, 0:128] = tri
        cm[:, 128:256] = NEG
    else:
        cm[:, 128:256] = tri
    put("cmask", cm)
    put("pflag", np.full((P, 1), float(parity), np.float32))
    put("neghalf", np.full((P, 8), -0.5, np.float32))
    put("sgu_wT", np.transpose(params["sgu_w"][l], (2, 0, 1)).reshape(P, 4 * P))
    return c


_NC_CACHE = {}


def _get_nc(S, dbg=None):
    key = (S, tuple(dbg) if dbg else None)
    if key not in _NC_CACHE:
        _NC_CACHE[key] = Builder(S, dbg).build()
    return _NC_CACHE[key]


def run_model(x, params, dbg=None):
    Bn, S, _ = x.shape
    NT = S // P
    NO = NT // 2
    nc = Builder(S, dbg).build()
    rope_all = np.ascontiguousarray(_rope_table(np.arange(S)).reshape(NT, P, 64))
    c00, c01 = _consts(params, 0, 0), _consts(params, 0, 1)
    wts = {k: np.ascontiguousarray(params[k]) for k in ("w_in", "w_branch", "w_out", "w_up", "w_down")}
    in_maps = []
    for core in range(8):
        b, par = core // 2, core % 2
        own = [2 * j + par for j in range(NO)]
        in_maps.append(dict(
            x=np.ascontiguousarray(x[b]), consts3=np.ascontiguousarray(np.stack([c00, c01, _consts(params, 1, par)])),
            rope_all=rope_all, rope_own=np.ascontiguousarray(rope_all[own]), **wts))
    res = run_bass_kernel_spmd(nc, in_maps, core_ids=list(range(8)))
    out = np.empty_like(x)
    for core in range(8):
        b, par = core // 2, core % 2
        y = np.asarray(res.results[core]["y"]).reshape(NO, P, D)
        ov = out[b].reshape(NT, P, D)
        for j in range(NO):
            ov[2 * j + par] = y[j]
    return out, res


def kernel(**inputs):
    x = np.asarray(inputs["x"], np.float32)
    params = {k: np.asarray(v, np.float32) for k, v in inputs.items() if k != "x"}
    out, _ = run_model(x, params)
    return out
```
